# Optimizing a Trainium2 kernel written in Bass

```python
import math
import jax, jax.numpy as jnp
from jax import lax
import numpy as np

D_MODEL = 1024
BATCH = 8
SEQ = 2048
DEPTH = 1
DEC_BATCH = 128
DEC_SEQ = 4
PAST_LEN = 16384
PAGE_SIZE = 128

D_MIX = 2 * D_MODEL
D_POOL = D_MIX // 2
POOL_WINDOWS = (2, 4, 8, 16)
N_POOL_GROUPS = len(POOL_WINDOWS)
POOL_GROUP = D_POOL // N_POOL_GROUPS
POOL_BUF = max(POOL_WINDOWS) - 1
D_SSM = D_MIX - D_POOL
SSM_HEAD_DIM = 64
SSM_HEADS = D_SSM // SSM_HEAD_DIM
SSM_GROUPS = 2
HEADS_PER_GROUP = SSM_HEADS // SSM_GROUPS
D_STATE = 128
CONV_WIDTH = 4
CONV_DIM = D_SSM + 2 * SSM_GROUPS * D_STATE
CHUNK = 128
D_FF = 4 * D_MODEL
D_PLE = 256
D_IN_PROJ = D_POOL + D_SSM + CONV_DIM + SSM_HEADS
EPS = 1e-6

kernel_name = 'hymba_pool_ssd_decoder_step'


def _rmsnorm(x, g):
    xf = x.astype(jnp.float32)
    y = xf * lax.rsqrt(jnp.mean(xf * xf, axis=-1, keepdims=True) + EPS)
    return (y * g.astype(jnp.float32)).astype(x.dtype)


def _pool_mixer(u, buf, pos, w_pool, pool_scale):
    T = u.shape[1]
    ext = jnp.concatenate([buf.astype(u.dtype), u], axis=1)
    ext32 = ext.astype(jnp.float32)
    cs = jnp.cumsum(ext32, axis=1)
    cs = jnp.concatenate([jnp.zeros_like(cs[:, :1]), cs], axis=1)
    outs = []
    for g, w in enumerate(POOL_WINDOWS):
        sl = slice(g * POOL_GROUP, (g + 1) * POOL_GROUP)
        win_sum = (cs[:, POOL_BUF + 1:POOL_BUF + 1 + T, sl]
                   - cs[:, POOL_BUF + 1 - w:POOL_BUF + 1 - w + T, sl])
        count = jnp.minimum(w, pos + 1).astype(jnp.float32)[None, :, None]
        d = win_sum / count - ext32[:, POOL_BUF:, sl]
        outs.append(jnp.einsum('btc,ce->bte', d, w_pool[g].astype(jnp.float32)))
    y = jnp.concatenate(outs, axis=-1) * pool_scale.astype(jnp.float32)
    return y.astype(u.dtype), ext[:, -POOL_BUF:]


def _causal_conv(xbc, buf, conv_w, conv_b):
    T = xbc.shape[1]
    ext = jnp.concatenate([buf.astype(xbc.dtype), xbc], axis=1)
    acc = conv_b
    for k in range(CONV_WIDTH):
        acc = acc + ext[:, k:k + T] * conv_w[k]
    return jax.nn.silu(acc), ext[:, -(CONV_WIDTH - 1):]


def _ssd_scan(x, dt, A, Bm, Cm, h0):
    b, T = x.shape[:2]
    L = min(CHUNK, T)
    nc = -(-T // L)
    pad = nc * L - T
    if pad:
        x = jnp.pad(x, ((0, 0), (0, pad), (0, 0), (0, 0)))
        dt = jnp.pad(dt, ((0, 0), (0, pad), (0, 0)))
        Bm = jnp.pad(Bm, ((0, 0), (0, pad), (0, 0), (0, 0)))
        Cm = jnp.pad(Cm, ((0, 0), (0, pad), (0, 0), (0, 0)))
    G, Hg = SSM_GROUPS, HEADS_PER_GROUP
    xc = x.reshape(b, nc, L, G, Hg, SSM_HEAD_DIM)
    dtc = dt.reshape(b, nc, L, G, Hg)
    Bc = Bm.reshape(b, nc, L, G, D_STATE)
    Cc = Cm.reshape(b, nc, L, G, D_STATE)
    a_cum = jnp.cumsum(dtc * A.reshape(G, Hg), axis=2)
    seg = a_cum[:, :, :, None] - a_cum[:, :, None, :]
    causal = jnp.tril(jnp.ones((L, L), dtype=bool))[None, None, :, :, None, None]
    decay = jnp.exp(jnp.where(causal, seg, -jnp.inf))
    cb = jnp.einsum('bclgn,bcsgn->bclsg', Cc, Bc)
    y_diag = jnp.einsum('bclsg,bclsgh,bcsgh,bcsghp->bclghp', cb, decay, dtc, xc)
    decay_end = jnp.exp(a_cum[:, :, -1:] - a_cum)
    chunk_states = jnp.einsum('bclgn,bclgh,bclghp->bcghpn', Bc, decay_end * dtc, xc)
    chunk_decay = jnp.exp(a_cum[:, :, -1])

    def step(h, inp):
        st, dec = inp
        return h * dec[..., None, None] + st, h

    h_init = h0.reshape(b, G, Hg, SSM_HEAD_DIM, D_STATE)
    h_final, h_prev = lax.scan(step, h_init,
                               (jnp.moveaxis(chunk_states, 1, 0), jnp.moveaxis(chunk_decay, 1, 0)))
    h_prev = jnp.moveaxis(h_prev, 0, 1)
    y_off = jnp.einsum('bclgn,bcghpn,bclgh->bclghp', Cc, h_prev, jnp.exp(a_cum))
    y = (y_diag + y_off).reshape(b, nc * L, SSM_HEADS, SSM_HEAD_DIM)[:, :T]
    return y, h_final.reshape(b, SSM_HEADS, SSM_HEAD_DIM, D_STATE)


def _mixer(xn, pos, pool_buf, conv_buf, ssm_state, w_in, w_pool, pool_scale, conv_w, conv_b,
           dt_bias, a_log, d_skip, ssm_norm_g, w_out):
    b, T = xn.shape[:2]
    proj = jnp.einsum('btd,de->bte', xn, w_in)
    u, z, xbc, dt_raw = jnp.split(proj, [D_POOL, D_POOL + D_SSM, D_POOL + D_SSM + CONV_DIM], axis=-1)
    y_pool, new_pool = _pool_mixer(u, pool_buf, pos, w_pool, pool_scale)
    xbc_c, new_conv = _causal_conv(xbc, conv_buf, conv_w, conv_b)
    xs, Bm, Cm = jnp.split(xbc_c, [D_SSM, D_SSM + SSM_GROUPS * D_STATE], axis=-1)
    x32 = xs.astype(jnp.float32).reshape(b, T, SSM_HEADS, SSM_HEAD_DIM)
    B32 = Bm.astype(jnp.float32).reshape(b, T, SSM_GROUPS, D_STATE)
    C32 = Cm.astype(jnp.float32).reshape(b, T, SSM_GROUPS, D_STATE)
    dt = jax.nn.softplus(dt_raw.astype(jnp.float32) + dt_bias.astype(jnp.float32))
    A = -jnp.exp(a_log.astype(jnp.float32))
    y, new_ssm = _ssd_scan(x32, dt, A, B32, C32, ssm_state.astype(jnp.float32))
    y = y + d_skip.astype(jnp.float32)[:, None] * x32
    y = y.reshape(b, T, D_SSM) * jax.nn.silu(z.astype(jnp.float32))
    y_ssm = _rmsnorm(y, ssm_norm_g).astype(xn.dtype)
    out = jnp.einsum('btm,md->btd', jnp.concatenate([y_pool, y_ssm], axis=-1), w_out)
    return out, new_pool, new_conv, new_ssm.astype(ssm_state.dtype)


def _layer(h, p, pos, pool_buf, conv_buf, ssm_state, w_in, w_pool, pool_scale, conv_w, conv_b,
           dt_bias, a_log, d_skip, ssm_norm_g, w_out, norm_mix_g, norm_mlp_g, w_ff1, w_ff2,
           norm_ple_g, w_gate, w_ple):
    mix, new_pool, new_conv, new_ssm = _mixer(
        _rmsnorm(h, norm_mix_g), pos, pool_buf, conv_buf, ssm_state, w_in, w_pool, pool_scale,
        conv_w, conv_b, dt_bias, a_log, d_skip, ssm_norm_g, w_out)
    h = h + mix
    hn = _rmsnorm(h, norm_mlp_g)
    h = h + jnp.einsum('btf,fd->btd', jnp.square(jax.nn.relu(jnp.einsum('btd,df->btf', hn, w_ff1))), w_ff2)
    gate = jax.nn.sigmoid(jnp.einsum('btd,de->bte', _rmsnorm(h, norm_ple_g), w_gate))
    h = h + gate * jnp.einsum('btk,kd->btd', p, w_ple)
    return h, new_pool, new_conv, new_ssm


def _trunk(x, p, pos, pool_st, conv_st, ssm_st, layer_params, final_norm_g):
    h = x
    pools, convs, ssms = [], [], []
    for i in range(DEPTH):
        lp = [a[i] for a in layer_params]
        h, np_, nc_, ns_ = _layer(h, p[i], pos, pool_st[i], conv_st[i], ssm_st[i], *lp)
        pools.append(np_)
        convs.append(nc_)
        ssms.append(ns_)
    return _rmsnorm(h, final_norm_g), jnp.stack(pools), jnp.stack(convs), jnp.stack(ssms)


def setup_inputs(seed: int = 0) -> dict:
    key = jax.random.key(seed)
    ks = jax.random.split(key, 32)
    f32 = jnp.float32
    nrm = lambda k, shape, s: jax.random.normal(k, shape, f32) * s
    dt_init = jnp.exp(jax.random.uniform(ks[11], (DEPTH, SSM_HEADS), f32, math.log(1e-3), math.log(1e-1)))
    return {
        'x_prompt': nrm(ks[0], (BATCH, SEQ, D_MODEL), 1.0),
        'x_sample': nrm(ks[1], (DEC_BATCH, DEC_SEQ, D_MODEL), 1.0),
        'p_prompt': nrm(ks[2], (DEPTH, BATCH, SEQ, D_PLE), 1.0),
        'p_sample': nrm(ks[3], (DEPTH, DEC_BATCH, DEC_SEQ, D_PLE), 1.0),
        'state_pool': nrm(ks[4], (DEPTH, DEC_BATCH, POOL_BUF, D_POOL), 1.0),
        'state_conv': nrm(ks[5], (DEPTH, DEC_BATCH, CONV_WIDTH - 1, CONV_DIM), 1.0),
        'state_ssm': nrm(ks[6], (DEPTH, DEC_BATCH, SSM_HEADS, SSM_HEAD_DIM, D_STATE), 0.5),
        'w_in': nrm(ks[7], (DEPTH, D_MODEL, D_IN_PROJ), D_MODEL ** -0.5),
        'w_pool': nrm(ks[8], (DEPTH, N_POOL_GROUPS, POOL_GROUP, POOL_GROUP), POOL_GROUP ** -0.5),
        'pool_scale': 1.0 + nrm(ks[9], (DEPTH, D_POOL), 0.1),
        'conv_w': nrm(ks[10], (DEPTH, CONV_WIDTH, CONV_DIM), CONV_WIDTH ** -0.5),
        'conv_b': nrm(ks[12], (DEPTH, CONV_DIM), 0.01),
        'dt_bias': dt_init + jnp.log(-jnp.expm1(-dt_init)),
        'a_log': jnp.log(jax.random.uniform(ks[13], (DEPTH, SSM_HEADS), f32, 1.0, 16.0)),
        'd_skip': 1.0 + nrm(ks[14], (DEPTH, SSM_HEADS), 0.1),
        'ssm_norm_g': 1.0 + nrm(ks[15], (DEPTH, D_SSM), 0.05),
        'w_out': nrm(ks[16], (DEPTH, D_MIX, D_MODEL), D_MIX ** -0.5),
        'norm_mix_g': 1.0 + nrm(ks[17], (DEPTH, D_MODEL), 0.05),
        'norm_mlp_g': 1.0 + nrm(ks[18], (DEPTH, D_MODEL), 0.05),
        'w_ff1': nrm(ks[19], (DEPTH, D_MODEL, D_FF), D_MODEL ** -0.5),
        'w_ff2': nrm(ks[20], (DEPTH, D_FF, D_MODEL), D_FF ** -0.5),
        'norm_ple_g': 1.0 + nrm(ks[21], (DEPTH, D_MODEL), 0.05),
        'w_gate': nrm(ks[22], (DEPTH, D_MODEL, D_MODEL), D_MODEL ** -0.5),
        'w_ple': nrm(ks[23], (DEPTH, D_PLE, D_MODEL), D_PLE ** -0.5),
        'final_norm_g': 1.0 + nrm(ks[24], (D_MODEL,), 0.05),
    }


def reference(x_prompt, x_sample, p_prompt, p_sample, state_pool, state_conv, state_ssm,
              w_in, w_pool, pool_scale, conv_w, conv_b, dt_bias, a_log, d_skip, ssm_norm_g,
              w_out, norm_mix_g, norm_mlp_g, w_ff1, w_ff2, norm_ple_g, w_gate, w_ple, final_norm_g):
    layer_params = (w_in, w_pool, pool_scale, conv_w, conv_b, dt_bias, a_log, d_skip, ssm_norm_g,
                    w_out, norm_mix_g, norm_mlp_g, w_ff1, w_ff2, norm_ple_g, w_gate, w_ple)
    b_p, t_p = x_prompt.shape[:2]
    t_s = x_sample.shape[1]
    zero_pool = jnp.zeros((DEPTH, b_p, POOL_BUF, D_POOL), x_prompt.dtype)
    zero_conv = jnp.zeros((DEPTH, b_p, CONV_WIDTH - 1, CONV_DIM), x_prompt.dtype)
    zero_ssm = jnp.zeros((DEPTH, b_p, SSM_HEADS, SSM_HEAD_DIM, D_STATE), x_prompt.dtype)
    pos_prompt = jnp.arange(t_p, dtype=jnp.int32)
    pos_sample = PAST_LEN + jnp.arange(t_s, dtype=jnp.int32)
    y_prompt, pool_p, conv_p, ssm_p = _trunk(x_prompt, p_prompt, pos_prompt, zero_pool, zero_conv,
                                             zero_ssm, layer_params, final_norm_g)
    y_sample, pool_s, conv_s, ssm_s = _trunk(x_sample, p_sample, pos_sample, state_pool, state_conv,
                                             state_ssm, layer_params, final_norm_g)
    return (y_prompt, y_sample, pool_p, conv_p, ssm_p, pool_s, conv_s, ssm_s)
```

```python
import numpy as np
from contextlib import ExitStack
import concourse.bass as bass
import concourse.mybir as mybir
from concourse.bass_utils import run_bass_kernel_spmd

F32 = mybir.dt.float32
BF16 = mybir.dt.bfloat16
AF = mybir.ActivationFunctionType
ALU = mybir.AluOpType
AX = mybir.AxisListType

D = 1024
KC = 8
NP = 2048
NS = 64
NT = NP + NS
NB = 17
TT = [(0, 512), (512, 512), (1024, 512), (1536, 512), (2048, 64)]
POOLW = (2, 4, 8, 16)
EPS = 1e-6
NV = 108
V_GMIX, V_GMLP, V_GPLE, V_GFIN, V_PSC, V_GSSM, V_CB, V_CW = 0, 8, 16, 24, 32, 40, 48, 60


class Buf:
    __slots__ = ("name", "w", "r", "excl")

    def __init__(self, name, excl=False):
        self.name = name
        self.w = None
        self.r = {}
        self.excl = excl


class DSem:
    def __init__(self, key, h):
        self.key = key
        self.h = h
        self.count = 0


class KB:
    def __init__(self):
        self.nc = bass.Bass("TRN2", target_bir_lowering=False)
        self.es = ExitStack()
        nc = self.nc
        self.E = {"pe": nc.tensor, "act": nc.scalar, "dve": nc.vector, "pool": nc.gpsimd, "sp": nc.sync}
        self.esem = {e: self.es.enter_context(nc.semaphore("sem_" + e)) for e in ("pe", "act", "dve", "pool")}
        self.seq = {e: 0 for e in self.esem}
        self.waited = {e: {} for e in self.E}
        self.bufs = []
        self.dsems = []
        self.bar = self.es.enter_context(nc.semaphore("sem_bar"))
        self.bar_count = 0
        self.out_toks = []
        self.nbuf = 0

    def buf(self, name="b"):
        b = Buf(name)
        self.bufs.append(b)
        return b

    def bufl(self, n, name="b"):
        return [self.buf(name + str(i)) for i in range(n)]

    def dsem(self, name):
        d = DSem("d_" + name + str(len(self.dsems)), self.es.enter_context(self.nc.semaphore("ds_" + name + str(len(self.dsems)))))
        self.dsems.append(d)
        return d

    def sb(self, name, shape, dt):
        return self.es.enter_context(self.nc.sbuf_tensor(name, shape, dt))

    def psum(self, name, shape, dt):
        return self.es.enter_context(self.nc.psum_tensor(name, shape, dt))

    def _deps(self, eng, reads, writes, is_dma):
        deps = {}

        def need(tok, kind):
            if tok is None:
                return
            key, sem, val = tok
            if (not is_dma) and key == eng and val > self.seq[eng]:
                return
            if key not in deps or deps[key][1] < val:
                deps[key] = (sem, val)

        for b in reads:
            need(b.w, "raw")
            if b.excl:
                for t in b.r.values():
                    need(t, "war")
        for b in writes:
            need(b.w, "waw")
            for t in b.r.values():
                need(t, "war")
        wd = self.waited[eng]
        for key, (sem, val) in deps.items():
            if wd.get(key, 0) < val:
                self.E[eng].wait_ge(sem, val)
                wd[key] = val

    def _note(self, tok, reads, writes):
        for b in reads:
            if b.excl and b not in writes:
                b.w = tok
                b.r = {}
                continue
            old = b.r.get(tok[0])
            if old is None or old[2] < tok[2]:
                b.r[tok[0]] = tok
        for b in writes:
            b.w = tok
            b.r = {}

    def op(self, eng, fn, reads=(), writes=(), track=True):
        self._deps(eng, reads, writes, False)
        ins = fn(self.E[eng])
        if track:
            self.seq[eng] += 1
            ins.then_inc(self.esem[eng], 1)
            tok = (eng, self.esem[eng], self.seq[eng])
        else:
            tok = (eng, self.esem[eng], self.seq[eng] + 1)
        self._note(tok, reads, writes)
        return tok

    def dma(self, q, out, in_, ds, reads=(), writes=(), is_out=False):
        self._deps(q, reads, writes, True)
        self.E[q].dma_start(out=out, in_=in_).then_inc(ds.h, 16)
        ds.count += 16
        tok = (ds.key, ds.h, ds.count)
        self._note(tok, reads, writes)
        if is_out:
            self.out_toks.append(tok)
        return tok

    def barrier(self, keep=()):
        skip = {d.key for d, _ in keep}
        keepb = {id(b) for _, b in keep}
        sp = self.E["sp"]
        wd = self.waited["sp"]
        for e, s in self.esem.items():
            if wd.get(e, 0) < self.seq[e]:
                sp.wait_ge(s, self.seq[e])
                wd[e] = self.seq[e]
        for d in self.dsems:
            if d.key in skip:
                continue
            if d.count and wd.get(d.key, 0) < d.count:
                sp.wait_ge(d.h, d.count)
                wd[d.key] = d.count
        self.bar_count += 1
        sp.sem_inc(self.bar, 1)
        for e in ("pe", "act", "dve", "pool"):
            self.E[e].wait_ge(self.bar, self.bar_count)
            w = self.waited[e]
            for e2 in self.esem:
                w[e2] = self.seq[e2]
            for d in self.dsems:
                if d.key not in skip:
                    w[d.key] = d.count
        for b in self.bufs:
            if id(b) in keepb:
                continue
            b.w = None
            b.r = {}

    def finish(self):
        sp = self.E["sp"]
        wd = self.waited["sp"]
        for key, sem, val in self.out_toks:
            if wd.get(key, 0) < val:
                sp.wait_ge(sem, val)
                wd[key] = val


class Region:
    def __init__(self, arena_ap, nbytes):
        self.a = arena_ap
        self.nbytes = nbytes
        self.off = 0

    def reset(self):
        self.off = 0

    def alloc(self, shape, dt, parts=128):
        esz = 4 if dt == F32 else 2
        n = 1
        for s in shape[1:]:
            n *= s
        nb = (n * esz + 31) // 32 * 32
        assert self.off + nb <= self.nbytes, ("region overflow", self.off, nb, self.nbytes)
        v = self.a[0:shape[0], self.off // 2:(self.off + n * esz) // 2]
        self.off += nb
        if dt == F32:
            v = v.bitcast(F32)
        if len(shape) == 3:
            v = v.rearrange("p (a b) -> p a b", a=shape[1])
        elif len(shape) == 4:
            v = v.rearrange("p (a b c) -> p a b c", a=shape[1], b=shape[2])
        return v


def build_program(debug=False):
    k = KB()
    nc = k.nc
    dram_in = lambda n, s: nc.dram_tensor(n, s, F32, kind="ExternalInput").ap()
    dram_out = lambda n, s: nc.dram_tensor(n, s, F32, kind="ExternalOutput").ap()
    xT = dram_in("xT", [D, NT])
    pT = dram_in("pT", [256, NT])
    spool = dram_in("spool", [128, 8 * 15 * 16])
    sconv = dram_in("sconv", [128, 12 * 3 * 16])
    sssm = dram_in("sssm", [16, 16, 64, 128])
    w_in = dram_in("w_in", [D, 3600])
    w_pool = dram_in("w_pool", [4, 256, 256])
    w_out = dram_in("w_out", [2048, D])
    w_ff1 = dram_in("w_ff1", [D, 4096])
    w_ff2 = dram_in("w_ff2", [4096, D])
    w_gate = dram_in("w_gate", [D, D])
    w_ple = dram_in("w_ple", [256, D])
    vecs = dram_in("vecs", [128, NV])
    hv = dram_in("hv", [16, 3])
    dsk = dram_in("dsk", [1, 16])
    yT = dram_out("yT", [D, NT])
    npool_p = dram_out("npool_p", [128, 8, 15])
    nconv_p = dram_out("nconv_p", [128, 12, 3])
    nssm_p = dram_out("nssm_p", [128, 1024])
    npool_s = dram_out("npool_s", [128, 8, 15 * 16])
    nconv_s = dram_out("nconv_s", [128, 12, 3 * 16])
    nssm_s = dram_out("nssm_s", [16, 16, 64, 128])
    dbg = {}

    XN = k.sb("XN", [128, KC, NT], BF16)
    R2 = k.sb("R2", [128, 33792], BF16)
    R3 = k.sb("R3", [128, 42752], BF16)
    YMIX = R2[:, :].rearrange("p (c t) -> p c t", c=16)
    r2 = Region(R2[:, :], 67584)
    r3 = Region(R3[:, :], 85504)
    r2h = Region(R2[:, 0:16896], 33792)
    H = R3[:, 0:33792].bitcast(F32).rearrange("p (c t) -> p c t", c=KC)
    r3x = Region(R3[:, 33792:42752], 17920)

    IDB = k.sb("IDB", [128, 128], BF16)
    IDF = k.sb("IDF", [128, 128], F32)
    ONESB = k.sb("ONESB", [128, 128], BF16)
    ONESF = k.sb("ONESF", [128, 128], F32)
    TRIF = k.sb("TRIF", [128, 128], F32)
    TRIB = k.sb("TRIB", [128, 128], BF16)
    UB = k.sb("UB", [128, 128], BF16)
    TRIFS = k.sb("TRIFS", [64, 64], F32)
    TRIBS = k.sb("TRIBS", [64, 64], BF16)
    UBS = k.sb("UBS", [64, 64], BF16)
    BDFS = k.sb("BDFS", [64, 64], F32)
    DI = k.sb("DI", [128, 16, 128], BF16)
    BDROW = k.sb("BDROW", [128, 16, 64], BF16)
    BDCOL = k.sb("BDCOL", [64, 16], BF16)
    VEC = k.sb("VEC", [128, NV], F32)
    HV = k.sb("HV", [16, 4], F32)
    DB = k.sb("DB", [128, 16], F32)
    INVC = k.sb("INVC", [128, 4, 16], F32)
    TOK = k.sb("TOK", [128, NB, 5, 16], F32)
    AHL = k.sb("AHL", [128, NB, 2, 16], BF16)
    DTAS = k.sb("DTAS", [16, 64], F32)
    SST = k.sb("SST", [128, 1024], F32)
    HP = k.sb("HP", [128, 1024], BF16)
    SMALL = k.sb("SMALL", [128, 64], F32)
    TMPC = k.sb("TMPC", [128, 128], F32)
    CDX = k.sb("CDX", [128, 8, 16], F32)
    WDT = k.sb("WDT", [128, KC, 16], BF16)

    PS = [k.psum("ps%d" % i, [128, 512], F32) for i in range(8)]
    PB = k.bufl(8, "pb")
    for b_ in PB:
        b_.excl = True
    bank_rr = [0]

    def next_bank():
        i = bank_rr[0]
        bank_rr[0] = (i + 1) % 8
        return i

    cb = k.buf("const")
    vb = k.buf("vec")
    XNb = k.bufl(5, "xn")
    WS_DS = [k.dsem("ws") for _ in range(3)]
    ld_ds = [k.dsem("ld") for _ in range(12)]
    out_ds = [k.dsem("out") for _ in range(8)]
    od_i = [0]

    def next_out_ds():
        d = out_ds[od_i[0] % len(out_ds)]
        od_i[0] += 1
        return d

    def dump(name, ap, bufs, dt=F32):
        if not debug:
            return
        t = nc.dram_tensor("dbg_" + name, list(ap.shape), dt, kind="ExternalOutput").ap()
        k.dma("sp", t, ap, k.dsem("dbg"), reads=bufs, is_out=True)

    r2t = Region(R2[:, 25600:33792], 16384)
    WSZ = [r2t.alloc([128, KC, 512], BF16) for _ in range(2)]
    wszb = k.bufl(2, "wsz")
    wdtb = k.buf("wdt")
    _wv = lambda c0, ncols: w_in.rearrange("(kc p) n -> p kc n", p=128)[:, :, c0:c0 + ncols]
    k.dma("pool", WSZ[0][:], _wv(1024, 512), WS_DS[0], writes=(wszb[0],))
    k.dma("pool", WDT[:], _wv(3584, 16), ld_ds[10], writes=(wdtb,))
    k.dma("pool", WSZ[1][:], _wv(1536, 512), WS_DS[1], writes=(wszb[1],))
    def pool_op(fn, reads=(), writes=(cb,)):
        return k.op("pool", fn, reads=reads, writes=writes)

    def sel(t_ap, pattern, cmp_op, base, cm):
        pool_op(lambda e: e.affine_select(out=t_ap, in_=t_ap, pattern=pattern, compare_op=cmp_op, fill=0.0,
                                          base=base, channel_multiplier=cm), reads=(cb,))

    for t in (IDF, ONESF, TRIF):
        pool_op(lambda e, t=t: e.memset(t[:], 1.0))
    sel(IDF[:], [[-1, 128]], ALU.is_equal, 0, 1)
    sel(TRIF[:], [[1, 128]], ALU.is_ge, 0, -1)
    UF = TMPC
    pool_op(lambda e: e.memset(UF[:], 1.0))
    sel(UF[:], [[-1, 128]], ALU.is_gt, 0, 1)
    pool_op(lambda e: e.memset(BDFS[:], 1.0))
    bdv = BDFS[:].rearrange("p (b l) -> p b l", l=4)
    sel(bdv, [[-4, 16], [0, 4]], ALU.is_ge, 0, 1)
    sel(bdv, [[4, 16], [0, 4]], ALU.is_ge, 3, -1)
    k.op("dve", lambda e: e.tensor_copy(IDB[:], IDF[:]), reads=(cb,), writes=(cb,))
    k.op("dve", lambda e: e.tensor_copy(ONESB[:], ONESF[:]), reads=(cb,), writes=(cb,))
    k.op("dve", lambda e: e.tensor_copy(TRIB[:], TRIF[:]), reads=(cb,), writes=(cb,))
    k.op("dve", lambda e: e.tensor_copy(UB[:], UF[:]), reads=(cb,), writes=(cb,))
    k.op("dve", lambda e: e.tensor_tensor(TRIFS[:], TRIF[0:64, 0:64], BDFS[:], ALU.mult), reads=(cb,), writes=(cb,))
    k.op("dve", lambda e: e.tensor_copy(TRIBS[:], TRIFS[:]), reads=(cb,), writes=(cb,))
    k.op("dve", lambda e: e.tensor_tensor(UBS[:], UF[0:64, 0:64], BDFS[:], ALU.mult), reads=(cb,), writes=(cb,))
    pool_op(lambda e: e.memset(BDROW[:], 1.0))
    sel(BDROW[:], [[-4, 16], [1, 64]], ALU.is_ge, 0, 0)
    sel(BDROW[:], [[4, 16], [-1, 64]], ALU.is_ge, 3, 0)
    pool_op(lambda e: e.memset(BDCOL[:], 1.0))
    sel(BDCOL[:], [[-4, 16]], ALU.is_ge, 0, 1)
    sel(BDCOL[:], [[4, 16]], ALU.is_ge, 3, -1)
    k.dma("sp", VEC[:], vecs, ld_ds[0], writes=(vb,))
    k.dma("sp", HV[:, 0:3], hv, ld_ds[1], writes=(vb,))
    k.dma("sp", DB[:], dsk.partition_broadcast(128), ld_ds[2], writes=(vb,))
    k.op("dve", lambda e: e.tensor_tensor(DI[:], IDF[:].unsqueeze(1).broadcast_to([128, 16, 128]),
                                          DB[:].unsqueeze(2).broadcast_to([128, 16, 128]), ALU.mult),
         reads=(cb, vb), writes=(cb,))
    for g, w in enumerate(POOLW):
        pool_op(lambda e, g=g: e.iota(INVC[:, g, :], pattern=[[1, 16]], base=1, channel_multiplier=0, allow_small_or_imprecise_dtypes=True))
    for g, w in enumerate(POOLW):
        k.op("dve", lambda e, g=g, w=w: e.tensor_scalar(INVC[:, g, :], INVC[:, g, :], float(w), None, ALU.min),
             reads=(cb,), writes=(cb,))
    k.op("dve", lambda e: e.reciprocal(INVC[:], INVC[:]), reads=(cb,), writes=(cb,))
    k.op("act", lambda e: e.activation(HV[:, 3:4], HV[:, 1:2], AF.Exp), reads=(vb,), writes=(vb,))
    k.op("dve", lambda e: e.tensor_scalar(HV[:, 3:4], HV[:, 3:4], -1.0, None, ALU.mult), reads=(vb,), writes=(vb,))

    def wview_kc(w_dram, c0, ncols):
        return w_dram.rearrange("(kc p) n -> p kc n", p=128)[:, :, c0:c0 + ncols]

    def rmsnorm_tile(ti, src, src_bufs, gcol, dst_fn, dst_bufs, SQ, RS, sqb, rsb, post=None):
        t0, n = TT[ti]
        s = ti % 2
        k.op("act", lambda e: e.activation(SQ[s][:, :, 0:n], src, AF.Square), reads=src_bufs, writes=(sqb[s],))
        bi = next_bank()
        for kc in range(KC):
            k.op("pe", lambda e, kc=kc: e.matmul(PS[bi][:, 0:n], ONESB[:], SQ[s][:, kc, 0:n], start=(kc == 0), stop=(kc == KC - 1)),
                 reads=(sqb[s], cb), writes=(PB[bi],), track=(kc == KC - 1))
        k.op("dve", lambda e: e.tensor_scalar(RS[s][:, 0:n], PS[bi][:, 0:n], 1.0 / D, EPS, ALU.mult, ALU.add),
             reads=(PB[bi],), writes=(rsb[s],))
        k.op("act", lambda e: e.activation(RS[s][:, 0:n], RS[s][:, 0:n], AF.Ln), reads=(rsb[s],), writes=(rsb[s],))
        k.op("act", lambda e: e.activation(RS[s][:, 0:n], RS[s][:, 0:n], AF.Exp, scale=-0.5), reads=(rsb[s],), writes=(rsb[s],))
        for kc in range(KC):
            db = dst_bufs(kc) if callable(dst_bufs) else dst_bufs
            k.op("dve", lambda e, kc=kc: e.scalar_tensor_tensor(dst_fn(kc), src[:, kc, :], VEC[:, gcol + kc:gcol + kc + 1],
                                                                RS[s][:, 0:n], ALU.mult, ALU.mult),
                 reads=tuple(src_bufs) + (rsb[s], vb), writes=db)
            if post is not None:
                post(kc)

    def load_w(slot_ap, src_ap, ds, slot_buf):
        return k.dma("pool", slot_ap, src_ap, ds, writes=(slot_buf,))

    r2.reset()
    XS = [r2.alloc([128, KC, 512], F32) for _ in range(2)]
    SQ = [r2.alloc([128, KC, 512], BF16) for _ in range(2)]
    RS = [r2.alloc([128, 512], F32)] * 2
    xsb, sqb, rsb = k.bufl(2, "xs"), k.bufl(2, "sq"), [k.buf("rs")] * 2
    assert r2.off <= 51200
    r3.reset()
    SZT = r3.alloc([128, NB, 1024], BF16)
    XBC = r3.alloc([128, 12, NT], BF16)
    sztb = k.bufl(NB, "szt")
    xbcb = k.bufl(12, "xbc")
    wxv = lambda c0: XBC[:, c0:c0 + 2, :].rearrange("p a b -> p (a b)")[:, 0:4096].rearrange("p (a b) -> p a b", a=KC)
    WX = {2: wxv(0), 1: wxv(2)}
    wxb = {2: k.buf("wx2"), 1: k.buf("wx1")}
    wx_ds = {2: k.dsem("wx2"), 1: k.dsem("wx1")}
    for grp in (2, 1):
        load_w(WX[grp][:], wview_kc(w_in, 2048 + 512 * grp, 512), wx_ds[grp], wxb[grp])
    WS = [WSZ[0], WSZ[1], None]
    wsb = [wszb[0], wszb[1], None]
    dtv_ = lambda c0: XBC[0:16, c0:c0 + 2, :].rearrange("p a b -> p (a b)")[:, 0:4096].bitcast(F32).rearrange("p (j t) -> p j t", j=4)
    DTT = [dtv_(4), dtv_(6)]
    dttb = k.bufl(2, "dtt")
    tokb = k.bufl(NB, "tok")
    dtasb = k.buf("dtas")
    def z_block(blk):
        m = 128 if blk < 16 else 64
        c0 = blk * 128
        for half in range(2):
            bi = next_bank()
            for kc in range(KC):
                k.op("pe", lambda e, kc=kc: e.matmul(PS[bi][0:m, :], XN[:, kc, c0:c0 + m], WS[half][:, kc, :],
                                                     start=(kc == 0), stop=(kc == KC - 1)),
                     reads=(XNb[min(blk // 4, 4)], wsb[half]), writes=(PB[bi],), track=(kc == KC - 1))
            k.op("act", lambda e: e.activation(SZT[0:m, blk, half * 512:(half + 1) * 512], PS[bi][0:m, :], AF.Silu),
                 reads=(PB[bi],), writes=(sztb[blk],))
    def dt_a(ti):
        t0, n = TT[ti]
        q = ti % 2
        bi = next_bank()
        for kc in range(KC):
            k.op("pe", lambda e, kc=kc: e.matmul(PS[bi][0:16, 0:n], WDT[:, kc, :], XN[:, kc, t0:t0 + n],
                                                 start=(kc == 0), stop=(kc == KC - 1)),
                 reads=(XNb[ti], wdtb), writes=(PB[bi],), track=(kc == KC - 1))
        raw, tmp, dtv, dta = (DTT[q][:, j, 0:n] for j in range(4))
        k.op("dve", lambda e: e.tensor_scalar(raw, PS[bi][0:16, 0:n], HV[:, 0:1], None, ALU.add), reads=(PB[bi], vb), writes=(dttb[q],))
        k.op("act", lambda e: e.activation(tmp, raw, AF.Abs), reads=(dttb[q],), writes=(dttb[q],))
        k.op("act", lambda e: e.activation(tmp, tmp, AF.Exp, scale=-1.0), reads=(dttb[q],), writes=(dttb[q],))
        k.op("act", lambda e: e.activation(tmp, tmp, AF.Ln, bias=1.0), reads=(dttb[q],), writes=(dttb[q],))
        k.op("dve", lambda e: e.scalar_tensor_tensor(dtv, raw, 0.0, tmp, ALU.max, ALU.add), reads=(dttb[q],), writes=(dttb[q],))
        k.op("dve", lambda e: e.tensor_scalar(dta, dtv, HV[:, 3:4], None, ALU.mult), reads=(dttb[q], vb), writes=(dttb[q],))
        if ti == 4:
            k.op("dve", lambda e: e.tensor_copy(DTAS[:], dta), reads=(dttb[q],), writes=(dtasb,))

    def dt_b(ti):
        t0, n = TT[ti]
        q = ti % 2
        raw, tmp, dtv, dta = (DTT[q][:, j, 0:n] for j in range(4))
        nblk = (n + 127) // 128
        bj = next_bank()
        for b4 in range(nblk):
            m = min(128, n - b4 * 128)
            for j, srcv in enumerate((dtv, dta)):
                k.op("pe", lambda e, j=j, srcv=srcv: e.transpose(PS[bj][0:m, (b4 * 2 + j) * 16:(b4 * 2 + j + 1) * 16],
                                                                 srcv[:, b4 * 128:b4 * 128 + m], IDF[0:16, 0:16]),
                     reads=(dttb[q], cb), writes=(PB[bj],), track=(b4 == nblk - 1 and j == 1))
        for b4 in range(nblk):
            m = min(128, n - b4 * 128)
            blk = ti * 4 + b4
            k.op("act", lambda e: e.copy(TOK[0:m, blk, 0:2, :], PS[bj][0:m, b4 * 32:(b4 + 1) * 32].rearrange("p (j h) -> p j h", j=2)),
                 reads=(PB[bj],), writes=(tokb[blk],))

    xTv = xT.rearrange("(c p) t -> p c t", p=128)

    def norm0(ti):
        t0, n = TT[ti]
        s_ = ti % 2
        k.dma("sp", XS[s_][:, :, 0:n], xTv[:, :, t0:t0 + n], ld_ds[3 + s_], writes=(xsb[s_],))
        rmsnorm_tile(ti, XS[s_][:, :, 0:n], (xsb[s_],), V_GMIX, lambda kc: XN[:, kc, t0:t0 + n], (XNb[ti],),
                     SQ, RS, sqb, rsb)

    def zdt(ti):
        dt_a(ti)
        for blk in ([16] if ti == 4 else range(4 * ti, 4 * ti + 4)):
            z_block(blk)
        dt_b(ti)

    norm0(0)
    norm0(1)
    zdt(0)
    norm0(2)
    zdt(1)
    norm0(3)
    zdt(2)
    norm0(4)
    zdt(3)
    zdt(4)
    dump("xn", XN[:], XNb, BF16)
    k.barrier()

    r2.reset()
    WS[2] = r2.alloc([128, KC, 512], BF16)
    wsb[2] = k.buf("ws2")
    XB = [r2.alloc([128, NP + 3], F32) for _ in range(2)]
    XBS = [r2.alloc([128, 7, 16], F32) for _ in range(2)]
    TC = [r2.alloc([128, NP], F32) for _ in range(2)]
    TCS = [r2.alloc([128, 4, 16], F32) for _ in range(2)]
    SCV = r2.alloc([128, 12, 48], F32)
    xbb, tcb = k.bufl(2, "xb"), k.bufl(2, "tc")
    scvb = k.buf("scv")
    k.dma("sp", SCV[:], sconv.rearrange("p (c f) -> p c f", c=12), ld_ds[5], writes=(scvb,))
    load_w(WS[2][:], wview_kc(w_in, 2048, 512), WS_DS[2], wsb[2])
    assert r2.off <= 51200
    pending_silu = []

    def flush_silu():
        while pending_silu:
            ch_, q_ = pending_silu.pop(0)
            park = (wxb[2],) if ch_ in (0, 1) else ((wxb[1],) if ch_ in (2, 3) else ())
            k.op("act", lambda e: e.activation(XBC[:, ch_, 0:NP], TC[q_][:], AF.Silu),
                 reads=(tcb[q_],), writes=(xbcb[ch_],) + park)
            k.op("act", lambda e: e.activation(XBC[:, ch_, NP:NT].rearrange("p (b l) -> p l b", l=4), TCS[q_][:], AF.Silu),
                 reads=(tcb[q_],), writes=(xbcb[ch_],) + park)

    for grp in (2, 1, 0):
        WG_, wgb_ = (WS[2], wsb[2]) if grp == 0 else (WX[grp], wxb[grp])
        for c4 in range(4):
            ch = grp * 4 + c4
            q = ch % 2
            k.op("pool", lambda e: e.memset(XB[q][:, 0:3], 0.0), writes=(xbb[q],))
            k.op("pool", lambda e: e.tensor_copy(XBS[q][:, 0:3, :], SCV[:, ch, :].rearrange("p (j b) -> p j b", j=3)),
                 reads=(scvb,), writes=(xbb[q],))
            for ti, (t0, n) in enumerate(TT):
                bi = next_bank()
                for kc in range(KC):
                    k.op("pe", lambda e, kc=kc: e.matmul(PS[bi][:, 0:n], WG_[:, kc, c4 * 128:(c4 + 1) * 128], XN[:, kc, t0:t0 + n],
                                                         start=(kc == 0), stop=(kc == KC - 1)),
                         reads=(XNb[ti], wgb_), writes=(PB[bi],), track=(kc == KC - 1))
                if ti < 4:
                    k.op("act", lambda e: e.copy(XB[q][:, 3 + t0:3 + t0 + n], PS[bi][:, 0:n]), reads=(PB[bi],), writes=(xbb[q],))
                else:
                    k.op("act", lambda e: e.copy(XBS[q][:, 3:7, :], PS[bi][:, 0:64].rearrange("p (b l) -> p l b", l=4)),
                         reads=(PB[bi],), writes=(xbb[q],))
            k.dma("sp", nconv_p[:, ch, :], XB[q][:, NP:NP + 3], next_out_ds(), reads=(xbb[q],), is_out=True)
            k.dma("sp", nconv_s[:, ch, :], XBS[q][:, 4:7, :].rearrange("p j b -> p (j b)"), next_out_ds(), reads=(xbb[q],), is_out=True)
            cw = lambda j: VEC[:, V_CW + j * 12 + ch:V_CW + j * 12 + ch + 1]
            bcol = VEC[:, V_CB + ch:V_CB + ch + 1]
            k.op("act", lambda e: e.activation(TC[q][:], XB[q][:, 0:NP], AF.Identity, bias=bcol, scale=cw(0)),
                 reads=(xbb[q], vb), writes=(tcb[q],))
            k.op("act", lambda e: e.activation(TCS[q][:], XBS[q][:, 0:4, :], AF.Identity, bias=bcol, scale=cw(0)),
                 reads=(xbb[q], vb), writes=(tcb[q],))
            flush_silu()
            for j in range(1, 4):
                k.op("dve", lambda e, j=j: e.scalar_tensor_tensor(TC[q][:], XB[q][:, j:j + NP], cw(j), TC[q][:], ALU.mult, ALU.add),
                     reads=(xbb[q], vb, tcb[q]), writes=(tcb[q],))
                k.op("dve", lambda e, j=j: e.scalar_tensor_tensor(TCS[q][:], XBS[q][:, j:j + 4, :], cw(j), TCS[q][:], ALU.mult, ALU.add),
                     reads=(xbb[q], vb, tcb[q]), writes=(tcb[q],))
            pending_silu.append((ch, q))
    flush_silu()
    dump("szt", SZT[:], sztb, BF16)
    dump("xbc", XBC[:], xbcb, BF16)
    dump("tok01", TOK[:], tokb)
    k.barrier()

    r2h.reset()
    TMPA = r2h.alloc([128, NB, 16], F32)
    TMPB = r2h.alloc([128, NB, 16], F32)
    EXPM = r2h.alloc([16, 8, 128], F32)
    CDF = r2h.alloc([16, 16], F32)
    tmpb_ = k.bufl(2, "tmpab")
    expb, cdfb, cdxb = k.buf("expm"), k.buf("cdf"), k.buf("cdx")
    ACUM_ps = PS[0][:, 0:NB * 16].rearrange("p (b h) -> p b h", h=16)
    ATOT_ps = PS[1][:, 0:NB * 16].rearrange("p (b h) -> p b h", h=16)
    for blk in range(NB):
        m = 128 if blk < 16 else 64
        tri = TRIF[:] if blk < 16 else TRIFS[:]
        one = ONESF[:] if blk < 16 else BDFS[:]
        k.op("pe", lambda e: e.matmul(ACUM_ps[0:m, blk, :], tri, TOK[0:m, blk, 1, :], start=True, stop=True),
             reads=(tokb[blk], cb), writes=(PB[0],), track=False)
        k.op("pe", lambda e: e.matmul(ATOT_ps[0:m, blk, :], one, TOK[0:m, blk, 1, :], start=True, stop=True),
             reads=(tokb[blk], cb), writes=(PB[1],), track=(blk == NB - 1))
    for (p0, p1, b0, b1) in ((0, 128, 0, 16), (0, 64, 16, 17)):
        tb = tokb[b0:b1]
        k.op("act", lambda e: e.activation(TOK[p0:p1, b0:b1, 2, :], ACUM_ps[p0:p1, b0:b1, :], AF.Exp), reads=(PB[0],), writes=tb)
        k.op("act", lambda e: e.activation(TOK[p0:p1, b0:b1, 3, :], ATOT_ps[p0:p1, b0:b1, :], AF.Exp), reads=(PB[1],), writes=tb)
        k.op("act", lambda e: e.copy(TMPA[p0:p1, b0:b1, :], ACUM_ps[p0:p1, b0:b1, :]), reads=(PB[0],), writes=(tmpb_[0],))
        k.op("dve", lambda e: e.tensor_tensor(TMPB[p0:p1, b0:b1, :], ATOT_ps[p0:p1, b0:b1, :], TMPA[p0:p1, b0:b1, :], ALU.subtract),
             reads=(PB[1], tmpb_[0]), writes=(tmpb_[1],))
        k.op("act", lambda e: e.activation(TMPB[p0:p1, b0:b1, :], TMPB[p0:p1, b0:b1, :], AF.Exp), reads=(tmpb_[1],), writes=(tmpb_[1],))
        k.op("dve", lambda e: e.tensor_tensor(TOK[p0:p1, b0:b1, 4, :], TMPB[p0:p1, b0:b1, :], TOK[p0:p1, b0:b1, 0, :], ALU.mult),
             reads=[tmpb_[1]] + tb, writes=tb)
        k.op("dve", lambda e: e.tensor_copy(AHL[p0:p1, b0:b1, 0, :], TOK[p0:p1, b0:b1, 1, :]), reads=tb, writes=tb)
        k.op("dve", lambda e: e.tensor_copy(TMPA[p0:p1, b0:b1, :], AHL[p0:p1, b0:b1, 0, :]), reads=tb + [tmpb_[0]], writes=(tmpb_[0],))
        k.op("dve", lambda e: e.tensor_tensor(AHL[p0:p1, b0:b1, 1, :], TOK[p0:p1, b0:b1, 1, :], TMPA[p0:p1, b0:b1, :], ALU.subtract),
             reads=tb + [tmpb_[0]], writes=tb)
    k.op("dve", lambda e: e.tensor_reduce(out=CDF[:], in_=DTAS[:].rearrange("p (b l) -> p b l", l=4), axis=AX.X, op=ALU.add),
         reads=(dtasb,), writes=(cdfb,))
    k.op("act", lambda e: e.activation(CDF[:], CDF[:], AF.Exp), reads=(cdfb,), writes=(cdfb,))
    k.op("pool", lambda e: e.memset(EXPM[:], 1.0), writes=(expb,))
    expv = EXPM[:].rearrange("p j (a d) -> p j a d", a=2)
    k.op("pool", lambda e: e.affine_select(out=expv, in_=expv, pattern=[[-2, 8], [-1, 2], [0, 64]], compare_op=ALU.is_equal,
                                           fill=0.0, base=0, channel_multiplier=1), reads=(expb,), writes=(expb,))
    for j in range(8):
        k.op("pe", lambda e, j=j: e.matmul(PS[2][:, j * 16:(j + 1) * 16], EXPM[:, j, :], CDF[:], start=True, stop=True),
             reads=(expb, cdfb), writes=(PB[2],), track=(j == 7))
    k.op("act", lambda e: e.copy(CDX[:], PS[2][:, 0:128].rearrange("p (j b) -> p j b", j=8)), reads=(PB[2],), writes=(cdxb,))
    dump("tok", TOK[:], tokb)
    k.barrier()

    xtokb, xwb, btokb, cbmb, mmb, yab_g, ynb_g, sdb, sstb, hpb, smb, xdtb = (k.buf(n) for n in
        ("xtok", "xw", "btok", "cbm", "mm", "ya", "yn", "sd", "sst", "hp", "small", "xdt"))
    rhb, decb = k.bufl(4, "rh"), k.bufl(4, "dec")
    ysb = k.bufl(NB, "ys")
    XT_ps = PS[0][:].bitcast(BF16)
    BT_ps = PS[1][:, 0:128].bitcast(BF16)
    CB_ps = PS[1][:, 256:512].rearrange("p (g l) -> p g l", g=2)
    bc16 = lambda ap, m: ap.unsqueeze(2).broadcast_to([m, 16, 64])
    v3 = lambda ap: ap.rearrange("p (h d) -> p h d", h=16)

    def ssd_alloc(m):
        r2h.reset()
        T = {}
        T["XTOK"] = r2h.alloc([128, 1024], BF16)
        T["XW"] = r2h.alloc([128, 1024], BF16)
        T["XDT"] = r2h.alloc([128, 1024], BF16)
        T["BTOK"] = r2h.alloc([128, 256], BF16)
        T["CBM"] = r2h.alloc([128, 2, m], F32)
        T["RH"] = [r2h.alloc([128, 4, m], BF16) for _ in range(4)]
        T["DEC"] = [r2h.alloc([128, 4, m], F32) for _ in range(4)]
        T["MM"] = r2h.alloc([128, 16, m], BF16)
        return T

    def ssd_A1(blk, T):
        m = 128 if blk < 16 else 64
        t0 = blk * 128
        trib = TRIB[:] if blk < 16 else TRIBS[:]
        maskf = TRIF[:] if blk < 16 else TRIFS[:]
        XTOK, XW, BTOK, CBM, RH = (T[n] for n in ("XTOK", "XW", "BTOK", "CBM", "RH"))
        for j in range(8):
            k.op("pe", lambda e, j=j: e.transpose(XT_ps[0:m, j * 128:(j + 1) * 128], XBC[:, j, t0:t0 + m], IDB[:]),
                 reads=(xbcb[j], cb), writes=(PB[0],), track=(j == 7))
        for g in range(2):
            k.op("pe", lambda e, g=g: e.transpose(BT_ps[0:m, g * 128:(g + 1) * 128], XBC[:, 8 + g, t0:t0 + m], IDB[:]),
                 reads=(xbcb[8 + g], cb), writes=(PB[1],), track=False)
        for g in range(2):
            k.op("pe", lambda e, g=g: e.matmul(CB_ps[0:m, g, 0:m], XBC[:, 8 + g, t0:t0 + m], XBC[:, 10 + g, t0:t0 + m], start=True, stop=True),
                 reads=(xbcb[8 + g], xbcb[10 + g]), writes=(PB[1],), track=(g == 1))
        for q in range(4):
            k.op("pool", lambda e, q=q: e.tensor_tensor(RH[q][0:m, :, :], trib[0:m, 0:m].unsqueeze(1).broadcast_to([m, 4, m]),
                                                        AHL[0:m, blk, 0, 4 * q:4 * q + 4].unsqueeze(2).broadcast_to([m, 4, m]), ALU.mult),
                 reads=(cb, tokb[blk]), writes=(rhb[q],))
        k.op("act", lambda e: e.copy(XTOK[0:m, :], XT_ps[0:m, :]), reads=(PB[0],), writes=(xtokb,))
        k.op("act", lambda e: e.copy(BTOK[0:m, :], BT_ps[0:m, :]), reads=(PB[1],), writes=(btokb,))
        k.op("dve", lambda e: e.tensor_tensor(CBM[0:m, :, :], CB_ps[0:m, :, 0:m], maskf[0:m, 0:m].unsqueeze(1).broadcast_to([m, 2, m]), ALU.mult),
             reads=(PB[1], cb), writes=(cbmb,))
        k.op("dve", lambda e: e.tensor_tensor(v3(T["XDT"][0:m, :]), v3(XT_ps[0:m, :]), bc16(TOK[0:m, blk, 0, :], m), ALU.mult),
             reads=(PB[0], tokb[blk]), writes=(xdtb,))
        k.op("pool", lambda e: e.tensor_tensor(v3(XW[0:m, :]), v3(XTOK[0:m, :]), bc16(TOK[0:m, blk, 4, :], m), ALU.mult),
             reads=(xtokb, tokb[blk]), writes=(xwb,))

    def ssd_A2(blk, T, part):
        m = 128 if blk < 16 else 64
        ub = UB[:] if blk < 16 else UBS[:]
        CBM, RH, DEC, MM = (T[n] for n in ("CBM", "RH", "DEC", "MM"))
        for q in range(4):
            bq = 2 + q % 2
            segv = PS[bq][:, 0:4 * m].rearrange("p (h l) -> p h l", h=4)
            if part == 0:
                k.op("pe", lambda e: e.matmul(PS[bq][0:m, 0:4 * m], ub[0:m, 0:m], RH[q][0:m, :, :].rearrange("p h l -> p (h l)"), start=True, stop=True),
                     reads=(cb, rhb[q]), writes=(PB[bq],))
                k.op("act", lambda e: e.activation(DEC[q][0:m, :, :], segv[0:m, :, :], AF.Exp), reads=(PB[bq],), writes=(decb[q],))
            else:
                g = q // 2
                k.op("dve", lambda e: e.tensor_tensor(MM[0:m, 4 * q:4 * q + 4, :], DEC[q][0:m, :, :],
                                                      CBM[0:m, g, :].unsqueeze(1).broadcast_to([m, 4, m]), ALU.mult),
                     reads=(decb[q], cbmb), writes=(mmb,))

    def ssd_Y(blk, T):
        m = 128 if blk < 16 else 64
        XTOK, MM, XDT = T["XTOK"], T["MM"], T["XDT"]
        for h in range(16):
            by = 4 + h // 8
            hc = (h % 8) * 64
            k.op("pe", lambda e: e.matmul(PS[by][0:m, hc:hc + 64], MM[0:m, h, :], XDT[0:m, h * 64:(h + 1) * 64],
                                          start=(h % 8 == 0), stop=False, skip_group_check=True),
                 reads=(mmb, xdtb), writes=(PB[by],), track=False)
            k.op("pe", lambda e: e.matmul(PS[by][0:m, hc:hc + 64], DI[0:m, h, 0:m], XTOK[0:m, h * 64:(h + 1) * 64],
                                          start=False, stop=True, skip_group_check=True),
                 reads=(cb, xtokb), writes=(PB[by],), track=(h % 8 == 7))

    def ssd_P(blk, have_off, YA, YN, part, yab=None, ynb=None):
        yab = yab if yab is not None else yab_g
        ynb = ynb if ynb is not None else ynb_g
        m = 128 if blk < 16 else 64
        t0 = blk * 128
        YT_ps = PS[0][:].bitcast(BF16).rearrange("p (j t) -> p j t", j=8)
        if part == 1:
            k.op("act", lambda e: e.copy(YMIX[:, 8:16, t0:t0 + m], YT_ps[:, :, 0:m]), reads=(PB[0],), writes=(ysb[blk],))
            return
        for g in range(2) if part in (0, "a") else ():
            cs = slice(g * 512, (g + 1) * 512)
            if have_off:
                k.op("dve", lambda e: e.tensor_tensor(YA[0:m, cs].rearrange("p (h d) -> p h d", h=8), PS[6 + g][0:m, :].rearrange("p (h d) -> p h d", h=8),
                                                      TOK[0:m, blk, 2, 8 * g:8 * g + 8].unsqueeze(2).broadcast_to([m, 8, 64]), ALU.mult),
                     reads=(PB[6 + g], tokb[blk]), writes=(yab,))
                k.op("dve", lambda e: e.tensor_tensor(YA[0:m, cs], YA[0:m, cs], PS[4 + g][0:m, :], ALU.add),
                     reads=(PB[4 + g], yab), writes=(yab,))
                k.op("dve", lambda e: e.tensor_tensor(YA[0:m, cs], YA[0:m, cs], SZT[0:m, blk, cs], ALU.mult),
                     reads=(yab, sztb[blk]), writes=(yab,))
            else:
                k.op("dve", lambda e: e.tensor_tensor(YA[0:m, cs], PS[4 + g][0:m, :], SZT[0:m, blk, cs], ALU.mult),
                     reads=(PB[4 + g], sztb[blk]), writes=(yab,))
        if part == "a":
            return
        k.op("act", lambda e: e.activation(YN[0:m, :], YA[0:m, :], AF.Square, accum_out=SMALL[0:m, 0:1]), reads=(yab,), writes=(ynb, smb))
        k.op("dve", lambda e: e.tensor_scalar(SMALL[0:m, 1:2], SMALL[0:m, 0:1], 1.0 / 1024, EPS, ALU.mult, ALU.add), reads=(smb,), writes=(smb,))
        k.op("act", lambda e: e.activation(SMALL[0:m, 2:3], SMALL[0:m, 1:2], AF.Ln), reads=(smb,), writes=(smb,))
        k.op("act", lambda e: e.activation(SMALL[0:m, 3:4], SMALL[0:m, 2:3], AF.Exp, scale=-0.5), reads=(smb,), writes=(smb,))
        k.op("act", lambda e: e.activation(YN[0:m, :], YA[0:m, :], AF.Copy, scale=SMALL[0:m, 3:4]), reads=(yab, smb), writes=(ynb,))
        for j in range(8):
            k.op("pe", lambda e, j=j: e.transpose(YT_ps[:, j, 0:m], YN[0:m, j * 128:(j + 1) * 128], IDB[0:m, 0:m]),
                 reads=(ynb, cb), writes=(PB[0],), track=(j == 7))

    T = ssd_alloc(128)
    YA = r2h.alloc([128, 1024], F32)
    YN = r2h.alloc([128, 1024], BF16)

    def ssd_S(blk, part):
        if part == 0:
            for g in range(2):
                k.op("pe", lambda e, g=g: e.matmul(PS[2 + g][:, :], T["BTOK"][:, g * 128:(g + 1) * 128], T["XW"][:, g * 512:(g + 1) * 512], start=True, stop=True),
                     reads=(btokb, xwb), writes=(PB[2 + g],))
            return
        if blk == 0:
            for g in range(2):
                k.op("act", lambda e, g=g: e.copy(SST[:, g * 512:(g + 1) * 512], PS[2 + g][:, :]), reads=(PB[2 + g],), writes=(sstb,))
        else:
            k.op("dve", lambda e: e.tensor_tensor(v3(SST[:, :]), v3(SST[:, :]), bc16(TOK[:, blk, 3, :], 128), ALU.mult),
                 reads=(sstb, tokb[blk]), writes=(sstb,))
            for g in range(2):
                k.op("dve", lambda e, g=g: e.tensor_tensor(SST[:, g * 512:(g + 1) * 512], SST[:, g * 512:(g + 1) * 512], PS[2 + g][:, :], ALU.add),
                     reads=(sstb, PB[2 + g]), writes=(sstb,))
        if blk < 15:
            k.op("act", lambda e: e.copy(HP[:, :], SST[:, :]), reads=(sstb,), writes=(hpb,))

    ssd_A1(0, T)
    ssd_A2(0, T, 0)
    ssd_A2(0, T, 1)
    for blk in range(16):
        t0 = blk * 128
        if blk > 0:
            for g in range(2):
                k.op("pe", lambda e, g=g: e.matmul(PS[6 + g][:, :], XBC[:, 10 + g, t0:t0 + 128], HP[:, g * 512:(g + 1) * 512], start=True, stop=True),
                     reads=(xbcb[10 + g], hpb), writes=(PB[6 + g],))
        ssd_S(blk, 0)
        ssd_Y(blk, T)
        if blk > 0:
            ssd_P(blk - 1, blk > 1, YA, YN, "b")
            ssd_P(blk - 1, blk > 1, YA, YN, 1)
        ssd_S(blk, 1)
        ssd_P(blk, blk > 0, YA, YN, "a")
        if blk < 15:
            ssd_A1(blk + 1, T)
            ssd_A2(blk + 1, T, 0)
            ssd_A2(blk + 1, T, 1)
    ssd_P(15, True, YA, YN, "b")
    ssd_P(15, True, YA, YN, 1)
    k.dma("sp", nssm_p, SST[:, :], next_out_ds(), reads=(sstb,), is_out=True)
    k.barrier()

    blk = 16
    t0 = NP
    T = ssd_alloc(64)
    CZb = [r2h.alloc([128, 2, 64], BF16) for _ in range(2)]
    BZb = [r2h.alloc([64, 256], BF16) for _ in range(2)]
    xbf = lambda j: XBC[:, j, 0:2048].bitcast(F32).rearrange("p (j n) -> p j n", j=8)
    H0 = [xbf(j) for j in range(6)] + [SST[:, :].rearrange("p (j n) -> p j n", j=8)]
    H0B = [r2h.alloc([128, 8, 128], BF16), HP[:, :].rearrange("p (j n) -> p j n", j=8)]
    H0T = [r2h.alloc([128, 1024], BF16) for _ in range(2)]
    szf = lambda i: SZT[:, 2 * i:2 * i + 2, :].rearrange("p a b -> p (a b)").bitcast(F32).rearrange("p (j n) -> p j n", j=8)
    OST = [szf(5), szf(6), szf(7)]
    NH = len(H0)
    NO = len(OST)
    czb, bzb, h0b, h0bb, h0tb = k.bufl(2, "cz"), k.bufl(2, "bz"), k.bufl(6, "h0") + [sstb], [k.buf("h0b"), hpb], k.bufl(2, "h0t")
    ostb = k.bufl(NO, "ost")
    h0_ds = [k.dsem("h0") for _ in range(NH)]
    ost_ds = [k.dsem("ost") for _ in range(NO)]
    WSU = [R3[:, 4096 * i:4096 * (i + 1)].rearrange("p (a b) -> p a b", a=KC) for i in range(2)]
    WPOOL = R3[:, 8192:10240].rearrange("p (g c e) -> p g c e", g=4, c=2)
    wsub, wpb = k.bufl(2, "wsu"), k.buf("wpool")
    load_w(WSU[0][:], wview_kc(w_in, 0, 512), WS_DS[0], wsub[0])
    load_w(WSU[1][:], wview_kc(w_in, 512, 512), WS_DS[1], wsub[1])
    k.dma("pool", WPOOL[:].rearrange("p g c e -> p (g c) e"), w_pool.rearrange("g (c p) e -> p (g c) e", p=128), ld_ds[7], writes=(wpb,))
    h0src = lambda b: sssm[b].rearrange("(j a) p n -> (a p) j n", a=2)
    for b in range(NH):
        k.dma("sp", H0[b][:], h0src(b), h0_ds[b], writes=(h0b[b],))
    ssd_A1(blk, T)
    ssd_A2(blk, T, 0)
    ssd_A2(blk, T, 1)
    HT_pss = [PS[2][:].bitcast(BF16), PS[3][:].bitcast(BF16)]
    for b in range(16):
        s = b % 2
        s3 = b % NH
        so = b % NO
        k.op("act", lambda e: e.copy(H0B[s][:], H0[s3][:]), reads=(h0b[s3],), writes=(h0bb[s],))
        HT_ps = HT_pss[s]
        for j in range(8):
            k.op("pe", lambda e, j=j: e.transpose(HT_ps[:, j * 128:(j + 1) * 128], H0B[s][:, j, :], IDB[:]),
                 reads=(h0bb[s], cb), writes=(PB[2 + s],), track=(j == 7))
        k.op("act", lambda e: e.copy(H0T[s][:, :], HT_ps[:, :]), reads=(PB[2 + s],), writes=(h0tb[s],))
        k.op("pool", lambda e: e.tensor_tensor(CZb[s][:], XBC[:, 10:12, t0:t0 + 64], BDROW[:, b, :].unsqueeze(1).broadcast_to([128, 2, 64]), ALU.mult),
             reads=(xbcb[10], xbcb[11], cb), writes=(czb[s],))
        k.op("pool", lambda e: e.tensor_tensor(BZb[s][:], T["BTOK"][0:64, :], BDCOL[:, b:b + 1].broadcast_to([64, 256]), ALU.mult),
             reads=(btokb, cb), writes=(bzb[s],))
        for g in range(2):
            k.op("pe", lambda e, g=g: e.matmul(PS[6 + g][0:64, :], CZb[s][:, g, :], H0T[s][:, g * 512:(g + 1) * 512], start=(b == 0), stop=(b == 15)),
                 reads=(czb[s], h0tb[s]), writes=(PB[6 + g],), track=(b == 15 or g == 1))
        cb0 = 4 if s == 0 else 0
        csv = lambda j: PS[cb0 + j // 4][:, (j % 4) * 128:(j % 4 + 1) * 128]
        for j in range(8):
            k.op("pe", lambda e, j=j: e.matmul(csv(j), T["XW"][0:64, j * 128:(j + 1) * 128], BZb[s][:, (j // 4) * 128:(j // 4 + 1) * 128], start=True, stop=True),
                 reads=(xwb, bzb[s]), writes=(PB[cb0 + j // 4],), track=(j % 4 == 3))
        for j in range(8):
            k.op("dve", lambda e, j=j: e.scalar_tensor_tensor(OST[so][:, j, :], H0[s3][:, j, :], CDX[:, j, b:b + 1], csv(j), ALU.mult, ALU.add),
                 reads=(cdxb, PB[cb0 + j // 4], h0b[s3]), writes=(ostb[so],), track=(j == 7))
        k.dma("sp", nssm_s[b].rearrange("(j a) p n -> (a p) j n", a=2), OST[so][:], ost_ds[so], reads=(ostb[so],), is_out=True)
        if b + NH < 16:
            k.dma("sp", H0[s3][:], h0src(b + NH), h0_ds[s3], writes=(h0b[s3],))
    ssd_Y(blk, T)
    ssd_P(blk, True, SST, HP, 0, sstb, hpb)
    ssd_P(blk, True, SST, HP, 1, sstb, hpb)
    dump("ys", YMIX[:, 8:16, :], ysb, BF16)
    k.barrier()

    r3.reset()
    r3.off = 20480
    WS = WSU
    wsb = wsub
    U = [r3.alloc([128, 16 + NP], F32) for _ in range(2)]
    US = [r3.alloc([128, 19, 16], F32) for _ in range(2)]
    TA = [r3.alloc([128, 16 + NP], F32) for _ in range(2)]
    TAS = [r3.alloc([128, 19, 16], F32) for _ in range(2)]
    DD = [r3.alloc([128, 2, NT], BF16) for _ in range(2)]
    SPL = r3.alloc([128, 8, 240], F32)
    ub_, tab, ddb = k.bufl(2, "u"), k.bufl(2, "ta"), k.bufl(2, "dd")
    splb = k.buf("spl")
    ypb = k.bufl(8, "yp")
    k.dma("sp", SPL[:], spool.rearrange("p (c f) -> p c f", c=8), ld_ds[6], writes=(splb,))
    for q in range(2):
        k.op("pool", lambda e, q=q: e.memset(U[q][:, 0:16], 0.0), writes=(ub_[q],))
    pending_pool = []

    def flush_pool():
        while pending_pool:
            g_, dq_ = pending_pool.pop(0)
            for ec in range(2):
                for ti, (t0, n) in enumerate(TT):
                    bi = next_bank()
                    for k2 in range(2):
                        k.op("pe", lambda e, k2=k2: e.matmul(PS[bi][:, 0:n], WPOOL[:, g_, k2, ec * 128:(ec + 1) * 128], DD[dq_][:, k2, t0:t0 + n],
                                                             start=(k2 == 0), stop=(k2 == 1)),
                             reads=(wpb, ddb[dq_]), writes=(PB[bi],), track=(k2 == 1))
                    yc = 2 * g_ + ec
                    k.op("act", lambda e: e.activation(YMIX[:, yc, t0:t0 + n], PS[bi][:, 0:n], AF.Copy, scale=VEC[:, V_PSC + yc:V_PSC + yc + 1]),
                         reads=(PB[bi], vb), writes=(ypb[yc],))

    for ci, c in enumerate((6, 7, 4, 5, 2, 3, 0, 1)):
        grp, c4 = c // 4, c % 4
        q = c % 2
        g = c // 2
        w = POOLW[g]
        k.op("pool", lambda e: e.tensor_copy(US[q][:, 0:15, :], SPL[:, c, :].rearrange("p (j b) -> p j b", j=15)),
             reads=(splb,), writes=(ub_[q],))
        for ti, (t0, n) in enumerate(TT):
            bi = next_bank()
            for kc in range(KC):
                k.op("pe", lambda e, kc=kc: e.matmul(PS[bi][:, 0:n], WS[grp][:, kc, c4 * 128:(c4 + 1) * 128], XN[:, kc, t0:t0 + n],
                                                     start=(kc == 0), stop=(kc == KC - 1)),
                     reads=(XNb[ti], wsb[grp]), writes=(PB[bi],), track=(kc == KC - 1))
            if ti < 4:
                k.op("act", lambda e: e.copy(U[q][:, 16 + t0:16 + t0 + n], PS[bi][:, 0:n]), reads=(PB[bi],), writes=(ub_[q],))
            else:
                k.op("act", lambda e: e.copy(US[q][:, 15:19, :], PS[bi][:, 0:64].rearrange("p (b l) -> p l b", l=4)),
                     reads=(PB[bi],), writes=(ub_[q],))
        flush_pool()
        if ci == 7:
            XNf = XN[:].rearrange("p c t -> p (c t)")
            WO = [XNf[:, 4096 * i:4096 * (i + 1)].rearrange("p (a b) -> p a b", a=16) for i in range(4)]
            wob = k.bufl(4, "wo")
            wo_ds = [k.dsem("wo") for _ in range(4)]
            for i in range(4):
                k.dma("pool", WO[i][:], w_out.rearrange("(kc p) n -> p kc n", p=128)[:, :, i * 256:(i + 1) * 256], wo_ds[i],
                      writes=[wob[i]] + XNb)
        k.dma("sp", npool_p[:, c, :], U[q][:, 16 + NP - 15:16 + NP], next_out_ds(), reads=(ub_[q],), is_out=True)
        k.dma("sp", npool_s[:, c, :], US[q][:, 4:19, :].rearrange("p j b -> p (j b)"), next_out_ds(), reads=(ub_[q],), is_out=True)
        src, srcs, srcb = U[q], US[q], ub_[q]
        step = 1
        pp = 0
        while step < w:
            dst, dsts, dstb = TA[pp], TAS[pp], tab[pp]
            lo = 2 * step - 1
            k.op("dve", lambda e, src=src, dst=dst, lo=lo, step=step: e.tensor_tensor(dst[:, lo:], src[:, lo:], src[:, lo - step:16 + NP - step], ALU.add),
                 reads=(srcb,), writes=(dstb,))
            k.op("dve", lambda e, srcs=srcs, dsts=dsts, lo=lo, step=step: e.tensor_tensor(dsts[:, lo:, :], srcs[:, lo:, :], srcs[:, lo - step:19 - step, :], ALU.add),
                 reads=(srcb,), writes=(dstb,))
            src, srcs, srcb = dst, dsts, dstb
            pp = 1 - pp
            step *= 2
        dq = g % 2
        cc = c % 2
        k.op("dve", lambda e: e.scalar_tensor_tensor(DD[dq][:, cc, 0:NP], src[:, 16:16 + NP], 1.0 / w, U[q][:, 16:16 + NP], ALU.mult, ALU.subtract),
             reads=(srcb, ub_[q]), writes=(ddb[dq],))
        k.op("dve", lambda e: e.tensor_tensor(SMALL[:, 16:32], src[:, 16:32], INVC[:, g, :], ALU.mult), reads=(srcb, cb), writes=(smb,))
        k.op("dve", lambda e: e.tensor_tensor(DD[dq][:, cc, 0:16], SMALL[:, 16:32], U[q][:, 16:32], ALU.subtract), reads=(smb, ub_[q]), writes=(ddb[dq],))
        k.op("dve", lambda e: e.scalar_tensor_tensor(DD[dq][:, cc, NP:NT].rearrange("p (b l) -> p l b", l=4), srcs[:, 15:19, :], 1.0 / w,
                                                     US[q][:, 15:19, :], ALU.mult, ALU.subtract),
             reads=(srcb, ub_[q]), writes=(ddb[dq],))
        if cc == 1:
            pending_pool.append((g, dq))
    flush_pool()
    dump("yp", YMIX[:, 0:8, :], ypb, BF16)
    k.barrier(keep=list(zip(wo_ds, wob)))

    Hb = [k.bufl(5, "h%d_" % c) for c in range(KC)]
    hld = [k.dsem("hld") for _ in range(KC)]
    for c in range(KC):
        k.dma("sp", H[:, c, :], xT[c * 128:(c + 1) * 128, :], hld[c], reads=(Hb[c - 1][0],) if c else (), writes=Hb[c])
    for i in range(4):
        k.op("dve", lambda e, i=i: e.tensor_tensor(WO[i][:, 8:16, :], WO[i][:, 8:16, :],
                                                   VEC[:, V_GSSM:V_GSSM + 8].unsqueeze(2).broadcast_to([128, 8, 256]), ALU.mult),
             reads=(wob[i], vb), writes=(wob[i],))
    for cg in range(4):
        s = cg
        for oc in range(2):
            c = cg * 2 + oc
            for ti, (t0, n) in enumerate(TT):
                bi = next_bank()
                for kc in range(16):
                    k.op("pe", lambda e, kc=kc: e.matmul(PS[bi][:, 0:n], WO[s][:, kc, oc * 128:(oc + 1) * 128], YMIX[:, kc, t0:t0 + n],
                                                         start=(kc == 0), stop=(kc == 15)),
                         reads=(wob[s],), writes=(PB[bi],), track=(kc == 15))
                k.op("dve", lambda e: e.tensor_tensor(H[:, c, t0:t0 + n], H[:, c, t0:t0 + n], PS[bi][:, 0:n], ALU.add),
                     reads=(PB[bi], Hb[c][ti]), writes=(Hb[c][ti],))
    dump("h1", H[:], [b for l in Hb for b in l])
    k.barrier()

    def norm_from_h(gcol):
        r3x.reset()
        SQ = [r3x.alloc([128, KC, 512], BF16)] * 2
        RS = [r3x.alloc([128, 512], F32) for _ in range(2)]
        sqb, rsb = [k.buf("sq")] * 2, k.bufl(2, "rs")
        def tile(ti):
            t0, n = TT[ti]
            rmsnorm_tile(ti, H[:, :, t0:t0 + n], [Hb[c][ti] for c in range(KC)], gcol,
                         lambda kc: XN[:, kc, t0:t0 + n], (XNb[ti],), SQ, RS, sqb, rsb)
        return [sqb[0]] + rsb, tile

    WS = [R2[:, 16896 + 4096 * i:16896 + 4096 * (i + 1)].rearrange("p (a b) -> p a b", a=KC) for i in range(3)]
    wsb = k.bufl(3, "wsf")
    wi = [0]

    def next_ws(src):
        s = wi[0] % 3
        wi[0] += 1
        load_w(WS[s][:], src, WS_DS[s], wsb[s])
        return s
    w2v = w_ff2.rearrange("(fc p) n -> p fc n", p=128)
    pref = [next_ws(wview_kc(w_ff1, 0, 512)), next_ws(wview_kc(w_ff1, 512, 512)), next_ws(w2v[:, 0:8, 0:512])]
    n2bufs, norm2_tile = norm_from_h(V_GMLP)
    norm2_tile(0)
    norm2_tile(1)
    r2.reset()
    A = r2.alloc([128, KC, NT], BF16)
    r2.off += 3 * 8192
    RT = [r2.alloc([128, 512], F32) for _ in range(2)]
    WPLE = r2.alloc([128, 2, D], BF16)
    wple_off = r2.off
    rtb = k.bufl(2, "rt")
    ab = [k.bufl(5, "a%d_" % c) for c in range(KC)]
    rti = [0]
    for G in range(4):
        for sub in range(2):
            s = pref.pop(0) if pref else next_ws(wview_kc(w_ff1, G * 1024 + sub * 512, 512))
            for c4 in range(4):
                fc = sub * 4 + c4
                for ti, (t0, n) in enumerate(TT):
                    bi = next_bank()
                    for kc in range(KC):
                        k.op("pe", lambda e, kc=kc: e.matmul(PS[bi][:, 0:n], WS[s][:, kc, c4 * 128:(c4 + 1) * 128], XN[:, kc, t0:t0 + n],
                                                             start=(kc == 0), stop=(kc == KC - 1)),
                             reads=(XNb[ti], wsb[s]), writes=(PB[bi],), track=(kc == KC - 1))
                    if (G, sub, c4) == (0, 0, 0) and ti + 2 < 5:
                        norm2_tile(ti + 2)
                    r = rti[0] % 2
                    rti[0] += 1
                    k.op("act", lambda e: e.activation(RT[r][:, 0:n], PS[bi][:, 0:n], AF.Relu), reads=(PB[bi],), writes=(rtb[r],))
                    k.op("dve", lambda e: e.tensor_tensor(A[:, fc, t0:t0 + n], RT[r][:, 0:n], RT[r][:, 0:n], ALU.mult),
                         reads=(rtb[r],), writes=(ab[fc][ti],))
        for half in range(2):
            s = pref.pop(0) if pref else next_ws(w2v[:, G * 8:(G + 1) * 8, half * 512:(half + 1) * 512])
            for oc in range(4):
                c = half * 4 + oc
                for ti, (t0, n) in enumerate(TT):
                    bi = next_bank()
                    for fc in range(KC):
                        k.op("pe", lambda e, fc=fc: e.matmul(PS[bi][:, 0:n], WS[s][:, fc, oc * 128:(oc + 1) * 128], A[:, fc, t0:t0 + n],
                                                             start=(fc == 0), stop=(fc == KC - 1)),
                             reads=(ab[fc][ti], wsb[s]), writes=(PB[bi],), track=(fc == KC - 1))
                    k.op("dve", lambda e: e.tensor_tensor(H[:, c, t0:t0 + n], H[:, c, t0:t0 + n], PS[bi][:, 0:n], ALU.add),
                         reads=(PB[bi], Hb[c][ti]), writes=(Hb[c][ti],))
    r3x.reset()
    WG = [r3x.alloc([128, KC, 512], BF16) for _ in range(2)]
    wgb = k.bufl(2, "wsg")
    wpleb = k.buf("wple")
    wg_ds = [k.dsem("wg") for _ in range(2)]
    wple_ds = k.dsem("wple")
    for half in range(2):
        k.dma("pool", WG[half][:], wview_kc(w_gate, half * 512, 512), wg_ds[half], writes=[wgb[half]] + n2bufs)
    k.dma("pool", WPLE[:], w_ple.rearrange("(c p) n -> p c n", p=128), wple_ds, writes=(wpleb,))
    dump("h2", H[:], [b for l in Hb for b in l])
    k.barrier(keep=[(wg_ds[0], wgb[0]), (wg_ds[1], wgb[1]), (wple_ds, wpleb)])

    r2.reset()
    SQ = [r2.alloc([128, KC, 512], BF16) for _ in range(2)]
    RS = [r2.alloc([128, 512], F32) for _ in range(2)]
    WS = WG
    PT = r2.alloc([128, 2, NT], BF16)
    SG = [r2.alloc([128, 512], F32) for _ in range(2)]
    YOC = [r2.alloc([128, 512], F32) for _ in range(4)]
    assert r2.off <= wple_off - 4096
    sqb, rsb = k.bufl(2, "sq"), k.bufl(2, "rs")
    wsb, sgb, yocb = wgb, k.bufl(2, "sg"), k.bufl(4, "yoc")
    yoc_ds = [k.dsem("yoc") for _ in range(4)]
    ptb = k.buf("pt")
    k.dma("pool", PT[:], pT.rearrange("(c p) t -> p c t", p=128), ld_ds[9], writes=(ptb,))
    gi = [0]

    def n3(ti):
        t0, n = TT[ti]
        rmsnorm_tile(ti, H[:, :, t0:t0 + n], [Hb[c][ti] for c in range(KC)], V_GPLE,
                     lambda kc: XN[:, kc, t0:t0 + n], (XNb[ti],), SQ, RS, sqb, rsb)

    def gate(ti):
        t0, n = TT[ti]
        for c in range(KC):
            half, oc = c // 4, c % 4
            bi = next_bank()
            for kc in range(KC):
                k.op("pe", lambda e, kc=kc: e.matmul(PS[bi][:, 0:n], WS[half][:, kc, oc * 128:(oc + 1) * 128], XN[:, kc, t0:t0 + n],
                                                     start=(kc == 0), stop=(kc == KC - 1)),
                     reads=(XNb[ti], wsb[half]), writes=(PB[bi],), track=(kc == KC - 1))
            bj = next_bank()
            for k2 in range(2):
                k.op("pe", lambda e, k2=k2: e.matmul(PS[bj][:, 0:n], WPLE[:, k2, c * 128:(c + 1) * 128], PT[:, k2, t0:t0 + n],
                                                     start=(k2 == 0), stop=(k2 == 1)),
                     reads=(wpleb, ptb), writes=(PB[bj],), track=(k2 == 1))
            r = gi[0] % 2
            gi[0] += 1
            k.op("act", lambda e: e.activation(SG[r][:, 0:n], PS[bi][:, 0:n], AF.Sigmoid), reads=(PB[bi],), writes=(sgb[r],))
            k.op("dve", lambda e: e.tensor_tensor(SG[r][:, 0:n], SG[r][:, 0:n], PS[bj][:, 0:n], ALU.mult),
                 reads=(PB[bj], sgb[r]), writes=(sgb[r],))
            k.op("dve", lambda e: e.tensor_tensor(H[:, c, t0:t0 + n], H[:, c, t0:t0 + n], SG[r][:, 0:n], ALU.add),
                 reads=(sgb[r], Hb[c][ti]), writes=(Hb[c][ti],))

    def fn(ti):
        t0, n = TT[ti]

        def post(kc):
            k.dma("sp", yT[kc * 128:(kc + 1) * 128, t0:t0 + n], YOC[kc % 4][:, 0:n], yoc_ds[kc % 4], reads=(yocb[kc % 4],), is_out=True)
        rmsnorm_tile(ti, H[:, :, t0:t0 + n], [Hb[c][ti] for c in range(KC)], V_GFIN,
                     lambda kc: YOC[kc % 4][:, 0:n], lambda kc: (yocb[kc % 4],), SQ, RS, sqb, rsb, post=post)

    for ti in range(5):
        n3(ti)
    gate(0)
    gate(1)
    fn(0)
    gate(2)
    fn(1)
    gate(3)
    fn(2)
    gate(4)
    fn(3)
    fn(4)
    k.finish()
    return k, dbg


_CACHE = {}


def _host_inputs(inp, i):
    f = np.float32
    xs = inp["x_sample"][16 * i:16 * i + 16].reshape(64, D)
    xTc = np.ascontiguousarray(np.concatenate([inp["x_prompt"][i], xs], axis=0).T, dtype=f)
    ps = inp["p_sample"][0, 16 * i:16 * i + 16].reshape(64, 256)
    pTc = np.ascontiguousarray(np.concatenate([inp["p_prompt"][0, i], ps], axis=0).T, dtype=f)
    spool = np.ascontiguousarray(inp["state_pool"][0, 16 * i:16 * i + 16].reshape(16, 15, 8, 128).transpose(3, 2, 1, 0).reshape(128, 8 * 15 * 16), dtype=f)
    sconv = np.ascontiguousarray(inp["state_conv"][0, 16 * i:16 * i + 16].reshape(16, 3, 12, 128).transpose(3, 2, 1, 0).reshape(128, 12 * 3 * 16), dtype=f)
    sssm = np.ascontiguousarray(inp["state_ssm"][0, 16 * i:16 * i + 16], dtype=f)
    return {"xT": xTc, "pT": pTc, "spool": spool, "sconv": sconv, "sssm": sssm}


def _host_shared(inp):
    f = np.float32
    cols = lambda v, n: np.asarray(v, dtype=f).reshape(n, 128).T
    vecs = np.zeros((128, NV), dtype=f)
    vecs[:, V_GMIX:V_GMIX + 8] = cols(inp["norm_mix_g"][0], 8)
    vecs[:, V_GMLP:V_GMLP + 8] = cols(inp["norm_mlp_g"][0], 8)
    vecs[:, V_GPLE:V_GPLE + 8] = cols(inp["norm_ple_g"][0], 8)
    vecs[:, V_GFIN:V_GFIN + 8] = cols(inp["final_norm_g"], 8)
    vecs[:, V_PSC:V_PSC + 8] = cols(inp["pool_scale"][0], 8)
    vecs[:, V_GSSM:V_GSSM + 8] = cols(inp["ssm_norm_g"][0], 8)
    vecs[:, V_CB:V_CB + 12] = cols(inp["conv_b"][0], 12)
    for j in range(4):
        vecs[:, V_CW + 12 * j:V_CW + 12 * j + 12] = cols(inp["conv_w"][0, j], 12)
    hvv = np.stack([inp["dt_bias"][0], inp["a_log"][0], inp["d_skip"][0]], axis=1).astype(f)
    c = np.ascontiguousarray
    return {"w_in": c(inp["w_in"][0], dtype=f), "w_pool": c(inp["w_pool"][0], dtype=f), "w_out": c(inp["w_out"][0], dtype=f),
            "w_ff1": c(inp["w_ff1"][0], dtype=f), "w_ff2": c(inp["w_ff2"][0], dtype=f), "w_gate": c(inp["w_gate"][0], dtype=f),
            "w_ple": c(inp["w_ple"][0], dtype=f), "vecs": vecs, "hv": c(hvv), "dsk": c(inp["d_skip"][0].reshape(1, 16), dtype=f)}


def kernel(debug=False, **inp):
    inp = {n: np.asarray(v) for n, v in inp.items()}
    key = bool(debug)
    if key not in _CACHE:
        _CACHE[key] = build_program(debug=debug)[0]
    kb = _CACHE[key]
    shared = _host_shared(inp)
    in_maps = []
    for i in range(8):
        m = dict(shared)
        m.update(_host_inputs(inp, i))
        in_maps.append(m)
    res = run_bass_kernel_spmd(kb.nc, in_maps, core_ids=list(range(8)))
    R = res.results
    f = np.float32
    y_prompt = np.stack([R[i]["yT"][:, :NP].T for i in range(8)]).astype(f)
    y_sample = np.concatenate([R[i]["yT"][:, NP:].T.reshape(16, 4, D) for i in range(8)]).astype(f)
    pool_p = np.stack([R[i]["npool_p"].transpose(2, 1, 0).reshape(15, D) for i in range(8)])[None].astype(f)
    conv_p = np.stack([R[i]["nconv_p"].transpose(2, 1, 0).reshape(3, 1536) for i in range(8)])[None].astype(f)
    ssm_p = np.stack([R[i]["nssm_p"].reshape(128, 16, 64).transpose(1, 2, 0) for i in range(8)])[None].astype(f)
    pool_s = np.concatenate([R[i]["npool_s"].reshape(128, 8, 15, 16).transpose(3, 2, 1, 0).reshape(16, 15, D) for i in range(8)])[None].astype(f)
    conv_s = np.concatenate([R[i]["nconv_s"].reshape(128, 12, 3, 16).transpose(3, 2, 1, 0).reshape(16, 3, 1536) for i in range(8)])[None].astype(f)
    ssm_s = np.concatenate([R[i]["nssm_s"] for i in range(8)])[None].astype(f)
    outs = (np.ascontiguousarray(y_prompt), np.ascontiguousarray(y_sample), np.ascontiguousarray(pool_p), np.ascontiguousarray(conv_p),
            np.ascontiguousarray(ssm_p), np.ascontiguousarray(pool_s), np.ascontiguousarray(conv_s), np.ascontiguousarray(ssm_s))
    if debug:
        return outs, R
    return outs
```

```python
import numpy as np
from contextlib import ExitStack
import concourse.bass as bass
import concourse.mybir as mybir
from concourse.bass_utils import run_bass_kernel_spmd

F32 = mybir.dt.float32
BF16 = mybir.dt.bfloat16
AF = mybir.ActivationFunctionType
ALU = mybir.AluOpType
AX = mybir.AxisListType

D = 1024
KC = 8
NP = 2048
NS = 64
NT = NP + NS
NB = 17
TT = [(0, 512), (512, 512), (1024, 512), (1536, 512), (2048, 64)]
POOLW = (2, 4, 8, 16)
EPS = 1e-6
NV = 108
V_GMIX, V_GMLP, V_GPLE, V_GFIN, V_PSC, V_GSSM, V_CB, V_CW = 0, 8, 16, 24, 32, 40, 48, 60


class Buf:
    __slots__ = ("name", "w", "r", "excl")

    def __init__(self, name, excl=False):
        self.name = name
        self.w = None
        self.r = {}
        self.excl = excl


class DSem:
    def __init__(self, key, h):
        self.key = key
        self.h = h
        self.count = 0


class KB:
    def __init__(self):
        self.nc = bass.Bass("TRN2", target_bir_lowering=False)
        self.es = ExitStack()
        nc = self.nc
        self.E = {"pe": nc.tensor, "act": nc.scalar, "dve": nc.vector, "pool": nc.gpsimd, "sp": nc.sync}
        self.esem = {e: self.es.enter_context(nc.semaphore("sem_" + e)) for e in ("pe", "act", "dve", "pool")}
        self.seq = {e: 0 for e in self.esem}
        self.waited = {e: {} for e in self.E}
        self.bufs = []
        self.dsems = []
        self.bar = self.es.enter_context(nc.semaphore("sem_bar"))
        self.bar_count = 0
        self.out_toks = []
        self.nbuf = 0

    def buf(self, name="b"):
        b = Buf(name)
        self.bufs.append(b)
        return b

    def bufl(self, n, name="b"):
        return [self.buf(name + str(i)) for i in range(n)]

    def dsem(self, name):
        d = DSem("d_" + name + str(len(self.dsems)), self.es.enter_context(self.nc.semaphore("ds_" + name + str(len(self.dsems)))))
        self.dsems.append(d)
        return d

    def sb(self, name, shape, dt):
        return self.es.enter_context(self.nc.sbuf_tensor(name, shape, dt))

    def psum(self, name, shape, dt):
        return self.es.enter_context(self.nc.psum_tensor(name, shape, dt))

    def _deps(self, eng, reads, writes, is_dma):
        deps = {}

        def need(tok, kind):
            if tok is None:
                return
            key, sem, val = tok
            if (not is_dma) and key == eng and val > self.seq[eng]:
                return
            if key not in deps or deps[key][1] < val:
                deps[key] = (sem, val)

        for b in reads:
            need(b.w, "raw")
            if b.excl:
                for t in b.r.values():
                    need(t, "war")
        for b in writes:
            need(b.w, "waw")
            for t in b.r.values():
                need(t, "war")
        wd = self.waited[eng]
        for key, (sem, val) in deps.items():
            if wd.get(key, 0) < val:
                self.E[eng].wait_ge(sem, val)
                wd[key] = val

    def _note(self, tok, reads, writes):
        for b in reads:
            if b.excl and b not in writes:
                b.w = tok
                b.r = {}
                continue
            old = b.r.get(tok[0])
            if old is None or old[2] < tok[2]:
                b.r[tok[0]] = tok
        for b in writes:
            b.w = tok
            b.r = {}

    def op(self, eng, fn, reads=(), writes=(), track=True):
        self._deps(eng, reads, writes, False)
        ins = fn(self.E[eng])
        if track:
            self.seq[eng] += 1
            ins.then_inc(self.esem[eng], 1)
            tok = (eng, self.esem[eng], self.seq[eng])
        else:
            tok = (eng, self.esem[eng], self.seq[eng] + 1)
        self._note(tok, reads, writes)
        return tok

    def dma(self, q, out, in_, ds, reads=(), writes=(), is_out=False):
        self._deps(q, reads, writes, True)
        self.E[q].dma_start(out=out, in_=in_).then_inc(ds.h, 16)
        ds.count += 16
        tok = (ds.key, ds.h, ds.count)
        self._note(tok, reads, writes)
        if is_out:
            self.out_toks.append(tok)
        return tok

    def barrier(self, keep=()):
        skip = {d.key for d, _ in keep}
        keepb = {id(b) for _, b in keep}
        sp = self.E["sp"]
        wd = self.waited["sp"]
        for e, s in self.esem.items():
            if wd.get(e, 0) < self.seq[e]:
                sp.wait_ge(s, self.seq[e])
                wd[e] = self.seq[e]
        for d in self.dsems:
            if d.key in skip:
                continue
            if d.count and wd.get(d.key, 0) < d.count:
                sp.wait_ge(d.h, d.count)
                wd[d.key] = d.count
        self.bar_count += 1
        sp.sem_inc(self.bar, 1)
        for e in ("pe", "act", "dve", "pool"):
            self.E[e].wait_ge(self.bar, self.bar_count)
            w = self.waited[e]
            for e2 in self.esem:
                w[e2] = self.seq[e2]
            for d in self.dsems:
                if d.key not in skip:
                    w[d.key] = d.count
        for b in self.bufs:
            if id(b) in keepb:
                continue
            b.w = None
            b.r = {}

    def finish(self):
        sp = self.E["sp"]
        wd = self.waited["sp"]
        for key, sem, val in self.out_toks:
            if wd.get(key, 0) < val:
                sp.wait_ge(sem, val)
                wd[key] = val


class Region:
    def __init__(self, arena_ap, nbytes):
        self.a = arena_ap
        self.nbytes = nbytes
        self.off = 0

    def reset(self):
        self.off = 0

    def alloc(self, shape, dt, parts=128):
        esz = 4 if dt == F32 else 2
        n = 1
        for s in shape[1:]:
            n *= s
        nb = (n * esz + 31) // 32 * 32
        assert self.off + nb <= self.nbytes, ("region overflow", self.off, nb, self.nbytes)
        v = self.a[0:shape[0], self.off // 2:(self.off + n * esz) // 2]
        self.off += nb
        if dt == F32:
            v = v.bitcast(F32)
        if len(shape) == 3:
            v = v.rearrange("p (a b) -> p a b", a=shape[1])
        elif len(shape) == 4:
            v = v.rearrange("p (a b c) -> p a b c", a=shape[1], b=shape[2])
        return v


def build_program(debug=False):
    k = KB()
    nc = k.nc
    dram_in = lambda n, s: nc.dram_tensor(n, s, F32, kind="ExternalInput").ap()
    dram_out = lambda n, s: nc.dram_tensor(n, s, F32, kind="ExternalOutput").ap()
    xT = dram_in("xT", [D, NT])
    pT = dram_in("pT", [256, NT])
    spool = dram_in("spool", [128, 8 * 15 * 16])
    sconv = dram_in("sconv", [128, 12 * 3 * 16])
    sssm = dram_in("sssm", [16, 16, 64, 128])
    w_in = dram_in("w_in", [D, 3600])
    w_pool = dram_in("w_pool", [4, 256, 256])
    w_out = dram_in("w_out", [2048, D])
    w_ff1 = dram_in("w_ff1", [D, 4096])
    w_ff2 = dram_in("w_ff2", [4096, D])
    w_gate = dram_in("w_gate", [D, D])
    w_ple = dram_in("w_ple", [256, D])
    vecs = dram_in("vecs", [128, NV])
    hv = dram_in("hv", [16, 3])
    dsk = dram_in("dsk", [1, 16])
    yT = dram_out("yT", [D, NT])
    npool_p = dram_out("npool_p", [128, 8, 15])
    nconv_p = dram_out("nconv_p", [128, 12, 3])
    nssm_p = dram_out("nssm_p", [128, 1024])
    npool_s = dram_out("npool_s", [128, 8, 15 * 16])
    nconv_s = dram_out("nconv_s", [128, 12, 3 * 16])
    nssm_s = dram_out("nssm_s", [16, 16, 64, 128])
    dbg = {}

    XN = k.sb("XN", [128, KC, NT], BF16)
    R2 = k.sb("R2", [128, 33792], BF16)
    R3 = k.sb("R3", [128, 42752], BF16)
    YMIX = R2[:, :].rearrange("p (c t) -> p c t", c=16)
    r2 = Region(R2[:, :], 67584)
    r3 = Region(R3[:, :], 85504)
    r2h = Region(R2[:, 0:16896], 33792)
    H = R3[:, 0:33792].bitcast(F32).rearrange("p (c t) -> p c t", c=KC)
    r3x = Region(R3[:, 33792:42752], 17920)

    IDB = k.sb("IDB", [128, 128], BF16)
    IDF = k.sb("IDF", [128, 128], F32)
    ONESB = k.sb("ONESB", [128, 128], BF16)
    ONESF = k.sb("ONESF", [128, 128], F32)
    TRIF = k.sb("TRIF", [128, 128], F32)
    TRIB = k.sb("TRIB", [128, 128], BF16)
    UB = k.sb("UB", [128, 128], BF16)
    TRIFS = k.sb("TRIFS", [64, 64], F32)
    TRIBS = k.sb("TRIBS", [64, 64], BF16)
    UBS = k.sb("UBS", [64, 64], BF16)
    BDFS = k.sb("BDFS", [64, 64], F32)
    DI = k.sb("DI", [128, 16, 128], BF16)
    BDROW = k.sb("BDROW", [128, 16, 64], BF16)
    BDCOL = k.sb("BDCOL", [64, 16], BF16)
    VEC = k.sb("VEC", [128, NV], F32)
    HV = k.sb("HV", [16, 4], F32)
    DB = k.sb("DB", [128, 16], F32)
    INVC = k.sb("INVC", [128, 4, 16], F32)
    TOK = k.sb("TOK", [128, NB, 5, 16], F32)
    AHL = k.sb("AHL", [128, NB, 2, 16], BF16)
    DTAS = k.sb("DTAS", [16, 64], F32)
    SST = k.sb("SST", [128, 1024], F32)
    HP = k.sb("HP", [128, 1024], BF16)
    SMALL = k.sb("SMALL", [128, 64], F32)
    TMPC = k.sb("TMPC", [128, 128], F32)
    CDX = k.sb("CDX", [128, 8, 16], F32)
    WDT = k.sb("WDT", [128, KC, 16], BF16)

    PS = [k.psum("ps%d" % i, [128, 512], F32) for i in range(8)]
    PB = k.bufl(8, "pb")
    for b_ in PB:
        b_.excl = True
    bank_rr = [0]

    def next_bank():
        i = bank_rr[0]
        bank_rr[0] = (i + 1) % 8
        return i

    cb = k.buf("const")
    vb = k.buf("vec")
    XNb = k.bufl(5, "xn")
    WS_DS = [k.dsem("ws") for _ in range(3)]
    ld_ds = [k.dsem("ld") for _ in range(12)]
    out_ds = [k.dsem("out") for _ in range(8)]
    od_i = [0]

    def next_out_ds():
        d = out_ds[od_i[0] % len(out_ds)]
        od_i[0] += 1
        return d

    def dump(name, ap, bufs, dt=F32):
        if not debug:
            return
        t = nc.dram_tensor("dbg_" + name, list(ap.shape), dt, kind="ExternalOutput").ap()
        k.dma("sp", t, ap, k.dsem("dbg"), reads=bufs, is_out=True)

    def pool_op(fn, reads=(), writes=(cb,)):
        return k.op("pool", fn, reads=reads, writes=writes)

    def sel(t_ap, pattern, cmp_op, base, cm):
        pool_op(lambda e: e.affine_select(out=t_ap, in_=t_ap, pattern=pattern, compare_op=cmp_op, fill=0.0,
                                          base=base, channel_multiplier=cm), reads=(cb,))

    for t in (IDF, ONESF, TRIF):
        pool_op(lambda e, t=t: e.memset(t[:], 1.0))
    sel(IDF[:], [[-1, 128]], ALU.is_equal, 0, 1)
    sel(TRIF[:], [[1, 128]], ALU.is_ge, 0, -1)
    UF = TMPC
    pool_op(lambda e: e.memset(UF[:], 1.0))
    sel(UF[:], [[-1, 128]], ALU.is_gt, 0, 1)
    pool_op(lambda e: e.memset(BDFS[:], 1.0))
    bdv = BDFS[:].rearrange("p (b l) -> p b l", l=4)
    sel(bdv, [[-4, 16], [0, 4]], ALU.is_ge, 0, 1)
    sel(bdv, [[4, 16], [0, 4]], ALU.is_ge, 3, -1)
    k.op("dve", lambda e: e.tensor_copy(IDB[:], IDF[:]), reads=(cb,), writes=(cb,))
    k.op("dve", lambda e: e.tensor_copy(ONESB[:], ONESF[:]), reads=(cb,), writes=(cb,))
    k.op("dve", lambda e: e.tensor_copy(TRIB[:], TRIF[:]), reads=(cb,), writes=(cb,))
    k.op("dve", lambda e: e.tensor_copy(UB[:], UF[:]), reads=(cb,), writes=(cb,))
    k.op("dve", lambda e: e.tensor_tensor(TRIFS[:], TRIF[0:64, 0:64], BDFS[:], ALU.mult), reads=(cb,), writes=(cb,))
    k.op("dve", lambda e: e.tensor_copy(TRIBS[:], TRIFS[:]), reads=(cb,), writes=(cb,))
    k.op("dve", lambda e: e.tensor_tensor(UBS[:], UF[0:64, 0:64], BDFS[:], ALU.mult), reads=(cb,), writes=(cb,))
    pool_op(lambda e: e.memset(BDROW[:], 1.0))
    sel(BDROW[:], [[-4, 16], [1, 64]], ALU.is_ge, 0, 0)
    sel(BDROW[:], [[4, 16], [-1, 64]], ALU.is_ge, 3, 0)
    pool_op(lambda e: e.memset(BDCOL[:], 1.0))
    sel(BDCOL[:], [[-4, 16]], ALU.is_ge, 0, 1)
    sel(BDCOL[:], [[4, 16]], ALU.is_ge, 3, -1)
    k.dma("sp", VEC[:], vecs, ld_ds[0], writes=(vb,))
    k.dma("sp", HV[:, 0:3], hv, ld_ds[1], writes=(vb,))
    k.dma("sp", DB[:], dsk.partition_broadcast(128), ld_ds[2], writes=(vb,))
    k.op("dve", lambda e: e.tensor_tensor(DI[:], IDF[:].unsqueeze(1).broadcast_to([128, 16, 128]),
                                          DB[:].unsqueeze(2).broadcast_to([128, 16, 128]), ALU.mult),
         reads=(cb, vb), writes=(cb,))
    for g, w in enumerate(POOLW):
        pool_op(lambda e, g=g: e.iota(INVC[:, g, :], pattern=[[1, 16]], base=1, channel_multiplier=0, allow_small_or_imprecise_dtypes=True))
    for g, w in enumerate(POOLW):
        k.op("dve", lambda e, g=g, w=w: e.tensor_scalar(INVC[:, g, :], INVC[:, g, :], float(w), None, ALU.min),
             reads=(cb,), writes=(cb,))
    k.op("dve", lambda e: e.reciprocal(INVC[:], INVC[:]), reads=(cb,), writes=(cb,))
    k.op("act", lambda e: e.activation(HV[:, 3:4], HV[:, 1:2], AF.Exp), reads=(vb,), writes=(vb,))
    k.op("dve", lambda e: e.tensor_scalar(HV[:, 3:4], HV[:, 3:4], -1.0, None, ALU.mult), reads=(vb,), writes=(vb,))

    def wview_kc(w_dram, c0, ncols):
        return w_dram.rearrange("(kc p) n -> p kc n", p=128)[:, :, c0:c0 + ncols]

    def rmsnorm_tile(ti, src, src_bufs, gcol, dst_fn, dst_bufs, SQ, RS, sqb, rsb, post=None):
        t0, n = TT[ti]
        s = ti % 2
        k.op("act", lambda e: e.activation(SQ[s][:, :, 0:n], src, AF.Square), reads=src_bufs, writes=(sqb[s],))
        bi = next_bank()
        for kc in range(KC):
            k.op("pe", lambda e, kc=kc: e.matmul(PS[bi][:, 0:n], ONESB[:], SQ[s][:, kc, 0:n], start=(kc == 0), stop=(kc == KC - 1)),
                 reads=(sqb[s], cb), writes=(PB[bi],), track=(kc == KC - 1))
        k.op("dve", lambda e: e.tensor_scalar(RS[s][:, 0:n], PS[bi][:, 0:n], 1.0 / D, EPS, ALU.mult, ALU.add),
             reads=(PB[bi],), writes=(rsb[s],))
        k.op("act", lambda e: e.activation(RS[s][:, 0:n], RS[s][:, 0:n], AF.Ln), reads=(rsb[s],), writes=(rsb[s],))
        k.op("act", lambda e: e.activation(RS[s][:, 0:n], RS[s][:, 0:n], AF.Exp, scale=-0.5), reads=(rsb[s],), writes=(rsb[s],))
        for kc in range(KC):
            db = dst_bufs(kc) if callable(dst_bufs) else dst_bufs
            k.op("dve", lambda e, kc=kc: e.scalar_tensor_tensor(dst_fn(kc), src[:, kc, :], VEC[:, gcol + kc:gcol + kc + 1],
                                                                RS[s][:, 0:n], ALU.mult, ALU.mult),
                 reads=tuple(src_bufs) + (rsb[s], vb), writes=db)
            if post is not None:
                post(kc)

    def load_w(slot_ap, src_ap, ds, slot_buf):
        return k.dma("pool", slot_ap, src_ap, ds, writes=(slot_buf,))

    r2.reset()
    XS = [r2.alloc([128, KC, 512], F32) for _ in range(2)]
    SQ = [r2.alloc([128, KC, 512], BF16) for _ in range(2)]
    RS = [r2.alloc([128, 512], F32)] * 2
    xsb, sqb, rsb = k.bufl(2, "xs"), k.bufl(2, "sq"), [k.buf("rs")] * 2
    assert r2.off <= 51200
    r2t = Region(R2[:, 25600:33792], 16384)
    WSZ = [r2t.alloc([128, KC, 512], BF16) for _ in range(2)]
    wszb = k.bufl(2, "wsz")
    load_w(WSZ[0][:], wview_kc(w_in, 1024, 512), WS_DS[0], wszb[0])
    load_w(WSZ[1][:], wview_kc(w_in, 1536, 512), WS_DS[1], wszb[1])
    wdtb = k.buf("wdt")
    load_w(WDT[:], wview_kc(w_in, 3584, 16), ld_ds[10], wdtb)
    r3.reset()
    SZT = r3.alloc([128, NB, 1024], BF16)
    XBC = r3.alloc([128, 12, NT], BF16)
    sztb = k.bufl(NB, "szt")
    xbcb = k.bufl(12, "xbc")
    wxv = lambda c0: XBC[:, c0:c0 + 2, :].rearrange("p a b -> p (a b)")[:, 0:4096].rearrange("p (a b) -> p a b", a=KC)
    WX = {2: wxv(0), 1: wxv(2)}
    wxb = {2: k.buf("wx2"), 1: k.buf("wx1")}
    wx_ds = {2: k.dsem("wx2"), 1: k.dsem("wx1")}
    WS = [WSZ[0], WSZ[1], None]
    wsb = [wszb[0], wszb[1], None]
    dtv_ = lambda c0: XBC[0:16, c0:c0 + 2, :].rearrange("p a b -> p (a b)")[:, 0:4096].bitcast(F32).rearrange("p (j t) -> p j t", j=4)
    DTT = [dtv_(4), dtv_(6)]
    dttb = k.bufl(2, "dtt")
    tokb = k.bufl(NB, "tok")
    dtasb = k.buf("dtas")
    def z_block(blk):
        m = 128 if blk < 16 else 64
        c0 = blk * 128
        for half in range(2):
            bi = next_bank()
            for kc in range(KC):
                k.op("pe", lambda e, kc=kc: e.matmul(PS[bi][0:m, :], XN[:, kc, c0:c0 + m], WS[half][:, kc, :],
                                                     start=(kc == 0), stop=(kc == KC - 1)),
                     reads=(XNb[min(blk // 4, 4)], wsb[half]), writes=(PB[bi],), track=(kc == KC - 1))
            k.op("act", lambda e: e.activation(SZT[0:m, blk, half * 512:(half + 1) * 512], PS[bi][0:m, :], AF.Silu),
                 reads=(PB[bi],), writes=(sztb[blk],))
    def dt_a(ti):
        t0, n = TT[ti]
        q = ti % 2
        bi = next_bank()
        for kc in range(KC):
            k.op("pe", lambda e, kc=kc: e.matmul(PS[bi][0:16, 0:n], WDT[:, kc, :], XN[:, kc, t0:t0 + n],
                                                 start=(kc == 0), stop=(kc == KC - 1)),
                 reads=(XNb[ti], wdtb), writes=(PB[bi],), track=(kc == KC - 1))
        raw, tmp, dtv, dta = (DTT[q][:, j, 0:n] for j in range(4))
        k.op("dve", lambda e: e.tensor_scalar(raw, PS[bi][0:16, 0:n], HV[:, 0:1], None, ALU.add), reads=(PB[bi], vb), writes=(dttb[q],))
        k.op("act", lambda e: e.activation(tmp, raw, AF.Abs), reads=(dttb[q],), writes=(dttb[q],))
        k.op("act", lambda e: e.activation(tmp, tmp, AF.Exp, scale=-1.0), reads=(dttb[q],), writes=(dttb[q],))
        k.op("act", lambda e: e.activation(tmp, tmp, AF.Ln, bias=1.0), reads=(dttb[q],), writes=(dttb[q],))
        k.op("dve", lambda e: e.scalar_tensor_tensor(dtv, raw, 0.0, tmp, ALU.max, ALU.add), reads=(dttb[q],), writes=(dttb[q],))
        k.op("dve", lambda e: e.tensor_scalar(dta, dtv, HV[:, 3:4], None, ALU.mult), reads=(dttb[q], vb), writes=(dttb[q],))
        if ti == 4:
            k.op("dve", lambda e: e.tensor_copy(DTAS[:], dta), reads=(dttb[q],), writes=(dtasb,))

    def dt_b(ti):
        t0, n = TT[ti]
        q = ti % 2
        raw, tmp, dtv, dta = (DTT[q][:, j, 0:n] for j in range(4))
        nblk = (n + 127) // 128
        bj = next_bank()
        for b4 in range(nblk):
            m = min(128, n - b4 * 128)
            for j, srcv in enumerate((dtv, dta)):
                k.op("pe", lambda e, j=j, srcv=srcv: e.transpose(PS[bj][0:m, (b4 * 2 + j) * 16:(b4 * 2 + j + 1) * 16],
                                                                 srcv[:, b4 * 128:b4 * 128 + m], IDF[0:16, 0:16]),
                     reads=(dttb[q], cb), writes=(PB[bj],), track=(b4 == nblk - 1 and j == 1))
        for b4 in range(nblk):
            m = min(128, n - b4 * 128)
            blk = ti * 4 + b4
            k.op("act", lambda e: e.copy(TOK[0:m, blk, 0:2, :], PS[bj][0:m, b4 * 32:(b4 + 1) * 32].rearrange("p (j h) -> p j h", j=2)),
                 reads=(PB[bj],), writes=(tokb[blk],))

    xTv = xT.rearrange("(c p) t -> p c t", p=128)

    def norm0(ti):
        t0, n = TT[ti]
        s_ = ti % 2
        k.dma("sp", XS[s_][:, :, 0:n], xTv[:, :, t0:t0 + n], ld_ds[3 + s_], writes=(xsb[s_],))
        rmsnorm_tile(ti, XS[s_][:, :, 0:n], (xsb[s_],), V_GMIX, lambda kc: XN[:, kc, t0:t0 + n], (XNb[ti],),
                     SQ, RS, sqb, rsb)

    def zdt(ti):
        dt_a(ti)
        for blk in ([16] if ti == 4 else range(4 * ti, 4 * ti + 4)):
            z_block(blk)
        dt_b(ti)

    norm0(0)
    norm0(1)
    for grp in (2, 1):
        k.dma("pool", WX[grp][:], wview_kc(w_in, 2048 + 512 * grp, 512), wx_ds[grp], reads=(XNb[1],), writes=(wxb[grp],))
    zdt(0)
    norm0(2)
    zdt(1)
    norm0(3)
    zdt(2)
    norm0(4)
    zdt(3)
    zdt(4)
    dump("xn", XN[:], XNb, BF16)
    k.barrier()

    r2.reset()
    WS[2] = r2.alloc([128, KC, 512], BF16)
    wsb[2] = k.buf("ws2")
    XB = [r2.alloc([128, NP + 3], F32) for _ in range(2)]
    XBS = [r2.alloc([128, 7, 16], F32) for _ in range(2)]
    TC = [r2.alloc([128, NP], F32) for _ in range(2)]
    TCS = [r2.alloc([128, 4, 16], F32) for _ in range(2)]
    SCV = r2.alloc([128, 12, 48], F32)
    xbb, tcb = k.bufl(2, "xb"), k.bufl(2, "tc")
    scvb = k.buf("scv")
    k.dma("sp", SCV[:], sconv.rearrange("p (c f) -> p c f", c=12), ld_ds[5], writes=(scvb,))
    load_w(WS[2][:], wview_kc(w_in, 2048, 512), WS_DS[2], wsb[2])
    assert r2.off <= 51200
    pending_silu = []

    def flush_silu():
        while pending_silu:
            ch_, q_ = pending_silu.pop(0)
            park = (wxb[2],) if ch_ in (0, 1) else ((wxb[1],) if ch_ in (2, 3) else ())
            k.op("act", lambda e: e.activation(XBC[:, ch_, 0:NP], TC[q_][:], AF.Silu),
                 reads=(tcb[q_],), writes=(xbcb[ch_],) + park)
            k.op("act", lambda e: e.activation(XBC[:, ch_, NP:NT].rearrange("p (b l) -> p l b", l=4), TCS[q_][:], AF.Silu),
                 reads=(tcb[q_],), writes=(xbcb[ch_],) + park)

    for grp in (2, 1, 0):
        WG_, wgb_ = (WS[2], wsb[2]) if grp == 0 else (WX[grp], wxb[grp])
        for c4 in range(4):
            ch = grp * 4 + c4
            q = ch % 2
            k.op("pool", lambda e: e.memset(XB[q][:, 0:3], 0.0), writes=(xbb[q],))
            k.op("pool", lambda e: e.tensor_copy(XBS[q][:, 0:3, :], SCV[:, ch, :].rearrange("p (j b) -> p j b", j=3)),
                 reads=(scvb,), writes=(xbb[q],))
            for ti, (t0, n) in enumerate(TT):
                bi = next_bank()
                for kc in range(KC):
                    k.op("pe", lambda e, kc=kc: e.matmul(PS[bi][:, 0:n], WG_[:, kc, c4 * 128:(c4 + 1) * 128], XN[:, kc, t0:t0 + n],
                                                         start=(kc == 0), stop=(kc == KC - 1)),
                         reads=(XNb[ti], wgb_), writes=(PB[bi],), track=(kc == KC - 1))
                if ti < 4:
                    k.op("act", lambda e: e.copy(XB[q][:, 3 + t0:3 + t0 + n], PS[bi][:, 0:n]), reads=(PB[bi],), writes=(xbb[q],))
                else:
                    k.op("act", lambda e: e.copy(XBS[q][:, 3:7, :], PS[bi][:, 0:64].rearrange("p (b l) -> p l b", l=4)),
                         reads=(PB[bi],), writes=(xbb[q],))
            k.dma("sp", nconv_p[:, ch, :], XB[q][:, NP:NP + 3], next_out_ds(), reads=(xbb[q],), is_out=True)
            k.dma("sp", nconv_s[:, ch, :], XBS[q][:, 4:7, :].rearrange("p j b -> p (j b)"), next_out_ds(), reads=(xbb[q],), is_out=True)
            cw = lambda j: VEC[:, V_CW + j * 12 + ch:V_CW + j * 12 + ch + 1]
            bcol = VEC[:, V_CB + ch:V_CB + ch + 1]
            k.op("act", lambda e: e.activation(TC[q][:], XB[q][:, 0:NP], AF.Identity, bias=bcol, scale=cw(0)),
                 reads=(xbb[q], vb), writes=(tcb[q],))
            k.op("act", lambda e: e.activation(TCS[q][:], XBS[q][:, 0:4, :], AF.Identity, bias=bcol, scale=cw(0)),
                 reads=(xbb[q], vb), writes=(tcb[q],))
            flush_silu()
            for j in range(1, 4):
                k.op("dve", lambda e, j=j: e.scalar_tensor_tensor(TC[q][:], XB[q][:, j:j + NP], cw(j), TC[q][:], ALU.mult, ALU.add),
                     reads=(xbb[q], vb, tcb[q]), writes=(tcb[q],))
                k.op("dve", lambda e, j=j: e.scalar_tensor_tensor(TCS[q][:], XBS[q][:, j:j + 4, :], cw(j), TCS[q][:], ALU.mult, ALU.add),
                     reads=(xbb[q], vb, tcb[q]), writes=(tcb[q],))
            pending_silu.append((ch, q))
    flush_silu()
    dump("szt", SZT[:], sztb, BF16)
    dump("xbc", XBC[:], xbcb, BF16)
    dump("tok01", TOK[:], tokb)
    k.barrier()

    r2h.reset()
    TMPA = r2h.alloc([128, NB, 16], F32)
    TMPB = r2h.alloc([128, NB, 16], F32)
    EXPM = r2h.alloc([16, 8, 128], F32)
    CDF = r2h.alloc([16, 16], F32)
    tmpb_ = k.bufl(2, "tmpab")
    expb, cdfb, cdxb = k.buf("expm"), k.buf("cdf"), k.buf("cdx")
    ACUM_ps = PS[0][:, 0:NB * 16].rearrange("p (b h) -> p b h", h=16)
    ATOT_ps = PS[1][:, 0:NB * 16].rearrange("p (b h) -> p b h", h=16)
    for blk in range(NB):
        m = 128 if blk < 16 else 64
        tri = TRIF[:] if blk < 16 else TRIFS[:]
        one = ONESF[:] if blk < 16 else BDFS[:]
        k.op("pe", lambda e: e.matmul(ACUM_ps[0:m, blk, :], tri, TOK[0:m, blk, 1, :], start=True, stop=True),
             reads=(tokb[blk], cb), writes=(PB[0],), track=False)
        k.op("pe", lambda e: e.matmul(ATOT_ps[0:m, blk, :], one, TOK[0:m, blk, 1, :], start=True, stop=True),
             reads=(tokb[blk], cb), writes=(PB[1],), track=(blk == NB - 1))
    for (p0, p1, b0, b1) in ((0, 128, 0, 16), (0, 64, 16, 17)):
        tb = tokb[b0:b1]
        k.op("act", lambda e: e.activation(TOK[p0:p1, b0:b1, 2, :], ACUM_ps[p0:p1, b0:b1, :], AF.Exp), reads=(PB[0],), writes=tb)
        k.op("act", lambda e: e.activation(TOK[p0:p1, b0:b1, 3, :], ATOT_ps[p0:p1, b0:b1, :], AF.Exp), reads=(PB[1],), writes=tb)
        k.op("act", lambda e: e.copy(TMPA[p0:p1, b0:b1, :], ACUM_ps[p0:p1, b0:b1, :]), reads=(PB[0],), writes=(tmpb_[0],))
        k.op("dve", lambda e: e.tensor_tensor(TMPB[p0:p1, b0:b1, :], ATOT_ps[p0:p1, b0:b1, :], TMPA[p0:p1, b0:b1, :], ALU.subtract),
             reads=(PB[1], tmpb_[0]), writes=(tmpb_[1],))
        k.op("act", lambda e: e.activation(TMPB[p0:p1, b0:b1, :], TMPB[p0:p1, b0:b1, :], AF.Exp), reads=(tmpb_[1],), writes=(tmpb_[1],))
        k.op("dve", lambda e: e.tensor_tensor(TOK[p0:p1, b0:b1, 4, :], TMPB[p0:p1, b0:b1, :], TOK[p0:p1, b0:b1, 0, :], ALU.mult),
             reads=[tmpb_[1]] + tb, writes=tb)
        k.op("dve", lambda e: e.tensor_copy(AHL[p0:p1, b0:b1, 0, :], TOK[p0:p1, b0:b1, 1, :]), reads=tb, writes=tb)
        k.op("dve", lambda e: e.tensor_copy(TMPA[p0:p1, b0:b1, :], AHL[p0:p1, b0:b1, 0, :]), reads=tb + [tmpb_[0]], writes=(tmpb_[0],))
        k.op("dve", lambda e: e.tensor_tensor(AHL[p0:p1, b0:b1, 1, :], TOK[p0:p1, b0:b1, 1, :], TMPA[p0:p1, b0:b1, :], ALU.subtract),
             reads=tb + [tmpb_[0]], writes=tb)
    k.op("dve", lambda e: e.tensor_reduce(out=CDF[:], in_=DTAS[:].rearrange("p (b l) -> p b l", l=4), axis=AX.X, op=ALU.add),
         reads=(dtasb,), writes=(cdfb,))
    k.op("act", lambda e: e.activation(CDF[:], CDF[:], AF.Exp), reads=(cdfb,), writes=(cdfb,))
    k.op("pool", lambda e: e.memset(EXPM[:], 1.0), writes=(expb,))
    expv = EXPM[:].rearrange("p j (a d) -> p j a d", a=2)
    k.op("pool", lambda e: e.affine_select(out=expv, in_=expv, pattern=[[-2, 8], [-1, 2], [0, 64]], compare_op=ALU.is_equal,
                                           fill=0.0, base=0, channel_multiplier=1), reads=(expb,), writes=(expb,))
    for j in range(8):
        k.op("pe", lambda e, j=j: e.matmul(PS[2][:, j * 16:(j + 1) * 16], EXPM[:, j, :], CDF[:], start=True, stop=True),
             reads=(expb, cdfb), writes=(PB[2],), track=(j == 7))
    k.op("act", lambda e: e.copy(CDX[:], PS[2][:, 0:128].rearrange("p (j b) -> p j b", j=8)), reads=(PB[2],), writes=(cdxb,))
    dump("tok", TOK[:], tokb)
    k.barrier()

    xtokb, xwb, btokb, cbmb, mmb, yab_g, ynb_g, sdb, sstb, hpb, smb, xdtb = (k.buf(n) for n in
        ("xtok", "xw", "btok", "cbm", "mm", "ya", "yn", "sd", "sst", "hp", "small", "xdt"))
    rhb, decb = k.bufl(4, "rh"), k.bufl(4, "dec")
    ysb = k.bufl(NB, "ys")
    XT_ps = PS[0][:].bitcast(BF16)
    BT_ps = PS[1][:, 0:128].bitcast(BF16)
    CB_ps = PS[1][:, 256:512].rearrange("p (g l) -> p g l", g=2)
    bc16 = lambda ap, m: ap.unsqueeze(2).broadcast_to([m, 16, 64])
    v3 = lambda ap: ap.rearrange("p (h d) -> p h d", h=16)

    def ssd_alloc(m):
        r2h.reset()
        T = {}
        T["XTOK"] = r2h.alloc([128, 1024], BF16)
        T["XW"] = r2h.alloc([128, 1024], BF16)
        T["XDT"] = r2h.alloc([128, 1024], BF16)
        T["BTOK"] = r2h.alloc([128, 256], BF16)
        T["CBM"] = r2h.alloc([128, 2, m], F32)
        T["RH"] = [r2h.alloc([128, 4, m], BF16) for _ in range(4)]
        T["DEC"] = [r2h.alloc([128, 4, m], F32) for _ in range(4)]
        T["MM"] = r2h.alloc([128, 16, m], BF16)
        return T

    def ssd_A1(blk, T):
        m = 128 if blk < 16 else 64
        t0 = blk * 128
        trib = TRIB[:] if blk < 16 else TRIBS[:]
        maskf = TRIF[:] if blk < 16 else TRIFS[:]
        XTOK, XW, BTOK, CBM, RH = (T[n] for n in ("XTOK", "XW", "BTOK", "CBM", "RH"))
        for j in range(8):
            k.op("pe", lambda e, j=j: e.transpose(XT_ps[0:m, j * 128:(j + 1) * 128], XBC[:, j, t0:t0 + m], IDB[:]),
                 reads=(xbcb[j], cb), writes=(PB[0],), track=(j == 7))
        for g in range(2):
            k.op("pe", lambda e, g=g: e.transpose(BT_ps[0:m, g * 128:(g + 1) * 128], XBC[:, 8 + g, t0:t0 + m], IDB[:]),
                 reads=(xbcb[8 + g], cb), writes=(PB[1],), track=False)
        for g in range(2):
            k.op("pe", lambda e, g=g: e.matmul(CB_ps[0:m, g, 0:m], XBC[:, 8 + g, t0:t0 + m], XBC[:, 10 + g, t0:t0 + m], start=True, stop=True),
                 reads=(xbcb[8 + g], xbcb[10 + g]), writes=(PB[1],), track=(g == 1))
        for q in range(4):
            k.op("pool", lambda e, q=q: e.tensor_tensor(RH[q][0:m, :, :], trib[0:m, 0:m].unsqueeze(1).broadcast_to([m, 4, m]),
                                                        AHL[0:m, blk, 0, 4 * q:4 * q + 4].unsqueeze(2).broadcast_to([m, 4, m]), ALU.mult),
                 reads=(cb, tokb[blk]), writes=(rhb[q],))
        k.op("act", lambda e: e.copy(XTOK[0:m, :], XT_ps[0:m, :]), reads=(PB[0],), writes=(xtokb,))
        k.op("act", lambda e: e.copy(BTOK[0:m, :], BT_ps[0:m, :]), reads=(PB[1],), writes=(btokb,))
        k.op("dve", lambda e: e.tensor_tensor(CBM[0:m, :, :], CB_ps[0:m, :, 0:m], maskf[0:m, 0:m].unsqueeze(1).broadcast_to([m, 2, m]), ALU.mult),
             reads=(PB[1], cb), writes=(cbmb,))
        k.op("dve", lambda e: e.tensor_tensor(v3(T["XDT"][0:m, :]), v3(XT_ps[0:m, :]), bc16(TOK[0:m, blk, 0, :], m), ALU.mult),
             reads=(PB[0], tokb[blk]), writes=(xdtb,))
        k.op("pool", lambda e: e.tensor_tensor(v3(XW[0:m, :]), v3(XTOK[0:m, :]), bc16(TOK[0:m, blk, 4, :], m), ALU.mult),
             reads=(xtokb, tokb[blk]), writes=(xwb,))

    def ssd_A2(blk, T, part):
        m = 128 if blk < 16 else 64
        ub = UB[:] if blk < 16 else UBS[:]
        CBM, RH, DEC, MM = (T[n] for n in ("CBM", "RH", "DEC", "MM"))
        for q in range(4):
            bq = 2 + q % 2
            segv = PS[bq][:, 0:4 * m].rearrange("p (h l) -> p h l", h=4)
            if part == 0:
                k.op("pe", lambda e: e.matmul(PS[bq][0:m, 0:4 * m], ub[0:m, 0:m], RH[q][0:m, :, :].rearrange("p h l -> p (h l)"), start=True, stop=True),
                     reads=(cb, rhb[q]), writes=(PB[bq],))
                k.op("act", lambda e: e.activation(DEC[q][0:m, :, :], segv[0:m, :, :], AF.Exp), reads=(PB[bq],), writes=(decb[q],))
            else:
                g = q // 2
                k.op("dve", lambda e: e.tensor_tensor(MM[0:m, 4 * q:4 * q + 4, :], DEC[q][0:m, :, :],
                                                      CBM[0:m, g, :].unsqueeze(1).broadcast_to([m, 4, m]), ALU.mult),
                     reads=(decb[q], cbmb), writes=(mmb,))

    def ssd_Y(blk, T):
        m = 128 if blk < 16 else 64
        XTOK, MM, XDT = T["XTOK"], T["MM"], T["XDT"]
        for h in range(16):
            by = 4 + h // 8
            hc = (h % 8) * 64
            k.op("pe", lambda e: e.matmul(PS[by][0:m, hc:hc + 64], MM[0:m, h, :], XDT[0:m, h * 64:(h + 1) * 64],
                                          start=(h % 8 == 0), stop=False, skip_group_check=True),
                 reads=(mmb, xdtb), writes=(PB[by],), track=False)
            k.op("pe", lambda e: e.matmul(PS[by][0:m, hc:hc + 64], DI[0:m, h, 0:m], XTOK[0:m, h * 64:(h + 1) * 64],
                                          start=False, stop=True, skip_group_check=True),
                 reads=(cb, xtokb), writes=(PB[by],), track=(h % 8 == 7))

    def ssd_P(blk, have_off, YA, YN, part, yab=None, ynb=None):
        yab = yab if yab is not None else yab_g
        ynb = ynb if ynb is not None else ynb_g
        m = 128 if blk < 16 else 64
        t0 = blk * 128
        YT_ps = PS[0][:].bitcast(BF16).rearrange("p (j t) -> p j t", j=8)
        if part == 1:
            k.op("act", lambda e: e.copy(YMIX[:, 8:16, t0:t0 + m], YT_ps[:, :, 0:m]), reads=(PB[0],), writes=(ysb[blk],))
            return
        for g in range(2) if part in (0, "a") else ():
            cs = slice(g * 512, (g + 1) * 512)
            if have_off:
                k.op("dve", lambda e: e.tensor_tensor(YA[0:m, cs].rearrange("p (h d) -> p h d", h=8), PS[6 + g][0:m, :].rearrange("p (h d) -> p h d", h=8),
                                                      TOK[0:m, blk, 2, 8 * g:8 * g + 8].unsqueeze(2).broadcast_to([m, 8, 64]), ALU.mult),
                     reads=(PB[6 + g], tokb[blk]), writes=(yab,))
                k.op("dve", lambda e: e.tensor_tensor(YA[0:m, cs], YA[0:m, cs], PS[4 + g][0:m, :], ALU.add),
                     reads=(PB[4 + g], yab), writes=(yab,))
                k.op("dve", lambda e: e.tensor_tensor(YA[0:m, cs], YA[0:m, cs], SZT[0:m, blk, cs], ALU.mult),
                     reads=(yab, sztb[blk]), writes=(yab,))
            else:
                k.op("dve", lambda e: e.tensor_tensor(YA[0:m, cs], PS[4 + g][0:m, :], SZT[0:m, blk, cs], ALU.mult),
                     reads=(PB[4 + g], sztb[blk]), writes=(yab,))
        if part == "a":
            return
        k.op("act", lambda e: e.activation(YN[0:m, :], YA[0:m, :], AF.Square, accum_out=SMALL[0:m, 0:1]), reads=(yab,), writes=(ynb, smb))
        k.op("dve", lambda e: e.tensor_scalar(SMALL[0:m, 1:2], SMALL[0:m, 0:1], 1.0 / 1024, EPS, ALU.mult, ALU.add), reads=(smb,), writes=(smb,))
        k.op("act", lambda e: e.activation(SMALL[0:m, 2:3], SMALL[0:m, 1:2], AF.Ln), reads=(smb,), writes=(smb,))
        k.op("act", lambda e: e.activation(SMALL[0:m, 3:4], SMALL[0:m, 2:3], AF.Exp, scale=-0.5), reads=(smb,), writes=(smb,))
        k.op("act", lambda e: e.activation(YN[0:m, :], YA[0:m, :], AF.Copy, scale=SMALL[0:m, 3:4]), reads=(yab, smb), writes=(ynb,))
        for j in range(8):
            k.op("pe", lambda e, j=j: e.transpose(YT_ps[:, j, 0:m], YN[0:m, j * 128:(j + 1) * 128], IDB[0:m, 0:m]),
                 reads=(ynb, cb), writes=(PB[0],), track=(j == 7))

    T = ssd_alloc(128)
    YA = r2h.alloc([128, 1024], F32)
    YN = r2h.alloc([128, 1024], BF16)

    def ssd_S(blk, part):
        if part == 0:
            for g in range(2):
                k.op("pe", lambda e, g=g: e.matmul(PS[2 + g][:, :], T["BTOK"][:, g * 128:(g + 1) * 128], T["XW"][:, g * 512:(g + 1) * 512], start=True, stop=True),
                     reads=(btokb, xwb), writes=(PB[2 + g],))
            return
        if blk == 0:
            for g in range(2):
                k.op("act", lambda e, g=g: e.copy(SST[:, g * 512:(g + 1) * 512], PS[2 + g][:, :]), reads=(PB[2 + g],), writes=(sstb,))
        else:
            k.op("dve", lambda e: e.tensor_tensor(v3(SST[:, :]), v3(SST[:, :]), bc16(TOK[:, blk, 3, :], 128), ALU.mult),
                 reads=(sstb, tokb[blk]), writes=(sstb,))
            for g in range(2):
                k.op("dve", lambda e, g=g: e.tensor_tensor(SST[:, g * 512:(g + 1) * 512], SST[:, g * 512:(g + 1) * 512], PS[2 + g][:, :], ALU.add),
                     reads=(sstb, PB[2 + g]), writes=(sstb,))
        if blk < 15:
            k.op("act", lambda e: e.copy(HP[:, :], SST[:, :]), reads=(sstb,), writes=(hpb,))

    ssd_A1(0, T)
    ssd_A2(0, T, 0)
    ssd_A2(0, T, 1)
    for blk in range(16):
        t0 = blk * 128
        if blk > 0:
            for g in range(2):
                k.op("pe", lambda e, g=g: e.matmul(PS[6 + g][:, :], XBC[:, 10 + g, t0:t0 + 128], HP[:, g * 512:(g + 1) * 512], start=True, stop=True),
                     reads=(xbcb[10 + g], hpb), writes=(PB[6 + g],))
        ssd_S(blk, 0)
        ssd_Y(blk, T)
        if blk > 0:
            ssd_P(blk - 1, blk > 1, YA, YN, "b")
            ssd_P(blk - 1, blk > 1, YA, YN, 1)
        ssd_S(blk, 1)
        ssd_P(blk, blk > 0, YA, YN, "a")
        if blk < 15:
            ssd_A1(blk + 1, T)
            ssd_A2(blk + 1, T, 0)
            ssd_A2(blk + 1, T, 1)
    ssd_P(15, True, YA, YN, "b")
    ssd_P(15, True, YA, YN, 1)
    k.dma("sp", nssm_p, SST[:, :], next_out_ds(), reads=(sstb,), is_out=True)
    k.barrier()

    blk = 16
    t0 = NP
    T = ssd_alloc(64)
    CZb = [r2h.alloc([128, 2, 64], BF16) for _ in range(2)]
    BZb = [r2h.alloc([64, 256], BF16) for _ in range(2)]
    xbf = lambda j: XBC[:, j, 0:2048].bitcast(F32).rearrange("p (j n) -> p j n", j=8)
    H0 = [xbf(j) for j in range(6)] + [SST[:, :].rearrange("p (j n) -> p j n", j=8)]
    H0B = [r2h.alloc([128, 8, 128], BF16), HP[:, :].rearrange("p (j n) -> p j n", j=8)]
    H0T = [r2h.alloc([128, 1024], BF16) for _ in range(2)]
    szf = lambda i: SZT[:, 2 * i:2 * i + 2, :].rearrange("p a b -> p (a b)").bitcast(F32).rearrange("p (j n) -> p j n", j=8)
    OST = [szf(5), szf(6), szf(7)]
    NH = len(H0)
    NO = len(OST)
    czb, bzb, h0b, h0bb, h0tb = k.bufl(2, "cz"), k.bufl(2, "bz"), k.bufl(6, "h0") + [sstb], [k.buf("h0b"), hpb], k.bufl(2, "h0t")
    ostb = k.bufl(NO, "ost")
    h0_ds = [k.dsem("h0") for _ in range(NH)]
    ost_ds = [k.dsem("ost") for _ in range(NO)]
    WSU = [R3[:, 4096 * i:4096 * (i + 1)].rearrange("p (a b) -> p a b", a=KC) for i in range(2)]
    WPOOL = R3[:, 8192:10240].rearrange("p (g c e) -> p g c e", g=4, c=2)
    wsub, wpb = k.bufl(2, "wsu"), k.buf("wpool")
    load_w(WSU[0][:], wview_kc(w_in, 0, 512), WS_DS[0], wsub[0])
    load_w(WSU[1][:], wview_kc(w_in, 512, 512), WS_DS[1], wsub[1])
    k.dma("pool", WPOOL[:].rearrange("p g c e -> p (g c) e"), w_pool.rearrange("g (c p) e -> p (g c) e", p=128), ld_ds[7], writes=(wpb,))
    h0src = lambda b: sssm[b].rearrange("(j a) p n -> (a p) j n", a=2)
    for b in range(NH):
        k.dma("sp", H0[b][:], h0src(b), h0_ds[b], writes=(h0b[b],))
    ssd_A1(blk, T)
    ssd_A2(blk, T, 0)
    ssd_A2(blk, T, 1)
    HT_pss = [PS[2][:].bitcast(BF16), PS[3][:].bitcast(BF16)]
    for b in range(16):
        s = b % 2
        s3 = b % NH
        so = b % NO
        k.op("act", lambda e: e.copy(H0B[s][:], H0[s3][:]), reads=(h0b[s3],), writes=(h0bb[s],))
        HT_ps = HT_pss[s]
        for j in range(8):
            k.op("pe", lambda e, j=j: e.transpose(HT_ps[:, j * 128:(j + 1) * 128], H0B[s][:, j, :], IDB[:]),
                 reads=(h0bb[s], cb), writes=(PB[2 + s],), track=(j == 7))
        k.op("act", lambda e: e.copy(H0T[s][:, :], HT_ps[:, :]), reads=(PB[2 + s],), writes=(h0tb[s],))
        k.op("pool", lambda e: e.tensor_tensor(CZb[s][:], XBC[:, 10:12, t0:t0 + 64], BDROW[:, b, :].unsqueeze(1).broadcast_to([128, 2, 64]), ALU.mult),
             reads=(xbcb[10], xbcb[11], cb), writes=(czb[s],))
        k.op("pool", lambda e: e.tensor_tensor(BZb[s][:], T["BTOK"][0:64, :], BDCOL[:, b:b + 1].broadcast_to([64, 256]), ALU.mult),
             reads=(btokb, cb), writes=(bzb[s],))
        for g in range(2):
            k.op("pe", lambda e, g=g: e.matmul(PS[6 + g][0:64, :], CZb[s][:, g, :], H0T[s][:, g * 512:(g + 1) * 512], start=(b == 0), stop=(b == 15)),
                 reads=(czb[s], h0tb[s]), writes=(PB[6 + g],), track=(b == 15 or g == 1))
        cb0 = 4 if s == 0 else 0
        csv = lambda j: PS[cb0 + j // 4][:, (j % 4) * 128:(j % 4 + 1) * 128]
        for j in range(8):
            k.op("pe", lambda e, j=j: e.matmul(csv(j), T["XW"][0:64, j * 128:(j + 1) * 128], BZb[s][:, (j // 4) * 128:(j // 4 + 1) * 128], start=True, stop=True),
                 reads=(xwb, bzb[s]), writes=(PB[cb0 + j // 4],), track=(j % 4 == 3))
        for j in range(8):
            k.op("dve", lambda e, j=j: e.scalar_tensor_tensor(OST[so][:, j, :], H0[s3][:, j, :], CDX[:, j, b:b + 1], csv(j), ALU.mult, ALU.add),
                 reads=(cdxb, PB[cb0 + j // 4], h0b[s3]), writes=(ostb[so],), track=(j == 7))
        k.dma("sp", nssm_s[b].rearrange("(j a) p n -> (a p) j n", a=2), OST[so][:], ost_ds[so], reads=(ostb[so],), is_out=True)
        if b + NH < 16:
            k.dma("sp", H0[s3][:], h0src(b + NH), h0_ds[s3], writes=(h0b[s3],))
    ssd_Y(blk, T)
    ssd_P(blk, True, SST, HP, 0, sstb, hpb)
    ssd_P(blk, True, SST, HP, 1, sstb, hpb)
    dump("ys", YMIX[:, 8:16, :], ysb, BF16)
    k.barrier()

    r3.reset()
    r3.off = 20480
    WS = WSU
    wsb = wsub
    U = [r3.alloc([128, 16 + NP], F32) for _ in range(2)]
    US = [r3.alloc([128, 19, 16], F32) for _ in range(2)]
    TA = [r3.alloc([128, 16 + NP], F32) for _ in range(2)]
    TAS = [r3.alloc([128, 19, 16], F32) for _ in range(2)]
    DD = [r3.alloc([128, 2, NT], BF16) for _ in range(2)]
    SPL = r3.alloc([128, 8, 240], F32)
    ub_, tab, ddb = k.bufl(2, "u"), k.bufl(2, "ta"), k.bufl(2, "dd")
    splb = k.buf("spl")
    ypb = k.bufl(8, "yp")
    k.dma("sp", SPL[:], spool.rearrange("p (c f) -> p c f", c=8), ld_ds[6], writes=(splb,))
    for q in range(2):
        k.op("pool", lambda e, q=q: e.memset(U[q][:, 0:16], 0.0), writes=(ub_[q],))
    pending_pool = []

    def flush_pool():
        while pending_pool:
            g_, dq_ = pending_pool.pop(0)
            for ec in range(2):
                for ti, (t0, n) in enumerate(TT):
                    bi = next_bank()
                    for k2 in range(2):
                        k.op("pe", lambda e, k2=k2: e.matmul(PS[bi][:, 0:n], WPOOL[:, g_, k2, ec * 128:(ec + 1) * 128], DD[dq_][:, k2, t0:t0 + n],
                                                             start=(k2 == 0), stop=(k2 == 1)),
                             reads=(wpb, ddb[dq_]), writes=(PB[bi],), track=(k2 == 1))
                    yc = 2 * g_ + ec
                    k.op("act", lambda e: e.activation(YMIX[:, yc, t0:t0 + n], PS[bi][:, 0:n], AF.Copy, scale=VEC[:, V_PSC + yc:V_PSC + yc + 1]),
                         reads=(PB[bi], vb), writes=(ypb[yc],))

    for ci, c in enumerate((6, 7, 4, 5, 2, 3, 0, 1)):
        grp, c4 = c // 4, c % 4
        q = c % 2
        g = c // 2
        w = POOLW[g]
        k.op("pool", lambda e: e.tensor_copy(US[q][:, 0:15, :], SPL[:, c, :].rearrange("p (j b) -> p j b", j=15)),
             reads=(splb,), writes=(ub_[q],))
        for ti, (t0, n) in enumerate(TT):
            bi = next_bank()
            for kc in range(KC):
                k.op("pe", lambda e, kc=kc: e.matmul(PS[bi][:, 0:n], WS[grp][:, kc, c4 * 128:(c4 + 1) * 128], XN[:, kc, t0:t0 + n],
                                                     start=(kc == 0), stop=(kc == KC - 1)),
                     reads=(XNb[ti], wsb[grp]), writes=(PB[bi],), track=(kc == KC - 1))
            if ti < 4:
                k.op("act", lambda e: e.copy(U[q][:, 16 + t0:16 + t0 + n], PS[bi][:, 0:n]), reads=(PB[bi],), writes=(ub_[q],))
            else:
                k.op("act", lambda e: e.copy(US[q][:, 15:19, :], PS[bi][:, 0:64].rearrange("p (b l) -> p l b", l=4)),
                     reads=(PB[bi],), writes=(ub_[q],))
        flush_pool()
        if ci == 7:
            XNf = XN[:].rearrange("p c t -> p (c t)")
            WO = [XNf[:, 4096 * i:4096 * (i + 1)].rearrange("p (a b) -> p a b", a=16) for i in range(4)]
            wob = k.bufl(4, "wo")
            wo_ds = [k.dsem("wo") for _ in range(4)]
            for i in range(4):
                k.dma("pool", WO[i][:], w_out.rearrange("(kc p) n -> p kc n", p=128)[:, :, i * 256:(i + 1) * 256], wo_ds[i],
                      writes=[wob[i]] + XNb)
        k.dma("sp", npool_p[:, c, :], U[q][:, 16 + NP - 15:16 + NP], next_out_ds(), reads=(ub_[q],), is_out=True)
        k.dma("sp", npool_s[:, c, :], US[q][:, 4:19, :].rearrange("p j b -> p (j b)"), next_out_ds(), reads=(ub_[q],), is_out=True)
        src, srcs, srcb = U[q], US[q], ub_[q]
        step = 1
        pp = 0
        while step < w:
            dst, dsts, dstb = TA[pp], TAS[pp], tab[pp]
            lo = 2 * step - 1
            k.op("dve", lambda e, src=src, dst=dst, lo=lo, step=step: e.tensor_tensor(dst[:, lo:], src[:, lo:], src[:, lo - step:16 + NP - step], ALU.add),
                 reads=(srcb,), writes=(dstb,))
            k.op("dve", lambda e, srcs=srcs, dsts=dsts, lo=lo, step=step: e.tensor_tensor(dsts[:, lo:, :], srcs[:, lo:, :], srcs[:, lo - step:19 - step, :], ALU.add),
                 reads=(srcb,), writes=(dstb,))
            src, srcs, srcb = dst, dsts, dstb
            pp = 1 - pp
            step *= 2
        dq = g % 2
        cc = c % 2
        k.op("dve", lambda e: e.scalar_tensor_tensor(DD[dq][:, cc, 0:NP], src[:, 16:16 + NP], 1.0 / w, U[q][:, 16:16 + NP], ALU.mult, ALU.subtract),
             reads=(srcb, ub_[q]), writes=(ddb[dq],))
        k.op("dve", lambda e: e.tensor_tensor(SMALL[:, 16:32], src[:, 16:32], INVC[:, g, :], ALU.mult), reads=(srcb, cb), writes=(smb,))
        k.op("dve", lambda e: e.tensor_tensor(DD[dq][:, cc, 0:16], SMALL[:, 16:32], U[q][:, 16:32], ALU.subtract), reads=(smb, ub_[q]), writes=(ddb[dq],))
        k.op("dve", lambda e: e.scalar_tensor_tensor(DD[dq][:, cc, NP:NT].rearrange("p (b l) -> p l b", l=4), srcs[:, 15:19, :], 1.0 / w,
                                                     US[q][:, 15:19, :], ALU.mult, ALU.subtract),
             reads=(srcb, ub_[q]), writes=(ddb[dq],))
        if cc == 1:
            pending_pool.append((g, dq))
    flush_pool()
    dump("yp", YMIX[:, 0:8, :], ypb, BF16)
    k.barrier(keep=list(zip(wo_ds, wob)))

    Hb = [k.bufl(5, "h%d_" % c) for c in range(KC)]
    hld = [k.dsem("hld") for _ in range(KC)]
    for c in range(KC):
        k.dma("sp", H[:, c, :], xT[c * 128:(c + 1) * 128, :], hld[c], reads=(Hb[c - 1][0],) if c else (), writes=Hb[c])
    for i in range(4):
        k.op("dve", lambda e, i=i: e.tensor_tensor(WO[i][:, 8:16, :], WO[i][:, 8:16, :],
                                                   VEC[:, V_GSSM:V_GSSM + 8].unsqueeze(2).broadcast_to([128, 8, 256]), ALU.mult),
             reads=(wob[i], vb), writes=(wob[i],))
    for cg in range(4):
        s = cg
        for oc in range(2):
            c = cg * 2 + oc
            for ti, (t0, n) in enumerate(TT):
                bi = next_bank()
                for kc in range(16):
                    k.op("pe", lambda e, kc=kc: e.matmul(PS[bi][:, 0:n], WO[s][:, kc, oc * 128:(oc + 1) * 128], YMIX[:, kc, t0:t0 + n],
                                                         start=(kc == 0), stop=(kc == 15)),
                         reads=(wob[s],), writes=(PB[bi],), track=(kc == 15))
                k.op("dve", lambda e: e.tensor_tensor(H[:, c, t0:t0 + n], H[:, c, t0:t0 + n], PS[bi][:, 0:n], ALU.add),
                     reads=(PB[bi], Hb[c][ti]), writes=(Hb[c][ti],))
    dump("h1", H[:], [b for l in Hb for b in l])
    k.barrier()

    def norm_from_h(gcol):
        r3x.reset()
        SQ = [r3x.alloc([128, KC, 512], BF16)] * 2
        RS = [r3x.alloc([128, 512], F32) for _ in range(2)]
        sqb, rsb = [k.buf("sq")] * 2, k.bufl(2, "rs")
        def tile(ti):
            t0, n = TT[ti]
            rmsnorm_tile(ti, H[:, :, t0:t0 + n], [Hb[c][ti] for c in range(KC)], gcol,
                         lambda kc: XN[:, kc, t0:t0 + n], (XNb[ti],), SQ, RS, sqb, rsb)
        return [sqb[0]] + rsb, tile

    WS = [R2[:, 16896 + 4096 * i:16896 + 4096 * (i + 1)].rearrange("p (a b) -> p a b", a=KC) for i in range(3)]
    wsb = k.bufl(3, "wsf")
    wi = [0]

    def next_ws(src):
        s = wi[0] % 3
        wi[0] += 1
        load_w(WS[s][:], src, WS_DS[s], wsb[s])
        return s
    w2v = w_ff2.rearrange("(fc p) n -> p fc n", p=128)
    pref = [next_ws(wview_kc(w_ff1, 0, 512)), next_ws(wview_kc(w_ff1, 512, 512)), next_ws(w2v[:, 0:8, 0:512])]
    n2bufs, norm2_tile = norm_from_h(V_GMLP)
    norm2_tile(0)
    norm2_tile(1)
    r2.reset()
    A = r2.alloc([128, KC, NT], BF16)
    r2.off += 3 * 8192
    RT = [r2.alloc([128, 512], F32) for _ in range(2)]
    WPLE = r2.alloc([128, 2, D], BF16)
    wple_off = r2.off
    rtb = k.bufl(2, "rt")
    ab = [k.bufl(5, "a%d_" % c) for c in range(KC)]
    rti = [0]
    for G in range(4):
        for sub in range(2):
            s = pref.pop(0) if pref else next_ws(wview_kc(w_ff1, G * 1024 + sub * 512, 512))
            for c4 in range(4):
                fc = sub * 4 + c4
                for ti, (t0, n) in enumerate(TT):
                    bi = next_bank()
                    for kc in range(KC):
                        k.op("pe", lambda e, kc=kc: e.matmul(PS[bi][:, 0:n], WS[s][:, kc, c4 * 128:(c4 + 1) * 128], XN[:, kc, t0:t0 + n],
                                                             start=(kc == 0), stop=(kc == KC - 1)),
                             reads=(XNb[ti], wsb[s]), writes=(PB[bi],), track=(kc == KC - 1))
                    if (G, sub, c4) == (0, 0, 0) and ti + 2 < 5:
                        norm2_tile(ti + 2)
                    r = rti[0] % 2
                    rti[0] += 1
                    k.op("act", lambda e: e.activation(RT[r][:, 0:n], PS[bi][:, 0:n], AF.Relu), reads=(PB[bi],), writes=(rtb[r],))
                    k.op("dve", lambda e: e.tensor_tensor(A[:, fc, t0:t0 + n], RT[r][:, 0:n], RT[r][:, 0:n], ALU.mult),
                         reads=(rtb[r],), writes=(ab[fc][ti],))
        for half in range(2):
            s = pref.pop(0) if pref else next_ws(w2v[:, G * 8:(G + 1) * 8, half * 512:(half + 1) * 512])
            for oc in range(4):
                c = half * 4 + oc
                for ti, (t0, n) in enumerate(TT):
                    bi = next_bank()
                    for fc in range(KC):
                        k.op("pe", lambda e, fc=fc: e.matmul(PS[bi][:, 0:n], WS[s][:, fc, oc * 128:(oc + 1) * 128], A[:, fc, t0:t0 + n],
                                                             start=(fc == 0), stop=(fc == KC - 1)),
                             reads=(ab[fc][ti], wsb[s]), writes=(PB[bi],), track=(fc == KC - 1))
                    k.op("dve", lambda e: e.tensor_tensor(H[:, c, t0:t0 + n], H[:, c, t0:t0 + n], PS[bi][:, 0:n], ALU.add),
                         reads=(PB[bi], Hb[c][ti]), writes=(Hb[c][ti],))
    r3x.reset()
    WG = [r3x.alloc([128, KC, 512], BF16) for _ in range(2)]
    wgb = k.bufl(2, "wsg")
    wpleb = k.buf("wple")
    wg_ds = [k.dsem("wg") for _ in range(2)]
    wple_ds = k.dsem("wple")
    for half in range(2):
        k.dma("pool", WG[half][:], wview_kc(w_gate, half * 512, 512), wg_ds[half], writes=[wgb[half]] + n2bufs)
    k.dma("pool", WPLE[:], w_ple.rearrange("(c p) n -> p c n", p=128), wple_ds, writes=(wpleb,))
    dump("h2", H[:], [b for l in Hb for b in l])
    k.barrier(keep=[(wg_ds[0], wgb[0]), (wg_ds[1], wgb[1]), (wple_ds, wpleb)])

    r2.reset()
    SQ = [r2.alloc([128, KC, 512], BF16) for _ in range(2)]
    RS = [r2.alloc([128, 512], F32) for _ in range(2)]
    WS = WG
    PT = r2.alloc([128, 2, NT], BF16)
    SG = [r2.alloc([128, 512], F32) for _ in range(2)]
    YOC = [r2.alloc([128, 512], F32) for _ in range(4)]
    assert r2.off <= wple_off - 4096
    sqb, rsb = k.bufl(2, "sq"), k.bufl(2, "rs")
    wsb, sgb, yocb = wgb, k.bufl(2, "sg"), k.bufl(4, "yoc")
    yoc_ds = [k.dsem("yoc") for _ in range(4)]
    ptb = k.buf("pt")
    k.dma("pool", PT[:], pT.rearrange("(c p) t -> p c t", p=128), ld_ds[9], writes=(ptb,))
    gi = [0]

    def n3(ti):
        t0, n = TT[ti]
        rmsnorm_tile(ti, H[:, :, t0:t0 + n], [Hb[c][ti] for c in range(KC)], V_GPLE,
                     lambda kc: XN[:, kc, t0:t0 + n], (XNb[ti],), SQ, RS, sqb, rsb)

    def gate(ti):
        t0, n = TT[ti]
        for c in range(KC):
            half, oc = c // 4, c % 4
            bi = next_bank()
            for kc in range(KC):
                k.op("pe", lambda e, kc=kc: e.matmul(PS[bi][:, 0:n], WS[half][:, kc, oc * 128:(oc + 1) * 128], XN[:, kc, t0:t0 + n],
                                                     start=(kc == 0), stop=(kc == KC - 1)),
                     reads=(XNb[ti], wsb[half]), writes=(PB[bi],), track=(kc == KC - 1))
            bj = next_bank()
            for k2 in range(2):
                k.op("pe", lambda e, k2=k2: e.matmul(PS[bj][:, 0:n], WPLE[:, k2, c * 128:(c + 1) * 128], PT[:, k2, t0:t0 + n],
                                                     start=(k2 == 0), stop=(k2 == 1)),
                     reads=(wpleb, ptb), writes=(PB[bj],), track=(k2 == 1))
            r = gi[0] % 2
            gi[0] += 1
            k.op("act", lambda e: e.activation(SG[r][:, 0:n], PS[bi][:, 0:n], AF.Sigmoid), reads=(PB[bi],), writes=(sgb[r],))
            k.op("dve", lambda e: e.tensor_tensor(SG[r][:, 0:n], SG[r][:, 0:n], PS[bj][:, 0:n], ALU.mult),
                 reads=(PB[bj], sgb[r]), writes=(sgb[r],))
            k.op("dve", lambda e: e.tensor_tensor(H[:, c, t0:t0 + n], H[:, c, t0:t0 + n], SG[r][:, 0:n], ALU.add),
                 reads=(sgb[r], Hb[c][ti]), writes=(Hb[c][ti],))

    def fn(ti):
        t0, n = TT[ti]

        def post(kc):
            k.dma("sp", yT[kc * 128:(kc + 1) * 128, t0:t0 + n], YOC[kc % 4][:, 0:n], yoc_ds[kc % 4], reads=(yocb[kc % 4],), is_out=True)
        rmsnorm_tile(ti, H[:, :, t0:t0 + n], [Hb[c][ti] for c in range(KC)], V_GFIN,
                     lambda kc: YOC[kc % 4][:, 0:n], lambda kc: (yocb[kc % 4],), SQ, RS, sqb, rsb, post=post)

    for ti in range(5):
        n3(ti)
    gate(0)
    gate(1)
    fn(0)
    gate(2)
    fn(1)
    gate(3)
    fn(2)
    gate(4)
    fn(3)
    fn(4)
    k.finish()
    return k, dbg


_CACHE = {}


def _host_inputs(inp, i):
    f = np.float32
    xs = inp["x_sample"][16 * i:16 * i + 16].reshape(64, D)
    xTc = np.ascontiguousarray(np.concatenate([inp["x_prompt"][i], xs], axis=0).T, dtype=f)
    ps = inp["p_sample"][0, 16 * i:16 * i + 16].reshape(64, 256)
    pTc = np.ascontiguousarray(np.concatenate([inp["p_prompt"][0, i], ps], axis=0).T, dtype=f)
    spool = np.ascontiguousarray(inp["state_pool"][0, 16 * i:16 * i + 16].reshape(16, 15, 8, 128).transpose(3, 2, 1, 0).reshape(128, 8 * 15 * 16), dtype=f)
    sconv = np.ascontiguousarray(inp["state_conv"][0, 16 * i:16 * i + 16].reshape(16, 3, 12, 128).transpose(3, 2, 1, 0).reshape(128, 12 * 3 * 16), dtype=f)
    sssm = np.ascontiguousarray(inp["state_ssm"][0, 16 * i:16 * i + 16], dtype=f)
    return {"xT": xTc, "pT": pTc, "spool": spool, "sconv": sconv, "sssm": sssm}


def _host_shared(inp):
    f = np.float32
    cols = lambda v, n: np.asarray(v, dtype=f).reshape(n, 128).T
    vecs = np.zeros((128, NV), dtype=f)
    vecs[:, V_GMIX:V_GMIX + 8] = cols(inp["norm_mix_g"][0], 8)
    vecs[:, V_GMLP:V_GMLP + 8] = cols(inp["norm_mlp_g"][0], 8)
    vecs[:, V_GPLE:V_GPLE + 8] = cols(inp["norm_ple_g"][0], 8)
    vecs[:, V_GFIN:V_GFIN + 8] = cols(inp["final_norm_g"], 8)
    vecs[:, V_PSC:V_PSC + 8] = cols(inp["pool_scale"][0], 8)
    vecs[:, V_GSSM:V_GSSM + 8] = cols(inp["ssm_norm_g"][0], 8)
    vecs[:, V_CB:V_CB + 12] = cols(inp["conv_b"][0], 12)
    for j in range(4):
        vecs[:, V_CW + 12 * j:V_CW + 12 * j + 12] = cols(inp["conv_w"][0, j], 12)
    hvv = np.stack([inp["dt_bias"][0], inp["a_log"][0], inp["d_skip"][0]], axis=1).astype(f)
    c = np.ascontiguousarray
    return {"w_in": c(inp["w_in"][0], dtype=f), "w_pool": c(inp["w_pool"][0], dtype=f), "w_out": c(inp["w_out"][0], dtype=f),
            "w_ff1": c(inp["w_ff1"][0], dtype=f), "w_ff2": c(inp["w_ff2"][0], dtype=f), "w_gate": c(inp["w_gate"][0], dtype=f),
            "w_ple": c(inp["w_ple"][0], dtype=f), "vecs": vecs, "hv": c(hvv), "dsk": c(inp["d_skip"][0].reshape(1, 16), dtype=f)}


def kernel(debug=False, **inp):
    inp = {n: np.asarray(v) for n, v in inp.items()}
    key = bool(debug)
    if key not in _CACHE:
        _CACHE[key] = build_program(debug=debug)[0]
    kb = _CACHE[key]
    shared = _host_shared(inp)
    in_maps = []
    for i in range(8):
        m = dict(shared)
        m.update(_host_inputs(inp, i))
        in_maps.append(m)
    res = run_bass_kernel_spmd(kb.nc, in_maps, core_ids=list(range(8)))
    R = res.results
    f = np.float32
    y_prompt = np.stack([R[i]["yT"][:, :NP].T for i in range(8)]).astype(f)
    y_sample = np.concatenate([R[i]["yT"][:, NP:].T.reshape(16, 4, D) for i in range(8)]).astype(f)
    pool_p = np.stack([R[i]["npool_p"].transpose(2, 1, 0).reshape(15, D) for i in range(8)])[None].astype(f)
    conv_p = np.stack([R[i]["nconv_p"].transpose(2, 1, 0).reshape(3, 1536) for i in range(8)])[None].astype(f)
    ssm_p = np.stack([R[i]["nssm_p"].reshape(128, 16, 64).transpose(1, 2, 0) for i in range(8)])[None].astype(f)
    pool_s = np.concatenate([R[i]["npool_s"].reshape(128, 8, 15, 16).transpose(3, 2, 1, 0).reshape(16, 15, D) for i in range(8)])[None].astype(f)
    conv_s = np.concatenate([R[i]["nconv_s"].reshape(128, 12, 3, 16).transpose(3, 2, 1, 0).reshape(16, 3, 1536) for i in range(8)])[None].astype(f)
    ssm_s = np.concatenate([R[i]["nssm_s"] for i in range(8)])[None].astype(f)
    outs = (np.ascontiguousarray(y_prompt), np.ascontiguousarray(y_sample), np.ascontiguousarray(pool_p), np.ascontiguousarray(conv_p),
            np.ascontiguousarray(ssm_p), np.ascontiguousarray(pool_s), np.ascontiguousarray(conv_s), np.ascontiguousarray(ssm_s))
    if debug:
        return outs, R
    return outs
```

```python
import numpy as np
from contextlib import ExitStack
import concourse.bass as bass
import concourse.mybir as mybir
from concourse.bass_utils import run_bass_kernel_spmd

F32 = mybir.dt.float32
BF16 = mybir.dt.bfloat16
AF = mybir.ActivationFunctionType
ALU = mybir.AluOpType
AX = mybir.AxisListType

D = 1024
KC = 8
NP = 2048
NS = 64
NT = NP + NS
NB = 17
TT = [(0, 512), (512, 512), (1024, 512), (1536, 512), (2048, 64)]
POOLW = (2, 4, 8, 16)
EPS = 1e-6
NV = 108
V_GMIX, V_GMLP, V_GPLE, V_GFIN, V_PSC, V_GSSM, V_CB, V_CW = 0, 8, 16, 24, 32, 40, 48, 60


class Buf:
    __slots__ = ("name", "w", "r", "excl")

    def __init__(self, name, excl=False):
        self.name = name
        self.w = None
        self.r = {}
        self.excl = excl


class DSem:
    def __init__(self, key, h):
        self.key = key
        self.h = h
        self.count = 0


class KB:
    def __init__(self):
        self.nc = bass.Bass("TRN2", target_bir_lowering=False)
        self.es = ExitStack()
        nc = self.nc
        self.E = {"pe": nc.tensor, "act": nc.scalar, "dve": nc.vector, "pool": nc.gpsimd, "sp": nc.sync}
        self.esem = {e: self.es.enter_context(nc.semaphore("sem_" + e)) for e in ("pe", "act", "dve", "pool")}
        self.seq = {e: 0 for e in self.esem}
        self.waited = {e: {} for e in self.E}
        self.bufs = []
        self.dsems = []
        self.bar = self.es.enter_context(nc.semaphore("sem_bar"))
        self.bar_count = 0
        self.out_toks = []
        self.nbuf = 0

    def buf(self, name="b"):
        b = Buf(name)
        self.bufs.append(b)
        return b

    def bufl(self, n, name="b"):
        return [self.buf(name + str(i)) for i in range(n)]

    def dsem(self, name):
        d = DSem("d_" + name + str(len(self.dsems)), self.es.enter_context(self.nc.semaphore("ds_" + name + str(len(self.dsems)))))
        self.dsems.append(d)
        return d

    def sb(self, name, shape, dt):
        return self.es.enter_context(self.nc.sbuf_tensor(name, shape, dt))

    def psum(self, name, shape, dt):
        return self.es.enter_context(self.nc.psum_tensor(name, shape, dt))

    def _deps(self, eng, reads, writes, is_dma):
        deps = {}

        def need(tok, kind):
            if tok is None:
                return
            key, sem, val = tok
            if (not is_dma) and key == eng and val > self.seq[eng]:
                return
            if key not in deps or deps[key][1] < val:
                deps[key] = (sem, val)

        for b in reads:
            need(b.w, "raw")
            if b.excl:
                for t in b.r.values():
                    need(t, "war")
        for b in writes:
            need(b.w, "waw")
            for t in b.r.values():
                need(t, "war")
        wd = self.waited[eng]
        for key, (sem, val) in deps.items():
            if wd.get(key, 0) < val:
                self.E[eng].wait_ge(sem, val)
                wd[key] = val

    def _note(self, tok, reads, writes):
        for b in reads:
            if b.excl and b not in writes:
                b.w = tok
                b.r = {}
                continue
            old = b.r.get(tok[0])
            if old is None or old[2] < tok[2]:
                b.r[tok[0]] = tok
        for b in writes:
            b.w = tok
            b.r = {}

    def op(self, eng, fn, reads=(), writes=(), track=True):
        self._deps(eng, reads, writes, False)
        ins = fn(self.E[eng])
        if track:
            self.seq[eng] += 1
            ins.then_inc(self.esem[eng], 1)
            tok = (eng, self.esem[eng], self.seq[eng])
        else:
            tok = (eng, self.esem[eng], self.seq[eng] + 1)
        self._note(tok, reads, writes)
        return tok

    def dma(self, q, out, in_, ds, reads=(), writes=(), is_out=False):
        self._deps(q, reads, writes, True)
        self.E[q].dma_start(out=out, in_=in_).then_inc(ds.h, 16)
        ds.count += 16
        tok = (ds.key, ds.h, ds.count)
        self._note(tok, reads, writes)
        if is_out:
            self.out_toks.append(tok)
        return tok

    def barrier(self, keep=()):
        skip = {d.key for d, _ in keep}
        keepb = {id(b) for _, b in keep}
        sp = self.E["sp"]
        wd = self.waited["sp"]
        for e, s in self.esem.items():
            if wd.get(e, 0) < self.seq[e]:
                sp.wait_ge(s, self.seq[e])
                wd[e] = self.seq[e]
        for d in self.dsems:
            if d.key in skip:
                continue
            if d.count and wd.get(d.key, 0) < d.count:
                sp.wait_ge(d.h, d.count)
                wd[d.key] = d.count
        self.bar_count += 1
        sp.sem_inc(self.bar, 1)
        for e in ("pe", "act", "dve", "pool"):
            self.E[e].wait_ge(self.bar, self.bar_count)
            w = self.waited[e]
            for e2 in self.esem:
                w[e2] = self.seq[e2]
            for d in self.dsems:
                if d.key not in skip:
                    w[d.key] = d.count
        for b in self.bufs:
            if id(b) in keepb:
                continue
            b.w = None
            b.r = {}

    def finish(self):
        sp = self.E["sp"]
        wd = self.waited["sp"]
        for key, sem, val in self.out_toks:
            if wd.get(key, 0) < val:
                sp.wait_ge(sem, val)
                wd[key] = val


class Region:
    def __init__(self, arena_ap, nbytes):
        self.a = arena_ap
        self.nbytes = nbytes
        self.off = 0

    def reset(self):
        self.off = 0

    def alloc(self, shape, dt, parts=128):
        esz = 4 if dt == F32 else 2
        n = 1
        for s in shape[1:]:
            n *= s
        nb = (n * esz + 31) // 32 * 32
        assert self.off + nb <= self.nbytes, ("region overflow", self.off, nb, self.nbytes)
        v = self.a[0:shape[0], self.off // 2:(self.off + n * esz) // 2]
        self.off += nb
        if dt == F32:
            v = v.bitcast(F32)
        if len(shape) == 3:
            v = v.rearrange("p (a b) -> p a b", a=shape[1])
        elif len(shape) == 4:
            v = v.rearrange("p (a b c) -> p a b c", a=shape[1], b=shape[2])
        return v


def build_program(debug=False):
    k = KB()
    nc = k.nc
    dram_in = lambda n, s: nc.dram_tensor(n, s, F32, kind="ExternalInput").ap()
    dram_out = lambda n, s: nc.dram_tensor(n, s, F32, kind="ExternalOutput").ap()
    xT = dram_in("xT", [D, NT])
    pT = dram_in("pT", [256, NT])
    spool = dram_in("spool", [128, 8 * 15 * 16])
    sconv = dram_in("sconv", [128, 12 * 3 * 16])
    sssm = dram_in("sssm", [16, 16, 64, 128])
    w_in = dram_in("w_in", [D, 3600])
    w_pool = dram_in("w_pool", [4, 256, 256])
    w_out = dram_in("w_out", [2048, D])
    w_ff1 = dram_in("w_ff1", [D, 4096])
    w_ff2 = dram_in("w_ff2", [4096, D])
    w_gate = dram_in("w_gate", [D, D])
    w_ple = dram_in("w_ple", [256, D])
    vecs = dram_in("vecs", [128, NV])
    hv = dram_in("hv", [16, 3])
    dsk = dram_in("dsk", [1, 16])
    yT = dram_out("yT", [D, NT])
    npool_p = dram_out("npool_p", [128, 8, 15])
    nconv_p = dram_out("nconv_p", [128, 12, 3])
    nssm_p = dram_out("nssm_p", [128, 1024])
    npool_s = dram_out("npool_s", [128, 8, 15 * 16])
    nconv_s = dram_out("nconv_s", [128, 12, 3 * 16])
    nssm_s = dram_out("nssm_s", [16, 16, 64, 128])
    dbg = {}

    XN = k.sb("XN", [128, KC, NT], BF16)
    R2 = k.sb("R2", [128, 33792], BF16)
    R3 = k.sb("R3", [128, 42752], BF16)
    YMIX = R2[:, :].rearrange("p (c t) -> p c t", c=16)
    r2 = Region(R2[:, :], 67584)
    r3 = Region(R3[:, :], 85504)
    r2h = Region(R2[:, 0:16896], 33792)
    H = R3[:, 0:33792].bitcast(F32).rearrange("p (c t) -> p c t", c=KC)
    r3x = Region(R3[:, 33792:42752], 17920)

    IDB = k.sb("IDB", [128, 128], BF16)
    IDF = k.sb("IDF", [128, 128], F32)
    ONESB = k.sb("ONESB", [128, 128], BF16)
    ONESF = k.sb("ONESF", [128, 128], F32)
    TRIF = k.sb("TRIF", [128, 128], F32)
    TRIB = k.sb("TRIB", [128, 128], BF16)
    UB = k.sb("UB", [128, 128], BF16)
    TRIFS = k.sb("TRIFS", [64, 64], F32)
    TRIBS = k.sb("TRIBS", [64, 64], BF16)
    UBS = k.sb("UBS", [64, 64], BF16)
    BDFS = k.sb("BDFS", [64, 64], F32)
    DI = k.sb("DI", [128, 16, 128], BF16)
    BDROW = k.sb("BDROW", [128, 16, 64], BF16)
    BDCOL = k.sb("BDCOL", [64, 16], BF16)
    VEC = k.sb("VEC", [128, NV], F32)
    HV = k.sb("HV", [16, 4], F32)
    DB = k.sb("DB", [128, 16], F32)
    INVC = k.sb("INVC", [128, 4, 16], F32)
    TOK = k.sb("TOK", [128, NB, 5, 16], F32)
    AHL = k.sb("AHL", [128, NB, 2, 16], BF16)
    DTAS = k.sb("DTAS", [16, 64], F32)
    SST = k.sb("SST", [128, 1024], F32)
    HP = k.sb("HP", [128, 1024], BF16)
    SMALL = k.sb("SMALL", [128, 64], F32)
    TMPC = k.sb("TMPC", [128, 128], F32)
    CDX = k.sb("CDX", [128, 8, 16], F32)
    WDT = k.sb("WDT", [128, KC, 16], BF16)

    PS = [k.psum("ps%d" % i, [128, 512], F32) for i in range(8)]
    PB = k.bufl(8, "pb")
    for b_ in PB:
        b_.excl = True
    bank_rr = [0]

    def next_bank():
        i = bank_rr[0]
        bank_rr[0] = (i + 1) % 8
        return i

    cb = k.buf("const")
    vb = k.buf("vec")
    XNb = k.bufl(5, "xn")
    WS_DS = [k.dsem("ws") for _ in range(3)]
    ld_ds = [k.dsem("ld") for _ in range(12)]
    out_ds = [k.dsem("out") for _ in range(8)]
    od_i = [0]

    def next_out_ds():
        d = out_ds[od_i[0] % len(out_ds)]
        od_i[0] += 1
        return d

    def dump(name, ap, bufs, dt=F32):
        if not debug:
            return
        t = nc.dram_tensor("dbg_" + name, list(ap.shape), dt, kind="ExternalOutput").ap()
        k.dma("sp", t, ap, k.dsem("dbg"), reads=bufs, is_out=True)

    def pool_op(fn, reads=(), writes=(cb,)):
        return k.op("pool", fn, reads=reads, writes=writes)

    def sel(t_ap, pattern, cmp_op, base, cm):
        pool_op(lambda e: e.affine_select(out=t_ap, in_=t_ap, pattern=pattern, compare_op=cmp_op, fill=0.0,
                                          base=base, channel_multiplier=cm), reads=(cb,))

    for t in (IDF, ONESF, TRIF):
        pool_op(lambda e, t=t: e.memset(t[:], 1.0))
    sel(IDF[:], [[-1, 128]], ALU.is_equal, 0, 1)
    sel(TRIF[:], [[1, 128]], ALU.is_ge, 0, -1)
    UF = TMPC
    pool_op(lambda e: e.memset(UF[:], 1.0))
    sel(UF[:], [[-1, 128]], ALU.is_gt, 0, 1)
    pool_op(lambda e: e.memset(BDFS[:], 1.0))
    bdv = BDFS[:].rearrange("p (b l) -> p b l", l=4)
    sel(bdv, [[-4, 16], [0, 4]], ALU.is_ge, 0, 1)
    sel(bdv, [[4, 16], [0, 4]], ALU.is_ge, 3, -1)
    k.op("dve", lambda e: e.tensor_copy(IDB[:], IDF[:]), reads=(cb,), writes=(cb,))
    k.op("dve", lambda e: e.tensor_copy(ONESB[:], ONESF[:]), reads=(cb,), writes=(cb,))
    k.op("dve", lambda e: e.tensor_copy(TRIB[:], TRIF[:]), reads=(cb,), writes=(cb,))
    k.op("dve", lambda e: e.tensor_copy(UB[:], UF[:]), reads=(cb,), writes=(cb,))
    k.op("dve", lambda e: e.tensor_tensor(TRIFS[:], TRIF[0:64, 0:64], BDFS[:], ALU.mult), reads=(cb,), writes=(cb,))
    k.op("dve", lambda e: e.tensor_copy(TRIBS[:], TRIFS[:]), reads=(cb,), writes=(cb,))
    k.op("dve", lambda e: e.tensor_tensor(UBS[:], UF[0:64, 0:64], BDFS[:], ALU.mult), reads=(cb,), writes=(cb,))
    pool_op(lambda e: e.memset(BDROW[:], 1.0))
    sel(BDROW[:], [[-4, 16], [1, 64]], ALU.is_ge, 0, 0)
    sel(BDROW[:], [[4, 16], [-1, 64]], ALU.is_ge, 3, 0)
    pool_op(lambda e: e.memset(BDCOL[:], 1.0))
    sel(BDCOL[:], [[-4, 16]], ALU.is_ge, 0, 1)
    sel(BDCOL[:], [[4, 16]], ALU.is_ge, 3, -1)
    k.dma("sp", VEC[:], vecs, ld_ds[0], writes=(vb,))
    k.dma("sp", HV[:, 0:3], hv, ld_ds[1], writes=(vb,))
    k.dma("sp", DB[:], dsk.partition_broadcast(128), ld_ds[2], writes=(vb,))
    k.op("dve", lambda e: e.tensor_tensor(DI[:], IDF[:].unsqueeze(1).broadcast_to([128, 16, 128]),
                                          DB[:].unsqueeze(2).broadcast_to([128, 16, 128]), ALU.mult),
         reads=(cb, vb), writes=(cb,))
    for g, w in enumerate(POOLW):
        pool_op(lambda e, g=g: e.iota(INVC[:, g, :], pattern=[[1, 16]], base=1, channel_multiplier=0, allow_small_or_imprecise_dtypes=True))
    for g, w in enumerate(POOLW):
        k.op("dve", lambda e, g=g, w=w: e.tensor_scalar(INVC[:, g, :], INVC[:, g, :], float(w), None, ALU.min),
             reads=(cb,), writes=(cb,))
    k.op("dve", lambda e: e.reciprocal(INVC[:], INVC[:]), reads=(cb,), writes=(cb,))
    k.op("act", lambda e: e.activation(HV[:, 3:4], HV[:, 1:2], AF.Exp), reads=(vb,), writes=(vb,))
    k.op("dve", lambda e: e.tensor_scalar(HV[:, 3:4], HV[:, 3:4], -1.0, None, ALU.mult), reads=(vb,), writes=(vb,))

    def wview_kc(w_dram, c0, ncols):
        return w_dram.rearrange("(kc p) n -> p kc n", p=128)[:, :, c0:c0 + ncols]

    def rmsnorm_tile(ti, src, src_bufs, gcol, dst_fn, dst_bufs, SQ, RS, sqb, rsb, post=None):
        t0, n = TT[ti]
        s = ti % 2
        k.op("act", lambda e: e.activation(SQ[s][:, :, 0:n], src, AF.Square), reads=src_bufs, writes=(sqb[s],))
        bi = next_bank()
        for kc in range(KC):
            k.op("pe", lambda e, kc=kc: e.matmul(PS[bi][:, 0:n], ONESB[:], SQ[s][:, kc, 0:n], start=(kc == 0), stop=(kc == KC - 1)),
                 reads=(sqb[s], cb), writes=(PB[bi],), track=(kc == KC - 1))
        k.op("dve", lambda e: e.tensor_scalar(RS[s][:, 0:n], PS[bi][:, 0:n], 1.0 / D, EPS, ALU.mult, ALU.add),
             reads=(PB[bi],), writes=(rsb[s],))
        k.op("act", lambda e: e.activation(RS[s][:, 0:n], RS[s][:, 0:n], AF.Ln), reads=(rsb[s],), writes=(rsb[s],))
        k.op("act", lambda e: e.activation(RS[s][:, 0:n], RS[s][:, 0:n], AF.Exp, scale=-0.5), reads=(rsb[s],), writes=(rsb[s],))
        for kc in range(KC):
            db = dst_bufs(kc) if callable(dst_bufs) else dst_bufs
            k.op("dve", lambda e, kc=kc: e.scalar_tensor_tensor(dst_fn(kc), src[:, kc, :], VEC[:, gcol + kc:gcol + kc + 1],
                                                                RS[s][:, 0:n], ALU.mult, ALU.mult),
                 reads=tuple(src_bufs) + (rsb[s], vb), writes=db)
            if post is not None:
                post(kc)

    def load_w(slot_ap, src_ap, ds, slot_buf):
        return k.dma("pool", slot_ap, src_ap, ds, writes=(slot_buf,))

    r2.reset()
    XS = [r2.alloc([128, KC, 512], F32) for _ in range(2)]
    SQ = [r2.alloc([128, KC, 512], BF16) for _ in range(2)]
    RS = [r2.alloc([128, 512], F32)] * 2
    xsb, sqb, rsb = k.bufl(2, "xs"), k.bufl(2, "sq"), [k.buf("rs")] * 2
    assert r2.off <= 51200
    r2t = Region(R2[:, 25600:33792], 16384)
    WSZ = [r2t.alloc([128, KC, 512], BF16) for _ in range(2)]
    wszb = k.bufl(2, "wsz")
    load_w(WSZ[0][:], wview_kc(w_in, 1024, 512), WS_DS[0], wszb[0])
    load_w(WSZ[1][:], wview_kc(w_in, 1536, 512), WS_DS[1], wszb[1])
    wdtb = k.buf("wdt")
    load_w(WDT[:], wview_kc(w_in, 3584, 16), ld_ds[10], wdtb)
    r3.reset()
    SZT = r3.alloc([128, NB, 1024], BF16)
    XBC = r3.alloc([128, 12, NT], BF16)
    sztb = k.bufl(NB, "szt")
    xbcb = k.bufl(12, "xbc")
    wxv = lambda c0: XBC[:, c0:c0 + 2, :].rearrange("p a b -> p (a b)")[:, 0:4096].rearrange("p (a b) -> p a b", a=KC)
    WX = {2: wxv(0), 1: wxv(2)}
    wxb = {2: k.buf("wx2"), 1: k.buf("wx1")}
    wx_ds = {2: k.dsem("wx2"), 1: k.dsem("wx1")}
    for grp in (2, 1):
        load_w(WX[grp][:], wview_kc(w_in, 2048 + 512 * grp, 512), wx_ds[grp], wxb[grp])
    WS = [WSZ[0], WSZ[1], None]
    wsb = [wszb[0], wszb[1], None]
    dtv_ = lambda c0: XBC[0:16, c0:c0 + 2, :].rearrange("p a b -> p (a b)")[:, 0:4096].bitcast(F32).rearrange("p (j t) -> p j t", j=4)
    DTT = [dtv_(4), dtv_(6)]
    dttb = k.bufl(2, "dtt")
    tokb = k.bufl(NB, "tok")
    dtasb = k.buf("dtas")
    def z_block(blk):
        m = 128 if blk < 16 else 64
        c0 = blk * 128
        for half in range(2):
            bi = next_bank()
            for kc in range(KC):
                k.op("pe", lambda e, kc=kc: e.matmul(PS[bi][0:m, :], XN[:, kc, c0:c0 + m], WS[half][:, kc, :],
                                                     start=(kc == 0), stop=(kc == KC - 1)),
                     reads=(XNb[min(blk // 4, 4)], wsb[half]), writes=(PB[bi],), track=(kc == KC - 1))
            k.op("act", lambda e: e.activation(SZT[0:m, blk, half * 512:(half + 1) * 512], PS[bi][0:m, :], AF.Silu),
                 reads=(PB[bi],), writes=(sztb[blk],))
    def dt_a(ti):
        t0, n = TT[ti]
        q = ti % 2
        bi = next_bank()
        for kc in range(KC):
            k.op("pe", lambda e, kc=kc: e.matmul(PS[bi][0:16, 0:n], WDT[:, kc, :], XN[:, kc, t0:t0 + n],
                                                 start=(kc == 0), stop=(kc == KC - 1)),
                 reads=(XNb[ti], wdtb), writes=(PB[bi],), track=(kc == KC - 1))
        raw, tmp, dtv, dta = (DTT[q][:, j, 0:n] for j in range(4))
        k.op("dve", lambda e: e.tensor_scalar(raw, PS[bi][0:16, 0:n], HV[:, 0:1], None, ALU.add), reads=(PB[bi], vb), writes=(dttb[q],))
        k.op("act", lambda e: e.activation(tmp, raw, AF.Abs), reads=(dttb[q],), writes=(dttb[q],))
        k.op("act", lambda e: e.activation(tmp, tmp, AF.Exp, scale=-1.0), reads=(dttb[q],), writes=(dttb[q],))
        k.op("act", lambda e: e.activation(tmp, tmp, AF.Ln, bias=1.0), reads=(dttb[q],), writes=(dttb[q],))
        k.op("dve", lambda e: e.scalar_tensor_tensor(dtv, raw, 0.0, tmp, ALU.max, ALU.add), reads=(dttb[q],), writes=(dttb[q],))
        k.op("dve", lambda e: e.tensor_scalar(dta, dtv, HV[:, 3:4], None, ALU.mult), reads=(dttb[q], vb), writes=(dttb[q],))
        if ti == 4:
            k.op("dve", lambda e: e.tensor_copy(DTAS[:], dta), reads=(dttb[q],), writes=(dtasb,))

    def dt_b(ti):
        t0, n = TT[ti]
        q = ti % 2
        raw, tmp, dtv, dta = (DTT[q][:, j, 0:n] for j in range(4))
        nblk = (n + 127) // 128
        bj = next_bank()
        for b4 in range(nblk):
            m = min(128, n - b4 * 128)
            for j, srcv in enumerate((dtv, dta)):
                k.op("pe", lambda e, j=j, srcv=srcv: e.transpose(PS[bj][0:m, (b4 * 2 + j) * 16:(b4 * 2 + j + 1) * 16],
                                                                 srcv[:, b4 * 128:b4 * 128 + m], IDF[0:16, 0:16]),
                     reads=(dttb[q], cb), writes=(PB[bj],), track=(b4 == nblk - 1 and j == 1))
        for b4 in range(nblk):
            m = min(128, n - b4 * 128)
            blk = ti * 4 + b4
            k.op("act", lambda e: e.copy(TOK[0:m, blk, 0:2, :], PS[bj][0:m, b4 * 32:(b4 + 1) * 32].rearrange("p (j h) -> p j h", j=2)),
                 reads=(PB[bj],), writes=(tokb[blk],))

    xTv = xT.rearrange("(c p) t -> p c t", p=128)

    def norm0(ti):
        t0, n = TT[ti]
        s_ = ti % 2
        k.dma("sp", XS[s_][:, :, 0:n], xTv[:, :, t0:t0 + n], ld_ds[3 + s_], writes=(xsb[s_],))
        rmsnorm_tile(ti, XS[s_][:, :, 0:n], (xsb[s_],), V_GMIX, lambda kc: XN[:, kc, t0:t0 + n], (XNb[ti],),
                     SQ, RS, sqb, rsb)

    def zdt(ti):
        dt_a(ti)
        for blk in ([16] if ti == 4 else range(4 * ti, 4 * ti + 4)):
            z_block(blk)
        dt_b(ti)

    norm0(0)
    norm0(1)
    zdt(0)
    norm0(2)
    zdt(1)
    norm0(3)
    zdt(2)
    norm0(4)
    zdt(3)
    zdt(4)
    dump("xn", XN[:], XNb, BF16)
    k.barrier()

    r2.reset()
    WS[2] = r2.alloc([128, KC, 512], BF16)
    wsb[2] = k.buf("ws2")
    XB = [r2.alloc([128, NP + 3], F32) for _ in range(2)]
    XBS = [r2.alloc([128, 7, 16], F32) for _ in range(2)]
    TC = [r2.alloc([128, NP], F32) for _ in range(2)]
    TCS = [r2.alloc([128, 4, 16], F32) for _ in range(2)]
    SCV = r2.alloc([128, 12, 48], F32)
    xbb, tcb = k.bufl(2, "xb"), k.bufl(2, "tc")
    scvb = k.buf("scv")
    k.dma("sp", SCV[:], sconv.rearrange("p (c f) -> p c f", c=12), ld_ds[5], writes=(scvb,))
    load_w(WS[2][:], wview_kc(w_in, 2048, 512), WS_DS[2], wsb[2])
    assert r2.off <= 51200
    pending_silu = []

    def flush_silu():
        while pending_silu:
            ch_, q_ = pending_silu.pop(0)
            park = (wxb[2],) if ch_ in (0, 1) else ((wxb[1],) if ch_ in (2, 3) else ())
            k.op("act", lambda e: e.activation(XBC[:, ch_, 0:NP], TC[q_][:], AF.Silu),
                 reads=(tcb[q_],), writes=(xbcb[ch_],) + park)
            k.op("act", lambda e: e.activation(XBC[:, ch_, NP:NT].rearrange("p (b l) -> p l b", l=4), TCS[q_][:], AF.Silu),
                 reads=(tcb[q_],), writes=(xbcb[ch_],) + park)

    for grp in (2, 1, 0):
        WG_, wgb_ = (WS[2], wsb[2]) if grp == 0 else (WX[grp], wxb[grp])
        for c4 in range(4):
            ch = grp * 4 + c4
            q = ch % 2
            k.op("pool", lambda e: e.memset(XB[q][:, 0:3], 0.0), writes=(xbb[q],))
            k.op("pool", lambda e: e.tensor_copy(XBS[q][:, 0:3, :], SCV[:, ch, :].rearrange("p (j b) -> p j b", j=3)),
                 reads=(scvb,), writes=(xbb[q],))
            for ti, (t0, n) in enumerate(TT):
                bi = next_bank()
                for kc in range(KC):
                    k.op("pe", lambda e, kc=kc: e.matmul(PS[bi][:, 0:n], WG_[:, kc, c4 * 128:(c4 + 1) * 128], XN[:, kc, t0:t0 + n],
                                                         start=(kc == 0), stop=(kc == KC - 1)),
                         reads=(XNb[ti], wgb_), writes=(PB[bi],), track=(kc == KC - 1))
                if ti < 4:
                    k.op("act", lambda e: e.copy(XB[q][:, 3 + t0:3 + t0 + n], PS[bi][:, 0:n]), reads=(PB[bi],), writes=(xbb[q],))
                else:
                    k.op("act", lambda e: e.copy(XBS[q][:, 3:7, :], PS[bi][:, 0:64].rearrange("p (b l) -> p l b", l=4)),
                         reads=(PB[bi],), writes=(xbb[q],))
            k.dma("sp", nconv_p[:, ch, :], XB[q][:, NP:NP + 3], next_out_ds(), reads=(xbb[q],), is_out=True)
            k.dma("sp", nconv_s[:, ch, :], XBS[q][:, 4:7, :].rearrange("p j b -> p (j b)"), next_out_ds(), reads=(xbb[q],), is_out=True)
            cw = lambda j: VEC[:, V_CW + j * 12 + ch:V_CW + j * 12 + ch + 1]
            bcol = VEC[:, V_CB + ch:V_CB + ch + 1]
            k.op("act", lambda e: e.activation(TC[q][:], XB[q][:, 0:NP], AF.Identity, bias=bcol, scale=cw(0)),
                 reads=(xbb[q], vb), writes=(tcb[q],))
            k.op("act", lambda e: e.activation(TCS[q][:], XBS[q][:, 0:4, :], AF.Identity, bias=bcol, scale=cw(0)),
                 reads=(xbb[q], vb), writes=(tcb[q],))
            flush_silu()
            for j in range(1, 4):
                k.op("dve", lambda e, j=j: e.scalar_tensor_tensor(TC[q][:], XB[q][:, j:j + NP], cw(j), TC[q][:], ALU.mult, ALU.add),
                     reads=(xbb[q], vb, tcb[q]), writes=(tcb[q],))
                k.op("dve", lambda e, j=j: e.scalar_tensor_tensor(TCS[q][:], XBS[q][:, j:j + 4, :], cw(j), TCS[q][:], ALU.mult, ALU.add),
                     reads=(xbb[q], vb, tcb[q]), writes=(tcb[q],))
            pending_silu.append((ch, q))
    flush_silu()
    dump("szt", SZT[:], sztb, BF16)
    dump("xbc", XBC[:], xbcb, BF16)
    dump("tok01", TOK[:], tokb)
    k.barrier()

    r2h.reset()
    TMPA = r2h.alloc([128, NB, 16], F32)
    TMPB = r2h.alloc([128, NB, 16], F32)
    EXPM = r2h.alloc([16, 8, 128], F32)
    CDF = r2h.alloc([16, 16], F32)
    tmpb_ = k.bufl(2, "tmpab")
    expb, cdfb, cdxb = k.buf("expm"), k.buf("cdf"), k.buf("cdx")
    ACUM_ps = PS[0][:, 0:NB * 16].rearrange("p (b h) -> p b h", h=16)
    ATOT_ps = PS[1][:, 0:NB * 16].rearrange("p (b h) -> p b h", h=16)
    for blk in range(NB):
        m = 128 if blk < 16 else 64
        tri = TRIF[:] if blk < 16 else TRIFS[:]
        one = ONESF[:] if blk < 16 else BDFS[:]
        k.op("pe", lambda e: e.matmul(ACUM_ps[0:m, blk, :], tri, TOK[0:m, blk, 1, :], start=True, stop=True),
             reads=(tokb[blk], cb), writes=(PB[0],), track=False)
        k.op("pe", lambda e: e.matmul(ATOT_ps[0:m, blk, :], one, TOK[0:m, blk, 1, :], start=True, stop=True),
             reads=(tokb[blk], cb), writes=(PB[1],), track=(blk == NB - 1))
    for (p0, p1, b0, b1) in ((0, 128, 0, 16), (0, 64, 16, 17)):
        tb = tokb[b0:b1]
        k.op("act", lambda e: e.activation(TOK[p0:p1, b0:b1, 2, :], ACUM_ps[p0:p1, b0:b1, :], AF.Exp), reads=(PB[0],), writes=tb)
        k.op("act", lambda e: e.activation(TOK[p0:p1, b0:b1, 3, :], ATOT_ps[p0:p1, b0:b1, :], AF.Exp), reads=(PB[1],), writes=tb)
        k.op("act", lambda e: e.copy(TMPA[p0:p1, b0:b1, :], ACUM_ps[p0:p1, b0:b1, :]), reads=(PB[0],), writes=(tmpb_[0],))
        k.op("dve", lambda e: e.tensor_tensor(TMPB[p0:p1, b0:b1, :], ATOT_ps[p0:p1, b0:b1, :], TMPA[p0:p1, b0:b1, :], ALU.subtract),
             reads=(PB[1], tmpb_[0]), writes=(tmpb_[1],))
        k.op("act", lambda e: e.activation(TMPB[p0:p1, b0:b1, :], TMPB[p0:p1, b0:b1, :], AF.Exp), reads=(tmpb_[1],), writes=(tmpb_[1],))
        k.op("dve", lambda e: e.tensor_tensor(TOK[p0:p1, b0:b1, 4, :], TMPB[p0:p1, b0:b1, :], TOK[p0:p1, b0:b1, 0, :], ALU.mult),
             reads=[tmpb_[1]] + tb, writes=tb)
        k.op("dve", lambda e: e.tensor_copy(AHL[p0:p1, b0:b1, 0, :], TOK[p0:p1, b0:b1, 1, :]), reads=tb, writes=tb)
        k.op("dve", lambda e: e.tensor_copy(TMPA[p0:p1, b0:b1, :], AHL[p0:p1, b0:b1, 0, :]), reads=tb + [tmpb_[0]], writes=(tmpb_[0],))
        k.op("dve", lambda e: e.tensor_tensor(AHL[p0:p1, b0:b1, 1, :], TOK[p0:p1, b0:b1, 1, :], TMPA[p0:p1, b0:b1, :], ALU.subtract),
             reads=tb + [tmpb_[0]], writes=tb)
    k.op("dve", lambda e: e.tensor_reduce(out=CDF[:], in_=DTAS[:].rearrange("p (b l) -> p b l", l=4), axis=AX.X, op=ALU.add),
         reads=(dtasb,), writes=(cdfb,))
    k.op("act", lambda e: e.activation(CDF[:], CDF[:], AF.Exp), reads=(cdfb,), writes=(cdfb,))
    k.op("pool", lambda e: e.memset(EXPM[:], 1.0), writes=(expb,))
    expv = EXPM[:].rearrange("p j (a d) -> p j a d", a=2)
    k.op("pool", lambda e: e.affine_select(out=expv, in_=expv, pattern=[[-2, 8], [-1, 2], [0, 64]], compare_op=ALU.is_equal,
                                           fill=0.0, base=0, channel_multiplier=1), reads=(expb,), writes=(expb,))
    for j in range(8):
        k.op("pe", lambda e, j=j: e.matmul(PS[2][:, j * 16:(j + 1) * 16], EXPM[:, j, :], CDF[:], start=True, stop=True),
             reads=(expb, cdfb), writes=(PB[2],), track=(j == 7))
    k.op("act", lambda e: e.copy(CDX[:], PS[2][:, 0:128].rearrange("p (j b) -> p j b", j=8)), reads=(PB[2],), writes=(cdxb,))
    dump("tok", TOK[:], tokb)
    k.barrier()

    xtokb, xwb, btokb, cbmb, mmb, yab_g, ynb_g, sdb, sstb, hpb, smb, xdtb = (k.buf(n) for n in
        ("xtok", "xw", "btok", "cbm", "mm", "ya", "yn", "sd", "sst", "hp", "small", "xdt"))
    rhb, decb = k.bufl(4, "rh"), k.bufl(4, "dec")
    mmq = k.bufl(4, "mmq")
    ysb = k.bufl(NB, "ys")
    XT_ps = PS[0][:].bitcast(BF16)
    BT_ps = PS[1][:, 0:128].bitcast(BF16)
    CB_ps = PS[1][:, 256:512].rearrange("p (g l) -> p g l", g=2)
    bc16 = lambda ap, m: ap.unsqueeze(2).broadcast_to([m, 16, 64])
    v3 = lambda ap: ap.rearrange("p (h d) -> p h d", h=16)

    def ssd_alloc(m):
        r2h.reset()
        T = {}
        T["XTOK"] = r2h.alloc([128, 1024], BF16)
        T["XW"] = r2h.alloc([128, 1024], BF16)
        T["XDT"] = r2h.alloc([128, 1024], BF16)
        T["BTOK"] = r2h.alloc([128, 256], BF16)
        T["CBM"] = r2h.alloc([128, 2, m], F32)
        T["RH"] = [r2h.alloc([128, 4, m], BF16) for _ in range(4)]
        T["DEC"] = [r2h.alloc([128, 4, m], F32) for _ in range(4)]
        T["MM"] = r2h.alloc([128, 16, m], BF16)
        return T

    def ssd_A1(blk, T):
        m = 128 if blk < 16 else 64
        t0 = blk * 128
        trib = TRIB[:] if blk < 16 else TRIBS[:]
        maskf = TRIF[:] if blk < 16 else TRIFS[:]
        XTOK, XW, BTOK, CBM, RH = (T[n] for n in ("XTOK", "XW", "BTOK", "CBM", "RH"))
        for j in range(8):
            k.op("pe", lambda e, j=j: e.transpose(XT_ps[0:m, j * 128:(j + 1) * 128], XBC[:, j, t0:t0 + m], IDB[:]),
                 reads=(xbcb[j], cb), writes=(PB[0],), track=(j == 7))
        for g in range(2):
            k.op("pe", lambda e, g=g: e.transpose(BT_ps[0:m, g * 128:(g + 1) * 128], XBC[:, 8 + g, t0:t0 + m], IDB[:]),
                 reads=(xbcb[8 + g], cb), writes=(PB[1],), track=False)
        for g in range(2):
            k.op("pe", lambda e, g=g: e.matmul(CB_ps[0:m, g, 0:m], XBC[:, 8 + g, t0:t0 + m], XBC[:, 10 + g, t0:t0 + m], start=True, stop=True),
                 reads=(xbcb[8 + g], xbcb[10 + g]), writes=(PB[1],), track=(g == 1))
        for q in range(4):
            k.op("pool", lambda e, q=q: e.tensor_tensor(RH[q][0:m, :, :], trib[0:m, 0:m].unsqueeze(1).broadcast_to([m, 4, m]),
                                                        AHL[0:m, blk, 0, 4 * q:4 * q + 4].unsqueeze(2).broadcast_to([m, 4, m]), ALU.mult),
                 reads=(cb, tokb[blk]), writes=(rhb[q],))
        k.op("act", lambda e: e.copy(XTOK[0:m, :], XT_ps[0:m, :]), reads=(PB[0],), writes=(xtokb,))
        k.op("act", lambda e: e.copy(BTOK[0:m, :], BT_ps[0:m, :]), reads=(PB[1],), writes=(btokb,))
        k.op("dve", lambda e: e.tensor_tensor(CBM[0:m, :, :], CB_ps[0:m, :, 0:m], maskf[0:m, 0:m].unsqueeze(1).broadcast_to([m, 2, m]), ALU.mult),
             reads=(PB[1], cb), writes=(cbmb,))
        k.op("dve", lambda e: e.tensor_tensor(v3(T["XDT"][0:m, :]), v3(XT_ps[0:m, :]), bc16(TOK[0:m, blk, 0, :], m), ALU.mult),
             reads=(PB[0], tokb[blk]), writes=(xdtb,))
        k.op("pool", lambda e: e.tensor_tensor(v3(XW[0:m, :]), v3(XTOK[0:m, :]), bc16(TOK[0:m, blk, 4, :], m), ALU.mult),
             reads=(xtokb, tokb[blk]), writes=(xwb,))

    def ssd_A2(blk, T, part):
        m = 128 if blk < 16 else 64
        ub = UB[:] if blk < 16 else UBS[:]
        CBM, RH, DEC, MM = (T[n] for n in ("CBM", "RH", "DEC", "MM"))
        for q in range(4):
            bq = 2 + q % 2
            segv = PS[bq][:, 0:4 * m].rearrange("p (h l) -> p h l", h=4)
            if part == 0:
                k.op("pe", lambda e: e.matmul(PS[bq][0:m, 0:4 * m], ub[0:m, 0:m], RH[q][0:m, :, :].rearrange("p h l -> p (h l)"), start=True, stop=True),
                     reads=(cb, rhb[q]), writes=(PB[bq],))
                k.op("act", lambda e: e.activation(DEC[q][0:m, :, :], segv[0:m, :, :], AF.Exp), reads=(PB[bq],), writes=(decb[q],))
            else:
                g = q // 2
                k.op("dve", lambda e: e.tensor_tensor(MM[0:m, 4 * q:4 * q + 4, :], DEC[q][0:m, :, :],
                                                      CBM[0:m, g, :].unsqueeze(1).broadcast_to([m, 4, m]), ALU.mult),
                     reads=(decb[q], cbmb), writes=(mmq[q],))

    def ssd_Y(blk, T):
        m = 128 if blk < 16 else 64
        XTOK, MM, XDT = T["XTOK"], T["MM"], T["XDT"]
        for h in range(16):
            by = 4 + h // 8
            hc = (h % 8) * 64
            k.op("pe", lambda e: e.matmul(PS[by][0:m, hc:hc + 64], MM[0:m, h, :], XDT[0:m, h * 64:(h + 1) * 64],
                                          start=(h % 8 == 0), stop=False, skip_group_check=True),
                 reads=(mmq[h // 4], xdtb), writes=(PB[by],), track=False)
            k.op("pe", lambda e: e.matmul(PS[by][0:m, hc:hc + 64], DI[0:m, h, 0:m], XTOK[0:m, h * 64:(h + 1) * 64],
                                          start=False, stop=True, skip_group_check=True),
                 reads=(cb, xtokb), writes=(PB[by],), track=(h % 8 == 7))

    def ssd_P(blk, have_off, YA, YN, part, yab=None, ynb=None):
        yab = yab if yab is not None else yab_g
        ynb = ynb if ynb is not None else ynb_g
        m = 128 if blk < 16 else 64
        t0 = blk * 128
        YT_ps = PS[0][:].bitcast(BF16).rearrange("p (j t) -> p j t", j=8)
        if part == 1:
            k.op("act", lambda e: e.copy(YMIX[:, 8:16, t0:t0 + m], YT_ps[:, :, 0:m]), reads=(PB[0],), writes=(ysb[blk],))
            return
        for g in range(2) if part in (0, "a") else ():
            cs = slice(g * 512, (g + 1) * 512)
            if have_off:
                k.op("dve", lambda e: e.tensor_tensor(YA[0:m, cs].rearrange("p (h d) -> p h d", h=8), PS[6 + g][0:m, :].rearrange("p (h d) -> p h d", h=8),
                                                      TOK[0:m, blk, 2, 8 * g:8 * g + 8].unsqueeze(2).broadcast_to([m, 8, 64]), ALU.mult),
                     reads=(PB[6 + g], tokb[blk]), writes=(yab,))
                k.op("dve", lambda e: e.tensor_tensor(YA[0:m, cs], YA[0:m, cs], PS[4 + g][0:m, :], ALU.add),
                     reads=(PB[4 + g], yab), writes=(yab,))
                k.op("dve", lambda e: e.tensor_tensor(YA[0:m, cs], YA[0:m, cs], SZT[0:m, blk, cs], ALU.mult),
                     reads=(yab, sztb[blk]), writes=(yab,))
            else:
                k.op("dve", lambda e: e.tensor_tensor(YA[0:m, cs], PS[4 + g][0:m, :], SZT[0:m, blk, cs], ALU.mult),
                     reads=(PB[4 + g], sztb[blk]), writes=(yab,))
        if part == "a":
            return
        k.op("act", lambda e: e.activation(YN[0:m, :], YA[0:m, :], AF.Square, accum_out=SMALL[0:m, 0:1]), reads=(yab,), writes=(ynb, smb))
        k.op("dve", lambda e: e.tensor_scalar(SMALL[0:m, 1:2], SMALL[0:m, 0:1], 1.0 / 1024, EPS, ALU.mult, ALU.add), reads=(smb,), writes=(smb,))
        k.op("act", lambda e: e.activation(SMALL[0:m, 2:3], SMALL[0:m, 1:2], AF.Ln), reads=(smb,), writes=(smb,))
        k.op("act", lambda e: e.activation(SMALL[0:m, 3:4], SMALL[0:m, 2:3], AF.Exp, scale=-0.5), reads=(smb,), writes=(smb,))
        k.op("act", lambda e: e.activation(YN[0:m, :], YA[0:m, :], AF.Copy, scale=SMALL[0:m, 3:4]), reads=(yab, smb), writes=(ynb,))
        for j in range(8):
            k.op("pe", lambda e, j=j: e.transpose(YT_ps[:, j, 0:m], YN[0:m, j * 128:(j + 1) * 128], IDB[0:m, 0:m]),
                 reads=(ynb, cb), writes=(PB[0],), track=(j == 7))

    T = ssd_alloc(128)
    YA = r2h.alloc([128, 1024], F32)
    YN = r2h.alloc([128, 1024], BF16)

    def ssd_S(blk, part):
        if part == 0:
            for g in range(2):
                k.op("pe", lambda e, g=g: e.matmul(PS[2 + g][:, :], T["BTOK"][:, g * 128:(g + 1) * 128], T["XW"][:, g * 512:(g + 1) * 512], start=True, stop=True),
                     reads=(btokb, xwb), writes=(PB[2 + g],))
            return
        if blk == 0:
            for g in range(2):
                k.op("act", lambda e, g=g: e.copy(SST[:, g * 512:(g + 1) * 512], PS[2 + g][:, :]), reads=(PB[2 + g],), writes=(sstb,))
        else:
            k.op("dve", lambda e: e.tensor_tensor(v3(SST[:, :]), v3(SST[:, :]), bc16(TOK[:, blk, 3, :], 128), ALU.mult),
                 reads=(sstb, tokb[blk]), writes=(sstb,))
            for g in range(2):
                k.op("dve", lambda e, g=g: e.tensor_tensor(SST[:, g * 512:(g + 1) * 512], SST[:, g * 512:(g + 1) * 512], PS[2 + g][:, :], ALU.add),
                     reads=(sstb, PB[2 + g]), writes=(sstb,))
        if blk < 15:
            k.op("act", lambda e: e.copy(HP[:, :], SST[:, :]), reads=(sstb,), writes=(hpb,))

    ssd_A1(0, T)
    ssd_A2(0, T, 0)
    ssd_A2(0, T, 1)
    for blk in range(16):
        t0 = blk * 128
        if blk > 0:
            for g in range(2):
                k.op("pe", lambda e, g=g: e.matmul(PS[6 + g][:, :], XBC[:, 10 + g, t0:t0 + 128], HP[:, g * 512:(g + 1) * 512], start=True, stop=True),
                     reads=(xbcb[10 + g], hpb), writes=(PB[6 + g],))
        ssd_S(blk, 0)
        ssd_Y(blk, T)
        if blk > 0:
            ssd_P(blk - 1, blk > 1, YA, YN, "b")
            ssd_P(blk - 1, blk > 1, YA, YN, 1)
        ssd_S(blk, 1)
        ssd_P(blk, blk > 0, YA, YN, "a")
        if blk < 15:
            ssd_A1(blk + 1, T)
            ssd_A2(blk + 1, T, 0)
            ssd_A2(blk + 1, T, 1)
    ssd_P(15, True, YA, YN, "b")
    ssd_P(15, True, YA, YN, 1)
    k.dma("sp", nssm_p, SST[:, :], next_out_ds(), reads=(sstb,), is_out=True)
    k.barrier()

    blk = 16
    t0 = NP
    T = ssd_alloc(64)
    CZb = [r2h.alloc([128, 2, 64], BF16) for _ in range(2)]
    BZb = [r2h.alloc([64, 256], BF16) for _ in range(2)]
    xbf = lambda j: XBC[:, j, 0:2048].bitcast(F32).rearrange("p (j n) -> p j n", j=8)
    H0 = [xbf(j) for j in range(6)] + [SST[:, :].rearrange("p (j n) -> p j n", j=8)]
    H0B = [r2h.alloc([128, 8, 128], BF16), HP[:, :].rearrange("p (j n) -> p j n", j=8)]
    H0T = [r2h.alloc([128, 1024], BF16) for _ in range(2)]
    szf = lambda i: SZT[:, 2 * i:2 * i + 2, :].rearrange("p a b -> p (a b)").bitcast(F32).rearrange("p (j n) -> p j n", j=8)
    OST = [szf(5), szf(6), szf(7)]
    NH = len(H0)
    NO = len(OST)
    czb, bzb, h0b, h0bb, h0tb = k.bufl(2, "cz"), k.bufl(2, "bz"), k.bufl(6, "h0") + [sstb], [k.buf("h0b"), hpb], k.bufl(2, "h0t")
    ostb = k.bufl(NO, "ost")
    h0_ds = [k.dsem("h0") for _ in range(NH)]
    ost_ds = [k.dsem("ost") for _ in range(NO)]
    WSU = [R3[:, 4096 * i:4096 * (i + 1)].rearrange("p (a b) -> p a b", a=KC) for i in range(2)]
    WPOOL = R3[:, 8192:10240].rearrange("p (g c e) -> p g c e", g=4, c=2)
    wsub, wpb = k.bufl(2, "wsu"), k.buf("wpool")
    load_w(WSU[0][:], wview_kc(w_in, 0, 512), WS_DS[0], wsub[0])
    load_w(WSU[1][:], wview_kc(w_in, 512, 512), WS_DS[1], wsub[1])
    k.dma("pool", WPOOL[:].rearrange("p g c e -> p (g c) e"), w_pool.rearrange("g (c p) e -> p (g c) e", p=128), ld_ds[7], writes=(wpb,))
    h0src = lambda b: sssm[b].rearrange("(j a) p n -> (a p) j n", a=2)
    for b in range(NH):
        k.dma("sp", H0[b][:], h0src(b), h0_ds[b], writes=(h0b[b],))
    ssd_A1(blk, T)
    ssd_A2(blk, T, 0)
    ssd_A2(blk, T, 1)
    HT_pss = [PS[2][:].bitcast(BF16), PS[3][:].bitcast(BF16)]
    for b in range(16):
        s = b % 2
        s3 = b % NH
        so = b % NO
        k.op("act", lambda e: e.copy(H0B[s][:], H0[s3][:]), reads=(h0b[s3],), writes=(h0bb[s],))
        HT_ps = HT_pss[s]
        for j in range(8):
            k.op("pe", lambda e, j=j: e.transpose(HT_ps[:, j * 128:(j + 1) * 128], H0B[s][:, j, :], IDB[:]),
                 reads=(h0bb[s], cb), writes=(PB[2 + s],), track=(j == 7))
        k.op("act", lambda e: e.copy(H0T[s][:, :], HT_ps[:, :]), reads=(PB[2 + s],), writes=(h0tb[s],))
        k.op("pool", lambda e: e.tensor_tensor(CZb[s][:], XBC[:, 10:12, t0:t0 + 64], BDROW[:, b, :].unsqueeze(1).broadcast_to([128, 2, 64]), ALU.mult),
             reads=(xbcb[10], xbcb[11], cb), writes=(czb[s],))
        k.op("pool", lambda e: e.tensor_tensor(BZb[s][:], T["BTOK"][0:64, :], BDCOL[:, b:b + 1].broadcast_to([64, 256]), ALU.mult),
             reads=(btokb, cb), writes=(bzb[s],))
        for g in range(2):
            k.op("pe", lambda e, g=g: e.matmul(PS[6 + g][0:64, :], CZb[s][:, g, :], H0T[s][:, g * 512:(g + 1) * 512], start=(b == 0), stop=(b == 15)),
                 reads=(czb[s], h0tb[s]), writes=(PB[6 + g],), track=(b == 15 or g == 1))
        cb0 = 4 if s == 0 else 0
        csv = lambda j: PS[cb0 + j // 4][:, (j % 4) * 128:(j % 4 + 1) * 128]
        for j in range(8):
            k.op("pe", lambda e, j=j: e.matmul(csv(j), T["XW"][0:64, j * 128:(j + 1) * 128], BZb[s][:, (j // 4) * 128:(j // 4 + 1) * 128], start=True, stop=True),
                 reads=(xwb, bzb[s]), writes=(PB[cb0 + j // 4],), track=(j % 4 == 3))
        for j in range(8):
            k.op("dve", lambda e, j=j: e.scalar_tensor_tensor(OST[so][:, j, :], H0[s3][:, j, :], CDX[:, j, b:b + 1], csv(j), ALU.mult, ALU.add),
                 reads=(cdxb, PB[cb0 + j // 4], h0b[s3]), writes=(ostb[so],), track=(j == 7))
        k.dma("sp", nssm_s[b].rearrange("(j a) p n -> (a p) j n", a=2), OST[so][:], ost_ds[so], reads=(ostb[so],), is_out=True)
        if b + NH < 16:
            k.dma("sp", H0[s3][:], h0src(b + NH), h0_ds[s3], writes=(h0b[s3],))
    ssd_Y(blk, T)
    ssd_P(blk, True, SST, HP, 0, sstb, hpb)
    ssd_P(blk, True, SST, HP, 1, sstb, hpb)
    dump("ys", YMIX[:, 8:16, :], ysb, BF16)
    k.barrier()

    r3.reset()
    r3.off = 20480
    WS = WSU
    wsb = wsub
    U = [r3.alloc([128, 16 + NP], F32) for _ in range(2)]
    US = [r3.alloc([128, 19, 16], F32) for _ in range(2)]
    TA = [r3.alloc([128, 16 + NP], F32) for _ in range(2)]
    TAS = [r3.alloc([128, 19, 16], F32) for _ in range(2)]
    DD = [r3.alloc([128, 2, NT], BF16) for _ in range(2)]
    SPL = r3.alloc([128, 8, 240], F32)
    ub_, tab, ddb = k.bufl(2, "u"), k.bufl(2, "ta"), k.bufl(2, "dd")
    splb = k.buf("spl")
    ypb = k.bufl(8, "yp")
    k.dma("sp", SPL[:], spool.rearrange("p (c f) -> p c f", c=8), ld_ds[6], writes=(splb,))
    for q in range(2):
        k.op("pool", lambda e, q=q: e.memset(U[q][:, 0:16], 0.0), writes=(ub_[q],))
    pending_pool = []

    def flush_pool():
        while pending_pool:
            g_, dq_ = pending_pool.pop(0)
            for ec in range(2):
                for ti, (t0, n) in enumerate(TT):
                    bi = next_bank()
                    for k2 in range(2):
                        k.op("pe", lambda e, k2=k2: e.matmul(PS[bi][:, 0:n], WPOOL[:, g_, k2, ec * 128:(ec + 1) * 128], DD[dq_][:, k2, t0:t0 + n],
                                                             start=(k2 == 0), stop=(k2 == 1)),
                             reads=(wpb, ddb[dq_]), writes=(PB[bi],), track=(k2 == 1))
                    yc = 2 * g_ + ec
                    k.op("act", lambda e: e.activation(YMIX[:, yc, t0:t0 + n], PS[bi][:, 0:n], AF.Copy, scale=VEC[:, V_PSC + yc:V_PSC + yc + 1]),
                         reads=(PB[bi], vb), writes=(ypb[yc],))

    for ci, c in enumerate((6, 7, 4, 5, 2, 3, 0, 1)):
        grp, c4 = c // 4, c % 4
        q = c % 2
        g = c // 2
        w = POOLW[g]
        k.op("pool", lambda e: e.tensor_copy(US[q][:, 0:15, :], SPL[:, c, :].rearrange("p (j b) -> p j b", j=15)),
             reads=(splb,), writes=(ub_[q],))
        for ti, (t0, n) in enumerate(TT):
            bi = next_bank()
            for kc in range(KC):
                k.op("pe", lambda e, kc=kc: e.matmul(PS[bi][:, 0:n], WS[grp][:, kc, c4 * 128:(c4 + 1) * 128], XN[:, kc, t0:t0 + n],
                                                     start=(kc == 0), stop=(kc == KC - 1)),
                     reads=(XNb[ti], wsb[grp]), writes=(PB[bi],), track=(kc == KC - 1))
            if ti < 4:
                k.op("act", lambda e: e.copy(U[q][:, 16 + t0:16 + t0 + n], PS[bi][:, 0:n]), reads=(PB[bi],), writes=(ub_[q],))
            else:
                k.op("act", lambda e: e.copy(US[q][:, 15:19, :], PS[bi][:, 0:64].rearrange("p (b l) -> p l b", l=4)),
                     reads=(PB[bi],), writes=(ub_[q],))
        flush_pool()
        if ci == 7:
            XNf = XN[:].rearrange("p c t -> p (c t)")
            WO = [XNf[:, 4096 * i:4096 * (i + 1)].rearrange("p (a b) -> p a b", a=16) for i in range(4)]
            wob = k.bufl(4, "wo")
            wo_ds = [k.dsem("wo") for _ in range(4)]
            for i in range(4):
                k.dma("pool", WO[i][:], w_out.rearrange("(kc p) n -> p kc n", p=128)[:, :, i * 256:(i + 1) * 256], wo_ds[i],
                      writes=[wob[i]] + XNb)
        k.dma("sp", npool_p[:, c, :], U[q][:, 16 + NP - 15:16 + NP], next_out_ds(), reads=(ub_[q],), is_out=True)
        k.dma("sp", npool_s[:, c, :], US[q][:, 4:19, :].rearrange("p j b -> p (j b)"), next_out_ds(), reads=(ub_[q],), is_out=True)
        src, srcs, srcb = U[q], US[q], ub_[q]
        step = 1
        pp = 0
        while step < w:
            dst, dsts, dstb = TA[pp], TAS[pp], tab[pp]
            lo = 2 * step - 1
            k.op("dve", lambda e, src=src, dst=dst, lo=lo, step=step: e.tensor_tensor(dst[:, lo:], src[:, lo:], src[:, lo - step:16 + NP - step], ALU.add),
                 reads=(srcb,), writes=(dstb,))
            k.op("dve", lambda e, srcs=srcs, dsts=dsts, lo=lo, step=step: e.tensor_tensor(dsts[:, lo:, :], srcs[:, lo:, :], srcs[:, lo - step:19 - step, :], ALU.add),
                 reads=(srcb,), writes=(dstb,))
            src, srcs, srcb = dst, dsts, dstb
            pp = 1 - pp
            step *= 2
        dq = g % 2
        cc = c % 2
        k.op("dve", lambda e: e.scalar_tensor_tensor(DD[dq][:, cc, 0:NP], src[:, 16:16 + NP], 1.0 / w, U[q][:, 16:16 + NP], ALU.mult, ALU.subtract),
             reads=(srcb, ub_[q]), writes=(ddb[dq],))
        k.op("dve", lambda e: e.tensor_tensor(SMALL[:, 16:32], src[:, 16:32], INVC[:, g, :], ALU.mult), reads=(srcb, cb), writes=(smb,))
        k.op("dve", lambda e: e.tensor_tensor(DD[dq][:, cc, 0:16], SMALL[:, 16:32], U[q][:, 16:32], ALU.subtract), reads=(smb, ub_[q]), writes=(ddb[dq],))
        k.op("dve", lambda e: e.scalar_tensor_tensor(DD[dq][:, cc, NP:NT].rearrange("p (b l) -> p l b", l=4), srcs[:, 15:19, :], 1.0 / w,
                                                     US[q][:, 15:19, :], ALU.mult, ALU.subtract),
             reads=(srcb, ub_[q]), writes=(ddb[dq],))
        if cc == 1:
            pending_pool.append((g, dq))
    flush_pool()
    dump("yp", YMIX[:, 0:8, :], ypb, BF16)
    k.barrier(keep=list(zip(wo_ds, wob)))

    Hb = [k.bufl(5, "h%d_" % c) for c in range(KC)]
    hld = [k.dsem("hld") for _ in range(KC)]
    for c in range(KC):
        k.dma("sp", H[:, c, :], xT[c * 128:(c + 1) * 128, :], hld[c], reads=(Hb[c - 1][0],) if c else (), writes=Hb[c])
    for i in range(4):
        k.op("dve", lambda e, i=i: e.tensor_tensor(WO[i][:, 8:16, :], WO[i][:, 8:16, :],
                                                   VEC[:, V_GSSM:V_GSSM + 8].unsqueeze(2).broadcast_to([128, 8, 256]), ALU.mult),
             reads=(wob[i], vb), writes=(wob[i],))
    for cg in range(4):
        s = cg
        for oc in range(2):
            c = cg * 2 + oc
            for ti, (t0, n) in enumerate(TT):
                bi = next_bank()
                for kc in range(16):
                    k.op("pe", lambda e, kc=kc: e.matmul(PS[bi][:, 0:n], WO[s][:, kc, oc * 128:(oc + 1) * 128], YMIX[:, kc, t0:t0 + n],
                                                         start=(kc == 0), stop=(kc == 15)),
                         reads=(wob[s],), writes=(PB[bi],), track=(kc == 15))
                k.op("dve", lambda e: e.tensor_tensor(H[:, c, t0:t0 + n], H[:, c, t0:t0 + n], PS[bi][:, 0:n], ALU.add),
                     reads=(PB[bi], Hb[c][ti]), writes=(Hb[c][ti],))
    dump("h1", H[:], [b for l in Hb for b in l])
    k.barrier()

    def norm_from_h(gcol):
        r3x.reset()
        SQ = [r3x.alloc([128, KC, 512], BF16)] * 2
        RS = [r3x.alloc([128, 512], F32) for _ in range(2)]
        sqb, rsb = [k.buf("sq")] * 2, k.bufl(2, "rs")
        def tile(ti):
            t0, n = TT[ti]
            rmsnorm_tile(ti, H[:, :, t0:t0 + n], [Hb[c][ti] for c in range(KC)], gcol,
                         lambda kc: XN[:, kc, t0:t0 + n], (XNb[ti],), SQ, RS, sqb, rsb)
        return [sqb[0]] + rsb, tile

    WS = [R2[:, 16896 + 4096 * i:16896 + 4096 * (i + 1)].rearrange("p (a b) -> p a b", a=KC) for i in range(3)]
    wsb = k.bufl(3, "wsf")
    wi = [0]

    def next_ws(src):
        s = wi[0] % 3
        wi[0] += 1
        load_w(WS[s][:], src, WS_DS[s], wsb[s])
        return s
    w2v = w_ff2.rearrange("(fc p) n -> p fc n", p=128)
    pref = [next_ws(wview_kc(w_ff1, 0, 512)), next_ws(wview_kc(w_ff1, 512, 512)), next_ws(w2v[:, 0:8, 0:512])]
    n2bufs, norm2_tile = norm_from_h(V_GMLP)
    norm2_tile(0)
    norm2_tile(1)
    r2.reset()
    A = r2.alloc([128, KC, NT], BF16)
    r2.off += 3 * 8192
    RT = [r2.alloc([128, 512], F32) for _ in range(2)]
    WPLE = r2.alloc([128, 2, D], BF16)
    wple_off = r2.off
    rtb = k.bufl(2, "rt")
    ab = [k.bufl(5, "a%d_" % c) for c in range(KC)]
    rti = [0]
    for G in range(4):
        for sub in range(2):
            s = pref.pop(0) if pref else next_ws(wview_kc(w_ff1, G * 1024 + sub * 512, 512))
            for c4 in range(4):
                fc = sub * 4 + c4
                for ti, (t0, n) in enumerate(TT):
                    bi = next_bank()
                    for kc in range(KC):
                        k.op("pe", lambda e, kc=kc: e.matmul(PS[bi][:, 0:n], WS[s][:, kc, c4 * 128:(c4 + 1) * 128], XN[:, kc, t0:t0 + n],
                                                             start=(kc == 0), stop=(kc == KC - 1)),
                             reads=(XNb[ti], wsb[s]), writes=(PB[bi],), track=(kc == KC - 1))
                    if (G, sub, c4) == (0, 0, 0) and ti + 2 < 5:
                        norm2_tile(ti + 2)
                    r = rti[0] % 2
                    rti[0] += 1
                    k.op("act", lambda e: e.activation(RT[r][:, 0:n], PS[bi][:, 0:n], AF.Relu), reads=(PB[bi],), writes=(rtb[r],))
                    k.op("dve", lambda e: e.tensor_tensor(A[:, fc, t0:t0 + n], RT[r][:, 0:n], RT[r][:, 0:n], ALU.mult),
                         reads=(rtb[r],), writes=(ab[fc][ti],))
        for half in range(2):
            s = pref.pop(0) if pref else next_ws(w2v[:, G * 8:(G + 1) * 8, half * 512:(half + 1) * 512])
            for oc in range(4):
                c = half * 4 + oc
                for ti, (t0, n) in enumerate(TT):
                    bi = next_bank()
                    for fc in range(KC):
                        k.op("pe", lambda e, fc=fc: e.matmul(PS[bi][:, 0:n], WS[s][:, fc, oc * 128:(oc + 1) * 128], A[:, fc, t0:t0 + n],
                                                             start=(fc == 0), stop=(fc == KC - 1)),
                             reads=(ab[fc][ti], wsb[s]), writes=(PB[bi],), track=(fc == KC - 1))
                    k.op("dve", lambda e: e.tensor_tensor(H[:, c, t0:t0 + n], H[:, c, t0:t0 + n], PS[bi][:, 0:n], ALU.add),
                         reads=(PB[bi], Hb[c][ti]), writes=(Hb[c][ti],))
    r3x.reset()
    WG = [r3x.alloc([128, KC, 512], BF16) for _ in range(2)]
    wgb = k.bufl(2, "wsg")
    wpleb = k.buf("wple")
    wg_ds = [k.dsem("wg") for _ in range(2)]
    wple_ds = k.dsem("wple")
    for half in range(2):
        k.dma("pool", WG[half][:], wview_kc(w_gate, half * 512, 512), wg_ds[half], writes=[wgb[half]] + n2bufs)
    k.dma("pool", WPLE[:], w_ple.rearrange("(c p) n -> p c n", p=128), wple_ds, writes=(wpleb,))
    dump("h2", H[:], [b for l in Hb for b in l])
    k.barrier(keep=[(wg_ds[0], wgb[0]), (wg_ds[1], wgb[1]), (wple_ds, wpleb)])

    r2.reset()
    SQ = [r2.alloc([128, KC, 512], BF16) for _ in range(2)]
    RS = [r2.alloc([128, 512], F32) for _ in range(2)]
    WS = WG
    PT = r2.alloc([128, 2, NT], BF16)
    SG = [r2.alloc([128, 512], F32) for _ in range(2)]
    YOC = [r2.alloc([128, 512], F32) for _ in range(4)]
    assert r2.off <= wple_off - 4096
    sqb, rsb = k.bufl(2, "sq"), k.bufl(2, "rs")
    wsb, sgb, yocb = wgb, k.bufl(2, "sg"), k.bufl(4, "yoc")
    yoc_ds = [k.dsem("yoc") for _ in range(4)]
    ptb = k.buf("pt")
    k.dma("pool", PT[:], pT.rearrange("(c p) t -> p c t", p=128), ld_ds[9], writes=(ptb,))
    gi = [0]

    def n3(ti):
        t0, n = TT[ti]
        rmsnorm_tile(ti, H[:, :, t0:t0 + n], [Hb[c][ti] for c in range(KC)], V_GPLE,
                     lambda kc: XN[:, kc, t0:t0 + n], (XNb[ti],), SQ, RS, sqb, rsb)

    def gate(ti):
        t0, n = TT[ti]
        for c in range(KC):
            half, oc = c // 4, c % 4
            bi = next_bank()
            for kc in range(KC):
                k.op("pe", lambda e, kc=kc: e.matmul(PS[bi][:, 0:n], WS[half][:, kc, oc * 128:(oc + 1) * 128], XN[:, kc, t0:t0 + n],
                                                     start=(kc == 0), stop=(kc == KC - 1)),
                     reads=(XNb[ti], wsb[half]), writes=(PB[bi],), track=(kc == KC - 1))
            bj = next_bank()
            for k2 in range(2):
                k.op("pe", lambda e, k2=k2: e.matmul(PS[bj][:, 0:n], WPLE[:, k2, c * 128:(c + 1) * 128], PT[:, k2, t0:t0 + n],
                                                     start=(k2 == 0), stop=(k2 == 1)),
                     reads=(wpleb, ptb), writes=(PB[bj],), track=(k2 == 1))
            r = gi[0] % 2
            gi[0] += 1
            k.op("act", lambda e: e.activation(SG[r][:, 0:n], PS[bi][:, 0:n], AF.Sigmoid), reads=(PB[bi],), writes=(sgb[r],))
            k.op("dve", lambda e: e.tensor_tensor(SG[r][:, 0:n], SG[r][:, 0:n], PS[bj][:, 0:n], ALU.mult),
                 reads=(PB[bj], sgb[r]), writes=(sgb[r],))
            k.op("dve", lambda e: e.tensor_tensor(H[:, c, t0:t0 + n], H[:, c, t0:t0 + n], SG[r][:, 0:n], ALU.add),
                 reads=(sgb[r], Hb[c][ti]), writes=(Hb[c][ti],))

    def fn(ti):
        t0, n = TT[ti]

        def post(kc):
            k.dma("sp", yT[kc * 128:(kc + 1) * 128, t0:t0 + n], YOC[kc % 4][:, 0:n], yoc_ds[kc % 4], reads=(yocb[kc % 4],), is_out=True)
        rmsnorm_tile(ti, H[:, :, t0:t0 + n], [Hb[c][ti] for c in range(KC)], V_GFIN,
                     lambda kc: YOC[kc % 4][:, 0:n], lambda kc: (yocb[kc % 4],), SQ, RS, sqb, rsb, post=post)

    for ti in range(5):
        n3(ti)
    gate(0)
    gate(1)
    fn(0)
    gate(2)
    fn(1)
    gate(3)
    fn(2)
    gate(4)
    fn(3)
    fn(4)
    k.finish()
    return k, dbg


_CACHE = {}


def _host_inputs(inp, i):
    f = np.float32
    xs = inp["x_sample"][16 * i:16 * i + 16].reshape(64, D)
    xTc = np.ascontiguousarray(np.concatenate([inp["x_prompt"][i], xs], axis=0).T, dtype=f)
    ps = inp["p_sample"][0, 16 * i:16 * i + 16].reshape(64, 256)
    pTc = np.ascontiguousarray(np.concatenate([inp["p_prompt"][0, i], ps], axis=0).T, dtype=f)
    spool = np.ascontiguousarray(inp["state_pool"][0, 16 * i:16 * i + 16].reshape(16, 15, 8, 128).transpose(3, 2, 1, 0).reshape(128, 8 * 15 * 16), dtype=f)
    sconv = np.ascontiguousarray(inp["state_conv"][0, 16 * i:16 * i + 16].reshape(16, 3, 12, 128).transpose(3, 2, 1, 0).reshape(128, 12 * 3 * 16), dtype=f)
    sssm = np.ascontiguousarray(inp["state_ssm"][0, 16 * i:16 * i + 16], dtype=f)
    return {"xT": xTc, "pT": pTc, "spool": spool, "sconv": sconv, "sssm": sssm}


def _host_shared(inp):
    f = np.float32
    cols = lambda v, n: np.asarray(v, dtype=f).reshape(n, 128).T
    vecs = np.zeros((128, NV), dtype=f)
    vecs[:, V_GMIX:V_GMIX + 8] = cols(inp["norm_mix_g"][0], 8)
    vecs[:, V_GMLP:V_GMLP + 8] = cols(inp["norm_mlp_g"][0], 8)
    vecs[:, V_GPLE:V_GPLE + 8] = cols(inp["norm_ple_g"][0], 8)
    vecs[:, V_GFIN:V_GFIN + 8] = cols(inp["final_norm_g"], 8)
    vecs[:, V_PSC:V_PSC + 8] = cols(inp["pool_scale"][0], 8)
    vecs[:, V_GSSM:V_GSSM + 8] = cols(inp["ssm_norm_g"][0], 8)
    vecs[:, V_CB:V_CB + 12] = cols(inp["conv_b"][0], 12)
    for j in range(4):
        vecs[:, V_CW + 12 * j:V_CW + 12 * j + 12] = cols(inp["conv_w"][0, j], 12)
    hvv = np.stack([inp["dt_bias"][0], inp["a_log"][0], inp["d_skip"][0]], axis=1).astype(f)
    c = np.ascontiguousarray
    return {"w_in": c(inp["w_in"][0], dtype=f), "w_pool": c(inp["w_pool"][0], dtype=f), "w_out": c(inp["w_out"][0], dtype=f),
            "w_ff1": c(inp["w_ff1"][0], dtype=f), "w_ff2": c(inp["w_ff2"][0], dtype=f), "w_gate": c(inp["w_gate"][0], dtype=f),
            "w_ple": c(inp["w_ple"][0], dtype=f), "vecs": vecs, "hv": c(hvv), "dsk": c(inp["d_skip"][0].reshape(1, 16), dtype=f)}


def kernel(debug=False, **inp):
    inp = {n: np.asarray(v) for n, v in inp.items()}
    key = bool(debug)
    if key not in _CACHE:
        _CACHE[key] = build_program(debug=debug)[0]
    kb = _CACHE[key]
    shared = _host_shared(inp)
    in_maps = []
    for i in range(8):
        m = dict(shared)
        m.update(_host_inputs(inp, i))
        in_maps.append(m)
    res = run_bass_kernel_spmd(kb.nc, in_maps, core_ids=list(range(8)))
    R = res.results
    f = np.float32
    y_prompt = np.stack([R[i]["yT"][:, :NP].T for i in range(8)]).astype(f)
    y_sample = np.concatenate([R[i]["yT"][:, NP:].T.reshape(16, 4, D) for i in range(8)]).astype(f)
    pool_p = np.stack([R[i]["npool_p"].transpose(2, 1, 0).reshape(15, D) for i in range(8)])[None].astype(f)
    conv_p = np.stack([R[i]["nconv_p"].transpose(2, 1, 0).reshape(3, 1536) for i in range(8)])[None].astype(f)
    ssm_p = np.stack([R[i]["nssm_p"].reshape(128, 16, 64).transpose(1, 2, 0) for i in range(8)])[None].astype(f)
    pool_s = np.concatenate([R[i]["npool_s"].reshape(128, 8, 15, 16).transpose(3, 2, 1, 0).reshape(16, 15, D) for i in range(8)])[None].astype(f)
    conv_s = np.concatenate([R[i]["nconv_s"].reshape(128, 12, 3, 16).transpose(3, 2, 1, 0).reshape(16, 3, 1536) for i in range(8)])[None].astype(f)
    ssm_s = np.concatenate([R[i]["nssm_s"] for i in range(8)])[None].astype(f)
    outs = (np.ascontiguousarray(y_prompt), np.ascontiguousarray(y_sample), np.ascontiguousarray(pool_p), np.ascontiguousarray(conv_p),
            np.ascontiguousarray(ssm_p), np.ascontiguousarray(pool_s), np.ascontiguousarray(conv_s), np.ascontiguousarray(ssm_s))
    if debug:
        return outs, R
    return outs
```

```python
import numpy as np
from contextlib import ExitStack
import concourse.bass as bass
import concourse.mybir as mybir
from concourse.bass_utils import run_bass_kernel_spmd

F32 = mybir.dt.float32
BF16 = mybir.dt.bfloat16
AF = mybir.ActivationFunctionType
ALU = mybir.AluOpType
AX = mybir.AxisListType

D = 1024
KC = 8
NP = 2048
NS = 64
NT = NP + NS
NB = 17
TT = [(0, 512), (512, 512), (1024, 512), (1536, 512), (2048, 64)]
POOLW = (2, 4, 8, 16)
EPS = 1e-6
NV = 108
V_GMIX, V_GMLP, V_GPLE, V_GFIN, V_PSC, V_GSSM, V_CB, V_CW = 0, 8, 16, 24, 32, 40, 48, 60


class Buf:
    __slots__ = ("name", "w", "r", "excl")

    def __init__(self, name, excl=False):
        self.name = name
        self.w = None
        self.r = {}
        self.excl = excl


class DSem:
    def __init__(self, key, h):
        self.key = key
        self.h = h
        self.count = 0


class KB:
    def __init__(self):
        self.nc = bass.Bass("TRN2", target_bir_lowering=False)
        self.es = ExitStack()
        nc = self.nc
        self.E = {"pe": nc.tensor, "act": nc.scalar, "dve": nc.vector, "pool": nc.gpsimd, "sp": nc.sync}
        self.esem = {e: self.es.enter_context(nc.semaphore("sem_" + e)) for e in ("pe", "act", "dve", "pool")}
        self.seq = {e: 0 for e in self.esem}
        self.waited = {e: {} for e in self.E}
        self.bufs = []
        self.dsems = []
        self.bar = self.es.enter_context(nc.semaphore("sem_bar"))
        self.bar_count = 0
        self.out_toks = []
        self.nbuf = 0

    def buf(self, name="b"):
        b = Buf(name)
        self.bufs.append(b)
        return b

    def bufl(self, n, name="b"):
        return [self.buf(name + str(i)) for i in range(n)]

    def dsem(self, name):
        d = DSem("d_" + name + str(len(self.dsems)), self.es.enter_context(self.nc.semaphore("ds_" + name + str(len(self.dsems)))))
        self.dsems.append(d)
        return d

    def sb(self, name, shape, dt):
        return self.es.enter_context(self.nc.sbuf_tensor(name, shape, dt))

    def psum(self, name, shape, dt):
        return self.es.enter_context(self.nc.psum_tensor(name, shape, dt))

    def _deps(self, eng, reads, writes, is_dma):
        deps = {}

        def need(tok, kind):
            if tok is None:
                return
            key, sem, val = tok
            if (not is_dma) and key == eng and val > self.seq[eng]:
                return
            if key not in deps or deps[key][1] < val:
                deps[key] = (sem, val)

        for b in reads:
            need(b.w, "raw")
            if b.excl:
                for t in b.r.values():
                    need(t, "war")
        for b in writes:
            need(b.w, "waw")
            for t in b.r.values():
                need(t, "war")
        wd = self.waited[eng]
        for key, (sem, val) in deps.items():
            if wd.get(key, 0) < val:
                self.E[eng].wait_ge(sem, val)
                wd[key] = val

    def _note(self, tok, reads, writes):
        for b in reads:
            if b.excl and b not in writes:
                b.w = tok
                b.r = {}
                continue
            old = b.r.get(tok[0])
            if old is None or old[2] < tok[2]:
                b.r[tok[0]] = tok
        for b in writes:
            b.w = tok
            b.r = {}

    def op(self, eng, fn, reads=(), writes=(), track=True):
        self._deps(eng, reads, writes, False)
        ins = fn(self.E[eng])
        if track:
            self.seq[eng] += 1
            ins.then_inc(self.esem[eng], 1)
            tok = (eng, self.esem[eng], self.seq[eng])
        else:
            tok = (eng, self.esem[eng], self.seq[eng] + 1)
        self._note(tok, reads, writes)
        return tok

    def dma(self, q, out, in_, ds, reads=(), writes=(), is_out=False):
        self._deps(q, reads, writes, True)
        self.E[q].dma_start(out=out, in_=in_).then_inc(ds.h, 16)
        ds.count += 16
        tok = (ds.key, ds.h, ds.count)
        self._note(tok, reads, writes)
        if is_out:
            self.out_toks.append(tok)
        return tok

    def barrier(self, keep=()):
        skip = {d.key for d, _ in keep}
        keepb = {id(b) for _, b in keep}
        sp = self.E["sp"]
        wd = self.waited["sp"]
        for e, s in self.esem.items():
            if wd.get(e, 0) < self.seq[e]:
                sp.wait_ge(s, self.seq[e])
                wd[e] = self.seq[e]
        for d in self.dsems:
            if d.key in skip:
                continue
            if d.count and wd.get(d.key, 0) < d.count:
                sp.wait_ge(d.h, d.count)
                wd[d.key] = d.count
        self.bar_count += 1
        sp.sem_inc(self.bar, 1)
        for e in ("pe", "act", "dve", "pool"):
            self.E[e].wait_ge(self.bar, self.bar_count)
            w = self.waited[e]
            for e2 in self.esem:
                w[e2] = self.seq[e2]
            for d in self.dsems:
                if d.key not in skip:
                    w[d.key] = d.count
        for b in self.bufs:
            if id(b) in keepb:
                continue
            b.w = None
            b.r = {}

    def finish(self):
        sp = self.E["sp"]
        wd = self.waited["sp"]
        for key, sem, val in self.out_toks:
            if wd.get(key, 0) < val:
                sp.wait_ge(sem, val)
                wd[key] = val


class Region:
    def __init__(self, arena_ap, nbytes):
        self.a = arena_ap
        self.nbytes = nbytes
        self.off = 0

    def reset(self):
        self.off = 0

    def alloc(self, shape, dt, parts=128):
        esz = 4 if dt == F32 else 2
        n = 1
        for s in shape[1:]:
            n *= s
        nb = (n * esz + 31) // 32 * 32
        assert self.off + nb <= self.nbytes, ("region overflow", self.off, nb, self.nbytes)
        v = self.a[0:shape[0], self.off // 2:(self.off + n * esz) // 2]
        self.off += nb
        if dt == F32:
            v = v.bitcast(F32)
        if len(shape) == 3:
            v = v.rearrange("p (a b) -> p a b", a=shape[1])
        elif len(shape) == 4:
            v = v.rearrange("p (a b c) -> p a b c", a=shape[1], b=shape[2])
        return v


def build_program(debug=False):
    k = KB()
    nc = k.nc
    dram_in = lambda n, s: nc.dram_tensor(n, s, F32, kind="ExternalInput").ap()
    dram_out = lambda n, s: nc.dram_tensor(n, s, F32, kind="ExternalOutput").ap()
    xT = dram_in("xT", [D, NT])
    pT = dram_in("pT", [256, NT])
    spool = dram_in("spool", [128, 8 * 15 * 16])
    sconv = dram_in("sconv", [128, 12 * 3 * 16])
    sssm = dram_in("sssm", [16, 16, 64, 128])
    w_in = dram_in("w_in", [D, 3600])
    w_pool = dram_in("w_pool", [4, 256, 256])
    w_out = dram_in("w_out", [2048, D])
    w_ff1 = dram_in("w_ff1", [D, 4096])
    w_ff2 = dram_in("w_ff2", [4096, D])
    w_gate = dram_in("w_gate", [D, D])
    w_ple = dram_in("w_ple", [256, D])
    vecs = dram_in("vecs", [128, NV])
    hv = dram_in("hv", [16, 3])
    dsk = dram_in("dsk", [1, 16])
    yT = dram_out("yT", [D, NT])
    npool_p = dram_out("npool_p", [128, 8, 15])
    nconv_p = dram_out("nconv_p", [128, 12, 3])
    nssm_p = dram_out("nssm_p", [128, 1024])
    npool_s = dram_out("npool_s", [128, 8, 15 * 16])
    nconv_s = dram_out("nconv_s", [128, 12, 3 * 16])
    nssm_s = dram_out("nssm_s", [16, 16, 64, 128])
    dbg = {}

    XN = k.sb("XN", [128, KC, NT], BF16)
    R2 = k.sb("R2", [128, 33792], BF16)
    R3 = k.sb("R3", [128, 42752], BF16)
    YMIX = R2[:, :].rearrange("p (c t) -> p c t", c=16)
    r2 = Region(R2[:, :], 67584)
    r3 = Region(R3[:, :], 85504)
    r2h = Region(R2[:, 0:16896], 33792)
    H = R3[:, 0:33792].bitcast(F32).rearrange("p (c t) -> p c t", c=KC)
    r3x = Region(R3[:, 33792:42752], 17920)

    IDB = k.sb("IDB", [128, 128], BF16)
    IDF = k.sb("IDF", [128, 128], F32)
    ONESB = k.sb("ONESB", [128, 128], BF16)
    ONESF = k.sb("ONESF", [128, 128], F32)
    TRIF = k.sb("TRIF", [128, 128], F32)
    TRIB = k.sb("TRIB", [128, 128], BF16)
    UB = k.sb("UB", [128, 128], BF16)
    TRIFS = k.sb("TRIFS", [64, 64], F32)
    TRIBS = k.sb("TRIBS", [64, 64], BF16)
    UBS = k.sb("UBS", [64, 64], BF16)
    BDFS = k.sb("BDFS", [64, 64], F32)
    DI = k.sb("DI", [128, 16, 128], BF16)
    BDROW = k.sb("BDROW", [128, 16, 64], BF16)
    BDCOL = k.sb("BDCOL", [64, 16], BF16)
    VEC = k.sb("VEC", [128, NV], F32)
    HV = k.sb("HV", [16, 4], F32)
    DB = k.sb("DB", [128, 16], F32)
    INVC = k.sb("INVC", [128, 4, 16], F32)
    TOK = k.sb("TOK", [128, NB, 5, 16], F32)
    AHL = k.sb("AHL", [128, NB, 2, 16], BF16)
    DTAS = k.sb("DTAS", [16, 64], F32)
    SST = k.sb("SST", [128, 1024], F32)
    HP = k.sb("HP", [128, 1024], BF16)
    SMALL = k.sb("SMALL", [128, 64], F32)
    TMPC = k.sb("TMPC", [128, 128], F32)
    CDX = k.sb("CDX", [128, 8, 16], F32)
    WDT = k.sb("WDT", [128, KC, 16], BF16)

    PS = [k.psum("ps%d" % i, [128, 512], F32) for i in range(8)]
    PB = k.bufl(8, "pb")
    for b_ in PB:
        b_.excl = True
    bank_rr = [0]

    def next_bank():
        i = bank_rr[0]
        bank_rr[0] = (i + 1) % 8
        return i

    cb = k.buf("const")
    vb = k.buf("vec")
    XNb = k.bufl(5, "xn")
    WS_DS = [k.dsem("ws") for _ in range(3)]
    ld_ds = [k.dsem("ld") for _ in range(12)]
    out_ds = [k.dsem("out") for _ in range(8)]
    od_i = [0]

    def next_out_ds():
        d = out_ds[od_i[0] % len(out_ds)]
        od_i[0] += 1
        return d

    def dump(name, ap, bufs, dt=F32):
        if not debug:
            return
        t = nc.dram_tensor("dbg_" + name, list(ap.shape), dt, kind="ExternalOutput").ap()
        k.dma("sp", t, ap, k.dsem("dbg"), reads=bufs, is_out=True)

    def pool_op(fn, reads=(), writes=(cb,)):
        return k.op("pool", fn, reads=reads, writes=writes)

    def sel(t_ap, pattern, cmp_op, base, cm):
        pool_op(lambda e: e.affine_select(out=t_ap, in_=t_ap, pattern=pattern, compare_op=cmp_op, fill=0.0,
                                          base=base, channel_multiplier=cm), reads=(cb,))

    for t in (IDF, ONESF, TRIF):
        pool_op(lambda e, t=t: e.memset(t[:], 1.0))
    sel(IDF[:], [[-1, 128]], ALU.is_equal, 0, 1)
    sel(TRIF[:], [[1, 128]], ALU.is_ge, 0, -1)
    UF = TMPC
    pool_op(lambda e: e.memset(UF[:], 1.0))
    sel(UF[:], [[-1, 128]], ALU.is_gt, 0, 1)
    pool_op(lambda e: e.memset(BDFS[:], 1.0))
    bdv = BDFS[:].rearrange("p (b l) -> p b l", l=4)
    sel(bdv, [[-4, 16], [0, 4]], ALU.is_ge, 0, 1)
    sel(bdv, [[4, 16], [0, 4]], ALU.is_ge, 3, -1)
    k.op("dve", lambda e: e.tensor_copy(IDB[:], IDF[:]), reads=(cb,), writes=(cb,))
    k.op("dve", lambda e: e.tensor_copy(ONESB[:], ONESF[:]), reads=(cb,), writes=(cb,))
    k.op("dve", lambda e: e.tensor_copy(TRIB[:], TRIF[:]), reads=(cb,), writes=(cb,))
    k.op("dve", lambda e: e.tensor_copy(UB[:], UF[:]), reads=(cb,), writes=(cb,))
    k.op("dve", lambda e: e.tensor_tensor(TRIFS[:], TRIF[0:64, 0:64], BDFS[:], ALU.mult), reads=(cb,), writes=(cb,))
    k.op("dve", lambda e: e.tensor_copy(TRIBS[:], TRIFS[:]), reads=(cb,), writes=(cb,))
    k.op("dve", lambda e: e.tensor_tensor(UBS[:], UF[0:64, 0:64], BDFS[:], ALU.mult), reads=(cb,), writes=(cb,))
    pool_op(lambda e: e.memset(BDROW[:], 1.0))
    sel(BDROW[:], [[-4, 16], [1, 64]], ALU.is_ge, 0, 0)
    sel(BDROW[:], [[4, 16], [-1, 64]], ALU.is_ge, 3, 0)
    pool_op(lambda e: e.memset(BDCOL[:], 1.0))
    sel(BDCOL[:], [[-4, 16]], ALU.is_ge, 0, 1)
    sel(BDCOL[:], [[4, 16]], ALU.is_ge, 3, -1)
    k.dma("sp", VEC[:], vecs, ld_ds[0], writes=(vb,))
    k.dma("sp", HV[:, 0:3], hv, ld_ds[1], writes=(vb,))
    k.dma("sp", DB[:], dsk.partition_broadcast(128), ld_ds[2], writes=(vb,))
    k.op("dve", lambda e: e.tensor_tensor(DI[:], IDF[:].unsqueeze(1).broadcast_to([128, 16, 128]),
                                          DB[:].unsqueeze(2).broadcast_to([128, 16, 128]), ALU.mult),
         reads=(cb, vb), writes=(cb,))
    for g, w in enumerate(POOLW):
        pool_op(lambda e, g=g: e.iota(INVC[:, g, :], pattern=[[1, 16]], base=1, channel_multiplier=0, allow_small_or_imprecise_dtypes=True))
    for g, w in enumerate(POOLW):
        k.op("dve", lambda e, g=g, w=w: e.tensor_scalar(INVC[:, g, :], INVC[:, g, :], float(w), None, ALU.min),
             reads=(cb,), writes=(cb,))
    k.op("dve", lambda e: e.reciprocal(INVC[:], INVC[:]), reads=(cb,), writes=(cb,))
    k.op("act", lambda e: e.activation(HV[:, 3:4], HV[:, 1:2], AF.Exp), reads=(vb,), writes=(vb,))
    k.op("dve", lambda e: e.tensor_scalar(HV[:, 3:4], HV[:, 3:4], -1.0, None, ALU.mult), reads=(vb,), writes=(vb,))

    def wview_kc(w_dram, c0, ncols):
        return w_dram.rearrange("(kc p) n -> p kc n", p=128)[:, :, c0:c0 + ncols]

    def rmsnorm_tile(ti, src, src_bufs, gcol, dst_fn, dst_bufs, SQ, RS, sqb, rsb, post=None):
        t0, n = TT[ti]
        s = ti % 2
        k.op("act", lambda e: e.activation(SQ[s][:, :, 0:n], src, AF.Square), reads=src_bufs, writes=(sqb[s],))
        bi = next_bank()
        for kc in range(KC):
            k.op("pe", lambda e, kc=kc: e.matmul(PS[bi][:, 0:n], ONESB[:], SQ[s][:, kc, 0:n], start=(kc == 0), stop=(kc == KC - 1)),
                 reads=(sqb[s], cb), writes=(PB[bi],), track=(kc == KC - 1))
        k.op("dve", lambda e: e.tensor_scalar(RS[s][:, 0:n], PS[bi][:, 0:n], 1.0 / D, EPS, ALU.mult, ALU.add),
             reads=(PB[bi],), writes=(rsb[s],))
        k.op("act", lambda e: e.activation(RS[s][:, 0:n], RS[s][:, 0:n], AF.Ln), reads=(rsb[s],), writes=(rsb[s],))
        k.op("act", lambda e: e.activation(RS[s][:, 0:n], RS[s][:, 0:n], AF.Exp, scale=-0.5), reads=(rsb[s],), writes=(rsb[s],))
        for kc in range(KC):
            db = dst_bufs(kc) if callable(dst_bufs) else dst_bufs
            k.op("dve", lambda e, kc=kc: e.scalar_tensor_tensor(dst_fn(kc), src[:, kc, :], VEC[:, gcol + kc:gcol + kc + 1],
                                                                RS[s][:, 0:n], ALU.mult, ALU.mult),
                 reads=tuple(src_bufs) + (rsb[s], vb), writes=db)
            if post is not None:
                post(kc)

    def load_w(slot_ap, src_ap, ds, slot_buf):
        return k.dma("pool", slot_ap, src_ap, ds, writes=(slot_buf,))

    r2.reset()
    XS = [r2.alloc([128, KC, 512], F32) for _ in range(2)]
    SQ = [r2.alloc([128, KC, 512], BF16) for _ in range(2)]
    RS = [r2.alloc([128, 512], F32)] * 2
    xsb, sqb, rsb = k.bufl(2, "xs"), k.bufl(2, "sq"), [k.buf("rs")] * 2
    assert r2.off <= 51200
    r2t = Region(R2[:, 25600:33792], 16384)
    WSZ = [r2t.alloc([128, KC, 512], BF16) for _ in range(2)]
    wszb = k.bufl(2, "wsz")
    load_w(WSZ[0][:], wview_kc(w_in, 1024, 512), WS_DS[0], wszb[0])
    load_w(WSZ[1][:], wview_kc(w_in, 1536, 512), WS_DS[1], wszb[1])
    wdtb = k.buf("wdt")
    load_w(WDT[:], wview_kc(w_in, 3584, 16), ld_ds[10], wdtb)
    r3.reset()
    SZT = r3.alloc([128, NB, 1024], BF16)
    XBC = r3.alloc([128, 12, NT], BF16)
    sztb = k.bufl(NB, "szt")
    xbcb = k.bufl(12, "xbc")
    wxv = lambda c0: XBC[:, c0:c0 + 2, :].rearrange("p a b -> p (a b)")[:, 0:4096].rearrange("p (a b) -> p a b", a=KC)
    WX = {2: wxv(0), 1: wxv(2)}
    wxb = {2: k.buf("wx2"), 1: k.buf("wx1")}
    wx_ds = {2: k.dsem("wx2"), 1: k.dsem("wx1")}
    WS = [WSZ[0], WSZ[1], None]
    wsb = [wszb[0], wszb[1], None]
    dtv_ = lambda c0: XBC[0:16, c0:c0 + 2, :].rearrange("p a b -> p (a b)")[:, 0:4096].bitcast(F32).rearrange("p (j t) -> p j t", j=4)
    DTT = [dtv_(4), dtv_(6)]
    dttb = k.bufl(2, "dtt")
    tokb = k.bufl(NB, "tok")
    dtasb = k.buf("dtas")
    def z_block(blk):
        m = 128 if blk < 16 else 64
        c0 = blk * 128
        for half in range(2):
            bi = next_bank()
            for kc in range(KC):
                k.op("pe", lambda e, kc=kc: e.matmul(PS[bi][0:m, :], XN[:, kc, c0:c0 + m], WS[half][:, kc, :],
                                                     start=(kc == 0), stop=(kc == KC - 1)),
                     reads=(XNb[min(blk // 4, 4)], wsb[half]), writes=(PB[bi],), track=(kc == KC - 1))
            k.op("act", lambda e: e.activation(SZT[0:m, blk, half * 512:(half + 1) * 512], PS[bi][0:m, :], AF.Silu),
                 reads=(PB[bi],), writes=(sztb[blk],))
    def dt_a(ti):
        t0, n = TT[ti]
        q = ti % 2
        bi = next_bank()
        for kc in range(KC):
            k.op("pe", lambda e, kc=kc: e.matmul(PS[bi][0:16, 0:n], WDT[:, kc, :], XN[:, kc, t0:t0 + n],
                                                 start=(kc == 0), stop=(kc == KC - 1)),
                 reads=(XNb[ti], wdtb), writes=(PB[bi],), track=(kc == KC - 1))
        raw, tmp, dtv, dta = (DTT[q][:, j, 0:n] for j in range(4))
        k.op("dve", lambda e: e.tensor_scalar(raw, PS[bi][0:16, 0:n], HV[:, 0:1], None, ALU.add), reads=(PB[bi], vb), writes=(dttb[q],))
        k.op("act", lambda e: e.activation(tmp, raw, AF.Abs), reads=(dttb[q],), writes=(dttb[q],))
        k.op("act", lambda e: e.activation(tmp, tmp, AF.Exp, scale=-1.0), reads=(dttb[q],), writes=(dttb[q],))
        k.op("act", lambda e: e.activation(tmp, tmp, AF.Ln, bias=1.0), reads=(dttb[q],), writes=(dttb[q],))
        k.op("dve", lambda e: e.scalar_tensor_tensor(dtv, raw, 0.0, tmp, ALU.max, ALU.add), reads=(dttb[q],), writes=(dttb[q],))
        k.op("dve", lambda e: e.tensor_scalar(dta, dtv, HV[:, 3:4], None, ALU.mult), reads=(dttb[q], vb), writes=(dttb[q],))
        if ti == 4:
            k.op("dve", lambda e: e.tensor_copy(DTAS[:], dta), reads=(dttb[q],), writes=(dtasb,))

    def dt_b(ti):
        t0, n = TT[ti]
        q = ti % 2
        raw, tmp, dtv, dta = (DTT[q][:, j, 0:n] for j in range(4))
        nblk = (n + 127) // 128
        bj = next_bank()
        for b4 in range(nblk):
            m = min(128, n - b4 * 128)
            for j, srcv in enumerate((dtv, dta)):
                k.op("pe", lambda e, j=j, srcv=srcv: e.transpose(PS[bj][0:m, (b4 * 2 + j) * 16:(b4 * 2 + j + 1) * 16],
                                                                 srcv[:, b4 * 128:b4 * 128 + m], IDF[0:16, 0:16]),
                     reads=(dttb[q], cb), writes=(PB[bj],), track=(b4 == nblk - 1 and j == 1))
        for b4 in range(nblk):
            m = min(128, n - b4 * 128)
            blk = ti * 4 + b4
            k.op("act", lambda e: e.copy(TOK[0:m, blk, 0:2, :], PS[bj][0:m, b4 * 32:(b4 + 1) * 32].rearrange("p (j h) -> p j h", j=2)),
                 reads=(PB[bj],), writes=(tokb[blk],))

    xTv = xT.rearrange("(c p) t -> p c t", p=128)

    def norm0(ti):
        t0, n = TT[ti]
        s_ = ti % 2
        k.dma("sp", XS[s_][:, :, 0:n], xTv[:, :, t0:t0 + n], ld_ds[3 + s_], writes=(xsb[s_],))
        rmsnorm_tile(ti, XS[s_][:, :, 0:n], (xsb[s_],), V_GMIX, lambda kc: XN[:, kc, t0:t0 + n], (XNb[ti],),
                     SQ, RS, sqb, rsb)

    def zdt(ti):
        dt_a(ti)
        for blk in ([16] if ti == 4 else range(4 * ti, 4 * ti + 4)):
            z_block(blk)
        dt_b(ti)

    norm0(0)
    norm0(1)
    for grp in (2, 1):
        k.dma("pool", WX[grp][:], wview_kc(w_in, 2048 + 512 * grp, 512), wx_ds[grp], reads=(XNb[1],), writes=(wxb[grp],))
    zdt(0)
    norm0(2)
    zdt(1)
    norm0(3)
    zdt(2)
    norm0(4)
    zdt(3)
    zdt(4)
    dump("xn", XN[:], XNb, BF16)
    k.barrier()

    r2.reset()
    WS[2] = r2.alloc([128, KC, 512], BF16)
    wsb[2] = k.buf("ws2")
    XB = [r2.alloc([128, NP + 3], F32) for _ in range(2)]
    XBS = [r2.alloc([128, 7, 16], F32) for _ in range(2)]
    TC = [r2.alloc([128, NP], F32) for _ in range(2)]
    TCS = [r2.alloc([128, 4, 16], F32) for _ in range(2)]
    SCV = r2.alloc([128, 12, 48], F32)
    xbb, tcb = k.bufl(2, "xb"), k.bufl(2, "tc")
    scvb = k.buf("scv")
    k.dma("sp", SCV[:], sconv.rearrange("p (c f) -> p c f", c=12), ld_ds[5], writes=(scvb,))
    load_w(WS[2][:], wview_kc(w_in, 2048, 512), WS_DS[2], wsb[2])
    assert r2.off <= 51200
    pending_silu = []

    def flush_silu():
        while pending_silu:
            ch_, q_ = pending_silu.pop(0)
            park = (wxb[2],) if ch_ in (0, 1) else ((wxb[1],) if ch_ in (2, 3) else ())
            k.op("act", lambda e: e.activation(XBC[:, ch_, 0:NP], TC[q_][:], AF.Silu),
                 reads=(tcb[q_],), writes=(xbcb[ch_],) + park)
            k.op("act", lambda e: e.activation(XBC[:, ch_, NP:NT].rearrange("p (b l) -> p l b", l=4), TCS[q_][:], AF.Silu),
                 reads=(tcb[q_],), writes=(xbcb[ch_],) + park)

    for grp in (2, 1, 0):
        WG_, wgb_ = (WS[2], wsb[2]) if grp == 0 else (WX[grp], wxb[grp])
        for c4 in range(4):
            ch = grp * 4 + c4
            q = ch % 2
            k.op("pool", lambda e: e.memset(XB[q][:, 0:3], 0.0), writes=(xbb[q],))
            k.op("pool", lambda e: e.tensor_copy(XBS[q][:, 0:3, :], SCV[:, ch, :].rearrange("p (j b) -> p j b", j=3)),
                 reads=(scvb,), writes=(xbb[q],))
            for ti, (t0, n) in enumerate(TT):
                bi = next_bank()
                for kc in range(KC):
                    k.op("pe", lambda e, kc=kc: e.matmul(PS[bi][:, 0:n], WG_[:, kc, c4 * 128:(c4 + 1) * 128], XN[:, kc, t0:t0 + n],
                                                         start=(kc == 0), stop=(kc == KC - 1)),
                         reads=(XNb[ti], wgb_), writes=(PB[bi],), track=(kc == KC - 1))
                if ti < 4:
                    k.op("act", lambda e: e.copy(XB[q][:, 3 + t0:3 + t0 + n], PS[bi][:, 0:n]), reads=(PB[bi],), writes=(xbb[q],))
                else:
                    k.op("act", lambda e: e.copy(XBS[q][:, 3:7, :], PS[bi][:, 0:64].rearrange("p (b l) -> p l b", l=4)),
                         reads=(PB[bi],), writes=(xbb[q],))
            k.dma("sp", nconv_p[:, ch, :], XB[q][:, NP:NP + 3], next_out_ds(), reads=(xbb[q],), is_out=True)
            k.dma("sp", nconv_s[:, ch, :], XBS[q][:, 4:7, :].rearrange("p j b -> p (j b)"), next_out_ds(), reads=(xbb[q],), is_out=True)
            cw = lambda j: VEC[:, V_CW + j * 12 + ch:V_CW + j * 12 + ch + 1]
            bcol = VEC[:, V_CB + ch:V_CB + ch + 1]
            k.op("act", lambda e: e.activation(TC[q][:], XB[q][:, 0:NP], AF.Identity, bias=bcol, scale=cw(0)),
                 reads=(xbb[q], vb), writes=(tcb[q],))
            k.op("act", lambda e: e.activation(TCS[q][:], XBS[q][:, 0:4, :], AF.Identity, bias=bcol, scale=cw(0)),
                 reads=(xbb[q], vb), writes=(tcb[q],))
            flush_silu()
            for j in range(1, 4):
                k.op("dve", lambda e, j=j: e.scalar_tensor_tensor(TC[q][:], XB[q][:, j:j + NP], cw(j), TC[q][:], ALU.mult, ALU.add),
                     reads=(xbb[q], vb, tcb[q]), writes=(tcb[q],))
                k.op("dve", lambda e, j=j: e.scalar_tensor_tensor(TCS[q][:], XBS[q][:, j:j + 4, :], cw(j), TCS[q][:], ALU.mult, ALU.add),
                     reads=(xbb[q], vb, tcb[q]), writes=(tcb[q],))
            pending_silu.append((ch, q))
    flush_silu()
    dump("szt", SZT[:], sztb, BF16)
    dump("xbc", XBC[:], xbcb, BF16)
    dump("tok01", TOK[:], tokb)
    k.barrier()

    r2h.reset()
    TMPA = r2h.alloc([128, NB, 16], F32)
    TMPB = r2h.alloc([128, NB, 16], F32)
    EXPM = r2h.alloc([16, 8, 128], F32)
    CDF = r2h.alloc([16, 16], F32)
    tmpb_ = k.bufl(2, "tmpab")
    expb, cdfb, cdxb = k.buf("expm"), k.buf("cdf"), k.buf("cdx")
    ACUM_ps = PS[0][:, 0:NB * 16].rearrange("p (b h) -> p b h", h=16)
    ATOT_ps = PS[1][:, 0:NB * 16].rearrange("p (b h) -> p b h", h=16)
    for blk in range(NB):
        m = 128 if blk < 16 else 64
        tri = TRIF[:] if blk < 16 else TRIFS[:]
        one = ONESF[:] if blk < 16 else BDFS[:]
        k.op("pe", lambda e: e.matmul(ACUM_ps[0:m, blk, :], tri, TOK[0:m, blk, 1, :], start=True, stop=True),
             reads=(tokb[blk], cb), writes=(PB[0],), track=False)
        k.op("pe", lambda e: e.matmul(ATOT_ps[0:m, blk, :], one, TOK[0:m, blk, 1, :], start=True, stop=True),
             reads=(tokb[blk], cb), writes=(PB[1],), track=(blk == NB - 1))
    for (p0, p1, b0, b1) in ((0, 128, 0, 16), (0, 64, 16, 17)):
        tb = tokb[b0:b1]
        k.op("act", lambda e: e.activation(TOK[p0:p1, b0:b1, 2, :], ACUM_ps[p0:p1, b0:b1, :], AF.Exp), reads=(PB[0],), writes=tb)
        k.op("act", lambda e: e.activation(TOK[p0:p1, b0:b1, 3, :], ATOT_ps[p0:p1, b0:b1, :], AF.Exp), reads=(PB[1],), writes=tb)
        k.op("act", lambda e: e.copy(TMPA[p0:p1, b0:b1, :], ACUM_ps[p0:p1, b0:b1, :]), reads=(PB[0],), writes=(tmpb_[0],))
        k.op("dve", lambda e: e.tensor_tensor(TMPB[p0:p1, b0:b1, :], ATOT_ps[p0:p1, b0:b1, :], TMPA[p0:p1, b0:b1, :], ALU.subtract),
             reads=(PB[1], tmpb_[0]), writes=(tmpb_[1],))
        k.op("act", lambda e: e.activation(TMPB[p0:p1, b0:b1, :], TMPB[p0:p1, b0:b1, :], AF.Exp), reads=(tmpb_[1],), writes=(tmpb_[1],))
        k.op("dve", lambda e: e.tensor_tensor(TOK[p0:p1, b0:b1, 4, :], TMPB[p0:p1, b0:b1, :], TOK[p0:p1, b0:b1, 0, :], ALU.mult),
             reads=[tmpb_[1]] + tb, writes=tb)
        k.op("dve", lambda e: e.tensor_copy(AHL[p0:p1, b0:b1, 0, :], TOK[p0:p1, b0:b1, 1, :]), reads=tb, writes=tb)
        k.op("dve", lambda e: e.tensor_copy(TMPA[p0:p1, b0:b1, :], AHL[p0:p1, b0:b1, 0, :]), reads=tb + [tmpb_[0]], writes=(tmpb_[0],))
        k.op("dve", lambda e: e.tensor_tensor(AHL[p0:p1, b0:b1, 1, :], TOK[p0:p1, b0:b1, 1, :], TMPA[p0:p1, b0:b1, :], ALU.subtract),
             reads=tb + [tmpb_[0]], writes=tb)
    k.op("dve", lambda e: e.tensor_reduce(out=CDF[:], in_=DTAS[:].rearrange("p (b l) -> p b l", l=4), axis=AX.X, op=ALU.add),
         reads=(dtasb,), writes=(cdfb,))
    k.op("act", lambda e: e.activation(CDF[:], CDF[:], AF.Exp), reads=(cdfb,), writes=(cdfb,))
    k.op("pool", lambda e: e.memset(EXPM[:], 1.0), writes=(expb,))
    expv = EXPM[:].rearrange("p j (a d) -> p j a d", a=2)
    k.op("pool", lambda e: e.affine_select(out=expv, in_=expv, pattern=[[-2, 8], [-1, 2], [0, 64]], compare_op=ALU.is_equal,
                                           fill=0.0, base=0, channel_multiplier=1), reads=(expb,), writes=(expb,))
    for j in range(8):
        k.op("pe", lambda e, j=j: e.matmul(PS[2][:, j * 16:(j + 1) * 16], EXPM[:, j, :], CDF[:], start=True, stop=True),
             reads=(expb, cdfb), writes=(PB[2],), track=(j == 7))
    k.op("act", lambda e: e.copy(CDX[:], PS[2][:, 0:128].rearrange("p (j b) -> p j b", j=8)), reads=(PB[2],), writes=(cdxb,))
    dump("tok", TOK[:], tokb)
    k.barrier()

    xtokb, xwb, btokb, cbmb, mmb, yab_g, ynb_g, sdb, sstb, hpb, smb, xdtb = (k.buf(n) for n in
        ("xtok", "xw", "btok", "cbm", "mm", "ya", "yn", "sd", "sst", "hp", "small", "xdt"))
    rhb, decb = k.bufl(4, "rh"), k.bufl(4, "dec")
    mmq = k.bufl(4, "mmq")
    ysb = k.bufl(NB, "ys")
    XT_ps = PS[0][:].bitcast(BF16)
    BT_ps = PS[1][:, 0:128].bitcast(BF16)
    CB_ps = PS[1][:, 256:512].rearrange("p (g l) -> p g l", g=2)
    bc16 = lambda ap, m: ap.unsqueeze(2).broadcast_to([m, 16, 64])
    v3 = lambda ap: ap.rearrange("p (h d) -> p h d", h=16)

    def ssd_alloc(m):
        r2h.reset()
        T = {}
        T["XTOK"] = r2h.alloc([128, 1024], BF16)
        T["XW"] = r2h.alloc([128, 1024], BF16)
        T["XDT"] = r2h.alloc([128, 1024], BF16)
        T["BTOK"] = r2h.alloc([128, 256], BF16)
        T["CBM"] = r2h.alloc([128, 2, m], F32)
        T["RH"] = [r2h.alloc([128, 4, m], BF16) for _ in range(4)]
        T["DEC"] = [r2h.alloc([128, 4, m], F32) for _ in range(4)]
        T["MM"] = r2h.alloc([128, 16, m], BF16)
        return T

    def ssd_A1(blk, T):
        m = 128 if blk < 16 else 64
        t0 = blk * 128
        trib = TRIB[:] if blk < 16 else TRIBS[:]
        maskf = TRIF[:] if blk < 16 else TRIFS[:]
        XTOK, XW, BTOK, CBM, RH = (T[n] for n in ("XTOK", "XW", "BTOK", "CBM", "RH"))
        for j in range(8):
            k.op("pe", lambda e, j=j: e.transpose(XT_ps[0:m, j * 128:(j + 1) * 128], XBC[:, j, t0:t0 + m], IDB[:]),
                 reads=(xbcb[j], cb), writes=(PB[0],), track=(j == 7))
        for g in range(2):
            k.op("pe", lambda e, g=g: e.transpose(BT_ps[0:m, g * 128:(g + 1) * 128], XBC[:, 8 + g, t0:t0 + m], IDB[:]),
                 reads=(xbcb[8 + g], cb), writes=(PB[1],), track=False)
        for g in range(2):
            k.op("pe", lambda e, g=g: e.matmul(CB_ps[0:m, g, 0:m], XBC[:, 8 + g, t0:t0 + m], XBC[:, 10 + g, t0:t0 + m], start=True, stop=True),
                 reads=(xbcb[8 + g], xbcb[10 + g]), writes=(PB[1],), track=(g == 1))
        for q in range(4):
            k.op("pool", lambda e, q=q: e.tensor_tensor(RH[q][0:m, :, :], trib[0:m, 0:m].unsqueeze(1).broadcast_to([m, 4, m]),
                                                        AHL[0:m, blk, 0, 4 * q:4 * q + 4].unsqueeze(2).broadcast_to([m, 4, m]), ALU.mult),
                 reads=(cb, tokb[blk]), writes=(rhb[q],))
        k.op("act", lambda e: e.copy(XTOK[0:m, :], XT_ps[0:m, :]), reads=(PB[0],), writes=(xtokb,))
        k.op("act", lambda e: e.copy(BTOK[0:m, :], BT_ps[0:m, :]), reads=(PB[1],), writes=(btokb,))
        k.op("dve", lambda e: e.tensor_tensor(CBM[0:m, :, :], CB_ps[0:m, :, 0:m], maskf[0:m, 0:m].unsqueeze(1).broadcast_to([m, 2, m]), ALU.mult),
             reads=(PB[1], cb), writes=(cbmb,))
        k.op("dve", lambda e: e.tensor_tensor(v3(T["XDT"][0:m, :]), v3(XT_ps[0:m, :]), bc16(TOK[0:m, blk, 0, :], m), ALU.mult),
             reads=(PB[0], tokb[blk]), writes=(xdtb,))
        k.op("pool", lambda e: e.tensor_tensor(v3(XW[0:m, :]), v3(XTOK[0:m, :]), bc16(TOK[0:m, blk, 4, :], m), ALU.mult),
             reads=(xtokb, tokb[blk]), writes=(xwb,))

    def ssd_A2(blk, T, part):
        m = 128 if blk < 16 else 64
        ub = UB[:] if blk < 16 else UBS[:]
        CBM, RH, DEC, MM = (T[n] for n in ("CBM", "RH", "DEC", "MM"))
        for q in range(4):
            bq = 2 + q % 2
            segv = PS[bq][:, 0:4 * m].rearrange("p (h l) -> p h l", h=4)
            if part == 0:
                k.op("pe", lambda e: e.matmul(PS[bq][0:m, 0:4 * m], ub[0:m, 0:m], RH[q][0:m, :, :].rearrange("p h l -> p (h l)"), start=True, stop=True),
                     reads=(cb, rhb[q]), writes=(PB[bq],))
                k.op("act", lambda e: e.activation(DEC[q][0:m, :, :], segv[0:m, :, :], AF.Exp), reads=(PB[bq],), writes=(decb[q],))
            else:
                g = q // 2
                k.op("dve", lambda e: e.tensor_tensor(MM[0:m, 4 * q:4 * q + 4, :], DEC[q][0:m, :, :],
                                                      CBM[0:m, g, :].unsqueeze(1).broadcast_to([m, 4, m]), ALU.mult),
                     reads=(decb[q], cbmb), writes=(mmq[q],))

    def ssd_Y(blk, T):
        m = 128 if blk < 16 else 64
        XTOK, MM, XDT = T["XTOK"], T["MM"], T["XDT"]
        for h in range(16):
            by = 4 + h // 8
            hc = (h % 8) * 64
            k.op("pe", lambda e: e.matmul(PS[by][0:m, hc:hc + 64], MM[0:m, h, :], XDT[0:m, h * 64:(h + 1) * 64],
                                          start=(h % 8 == 0), stop=False, skip_group_check=True),
                 reads=(mmq[h // 4], xdtb), writes=(PB[by],), track=False)
            k.op("pe", lambda e: e.matmul(PS[by][0:m, hc:hc + 64], DI[0:m, h, 0:m], XTOK[0:m, h * 64:(h + 1) * 64],
                                          start=False, stop=True, skip_group_check=True),
                 reads=(cb, xtokb), writes=(PB[by],), track=(h % 8 == 7))

    def ssd_P(blk, have_off, YA, YN, part, yab=None, ynb=None):
        yab = yab if yab is not None else yab_g
        ynb = ynb if ynb is not None else ynb_g
        m = 128 if blk < 16 else 64
        t0 = blk * 128
        YT_ps = PS[0][:].bitcast(BF16).rearrange("p (j t) -> p j t", j=8)
        if part == 1:
            k.op("act", lambda e: e.copy(YMIX[:, 8:16, t0:t0 + m], YT_ps[:, :, 0:m]), reads=(PB[0],), writes=(ysb[blk],))
            return
        for g in range(2) if part in (0, "a") else ():
            cs = slice(g * 512, (g + 1) * 512)
            if have_off:
                k.op("dve", lambda e: e.tensor_tensor(YA[0:m, cs].rearrange("p (h d) -> p h d", h=8), PS[6 + g][0:m, :].rearrange("p (h d) -> p h d", h=8),
                                                      TOK[0:m, blk, 2, 8 * g:8 * g + 8].unsqueeze(2).broadcast_to([m, 8, 64]), ALU.mult),
                     reads=(PB[6 + g], tokb[blk]), writes=(yab,))
                k.op("dve", lambda e: e.tensor_tensor(YA[0:m, cs], YA[0:m, cs], PS[4 + g][0:m, :], ALU.add),
                     reads=(PB[4 + g], yab), writes=(yab,))
                k.op("dve", lambda e: e.tensor_tensor(YA[0:m, cs], YA[0:m, cs], SZT[0:m, blk, cs], ALU.mult),
                     reads=(yab, sztb[blk]), writes=(yab,))
            else:
                k.op("dve", lambda e: e.tensor_tensor(YA[0:m, cs], PS[4 + g][0:m, :], SZT[0:m, blk, cs], ALU.mult),
                     reads=(PB[4 + g], sztb[blk]), writes=(yab,))
        if part == "a":
            return
        k.op("act", lambda e: e.activation(YN[0:m, :], YA[0:m, :], AF.Square, accum_out=SMALL[0:m, 0:1]), reads=(yab,), writes=(ynb, smb))
        k.op("dve", lambda e: e.tensor_scalar(SMALL[0:m, 1:2], SMALL[0:m, 0:1], 1.0 / 1024, EPS, ALU.mult, ALU.add), reads=(smb,), writes=(smb,))
        k.op("act", lambda e: e.activation(SMALL[0:m, 2:3], SMALL[0:m, 1:2], AF.Ln), reads=(smb,), writes=(smb,))
        k.op("act", lambda e: e.activation(SMALL[0:m, 3:4], SMALL[0:m, 2:3], AF.Exp, scale=-0.5), reads=(smb,), writes=(smb,))
        k.op("act", lambda e: e.activation(YN[0:m, :], YA[0:m, :], AF.Copy, scale=SMALL[0:m, 3:4]), reads=(yab, smb), writes=(ynb,))
        for j in range(8):
            k.op("pe", lambda e, j=j: e.transpose(YT_ps[:, j, 0:m], YN[0:m, j * 128:(j + 1) * 128], IDB[0:m, 0:m]),
                 reads=(ynb, cb), writes=(PB[0],), track=(j == 7))

    T = ssd_alloc(128)
    YA = r2h.alloc([128, 1024], F32)
    YN = r2h.alloc([128, 1024], BF16)

    def ssd_S(blk, part):
        if part == 0:
            for g in range(2):
                k.op("pe", lambda e, g=g: e.matmul(PS[2 + g][:, :], T["BTOK"][:, g * 128:(g + 1) * 128], T["XW"][:, g * 512:(g + 1) * 512], start=True, stop=True),
                     reads=(btokb, xwb), writes=(PB[2 + g],))
            return
        if blk == 0:
            for g in range(2):
                k.op("act", lambda e, g=g: e.copy(SST[:, g * 512:(g + 1) * 512], PS[2 + g][:, :]), reads=(PB[2 + g],), writes=(sstb,))
        else:
            k.op("dve", lambda e: e.tensor_tensor(v3(SST[:, :]), v3(SST[:, :]), bc16(TOK[:, blk, 3, :], 128), ALU.mult),
                 reads=(sstb, tokb[blk]), writes=(sstb,))
            for g in range(2):
                k.op("dve", lambda e, g=g: e.tensor_tensor(SST[:, g * 512:(g + 1) * 512], SST[:, g * 512:(g + 1) * 512], PS[2 + g][:, :], ALU.add),
                     reads=(sstb, PB[2 + g]), writes=(sstb,))
        if blk < 15:
            k.op("act", lambda e: e.copy(HP[:, :], SST[:, :]), reads=(sstb,), writes=(hpb,))

    ssd_A1(0, T)
    ssd_A2(0, T, 0)
    ssd_A2(0, T, 1)
    for blk in range(16):
        t0 = blk * 128
        if blk > 0:
            for g in range(2):
                k.op("pe", lambda e, g=g: e.matmul(PS[6 + g][:, :], XBC[:, 10 + g, t0:t0 + 128], HP[:, g * 512:(g + 1) * 512], start=True, stop=True),
                     reads=(xbcb[10 + g], hpb), writes=(PB[6 + g],))
        ssd_S(blk, 0)
        ssd_Y(blk, T)
        if blk > 0:
            ssd_P(blk - 1, blk > 1, YA, YN, "b")
            ssd_P(blk - 1, blk > 1, YA, YN, 1)
        ssd_S(blk, 1)
        ssd_P(blk, blk > 0, YA, YN, "a")
        if blk < 15:
            ssd_A1(blk + 1, T)
            ssd_A2(blk + 1, T, 0)
            ssd_A2(blk + 1, T, 1)
    ssd_P(15, True, YA, YN, "b")
    ssd_P(15, True, YA, YN, 1)
    k.dma("sp", nssm_p, SST[:, :], next_out_ds(), reads=(sstb,), is_out=True)
    k.barrier()

    blk = 16
    t0 = NP
    T = ssd_alloc(64)
    CZb = [r2h.alloc([128, 2, 64], BF16) for _ in range(2)]
    BZb = [r2h.alloc([64, 256], BF16) for _ in range(2)]
    xbf = lambda j: XBC[:, j, 0:2048].bitcast(F32).rearrange("p (j n) -> p j n", j=8)
    H0 = [xbf(j) for j in range(6)] + [SST[:, :].rearrange("p (j n) -> p j n", j=8)]
    H0B = [r2h.alloc([128, 8, 128], BF16), HP[:, :].rearrange("p (j n) -> p j n", j=8)]
    H0T = [r2h.alloc([128, 1024], BF16) for _ in range(2)]
    szf = lambda i: SZT[:, 2 * i:2 * i + 2, :].rearrange("p a b -> p (a b)").bitcast(F32).rearrange("p (j n) -> p j n", j=8)
    OST = [szf(5), szf(6), szf(7)]
    NH = len(H0)
    NO = len(OST)
    czb, bzb, h0b, h0bb, h0tb = k.bufl(2, "cz"), k.bufl(2, "bz"), k.bufl(6, "h0") + [sstb], [k.buf("h0b"), hpb], k.bufl(2, "h0t")
    ostb = k.bufl(NO, "ost")
    h0_ds = [k.dsem("h0") for _ in range(NH)]
    ost_ds = [k.dsem("ost") for _ in range(NO)]
    WSU = [R3[:, 4096 * i:4096 * (i + 1)].rearrange("p (a b) -> p a b", a=KC) for i in range(2)]
    WPOOL = R3[:, 8192:10240].rearrange("p (g c e) -> p g c e", g=4, c=2)
    wsub, wpb = k.bufl(2, "wsu"), k.buf("wpool")
    load_w(WSU[0][:], wview_kc(w_in, 0, 512), WS_DS[0], wsub[0])
    load_w(WSU[1][:], wview_kc(w_in, 512, 512), WS_DS[1], wsub[1])
    k.dma("pool", WPOOL[:].rearrange("p g c e -> p (g c) e"), w_pool.rearrange("g (c p) e -> p (g c) e", p=128), ld_ds[7], writes=(wpb,))
    h0src = lambda b: sssm[b].rearrange("(j a) p n -> (a p) j n", a=2)
    for b in range(NH):
        k.dma("sp", H0[b][:], h0src(b), h0_ds[b], writes=(h0b[b],))
    ssd_A1(blk, T)
    ssd_A2(blk, T, 0)
    ssd_A2(blk, T, 1)
    HT_pss = [PS[2][:].bitcast(BF16), PS[3][:].bitcast(BF16)]
    for b in range(16):
        s = b % 2
        s3 = b % NH
        so = b % NO
        k.op("act", lambda e: e.copy(H0B[s][:], H0[s3][:]), reads=(h0b[s3],), writes=(h0bb[s],))
        HT_ps = HT_pss[s]
        for j in range(8):
            k.op("pe", lambda e, j=j: e.transpose(HT_ps[:, j * 128:(j + 1) * 128], H0B[s][:, j, :], IDB[:]),
                 reads=(h0bb[s], cb), writes=(PB[2 + s],), track=(j == 7))
        k.op("act", lambda e: e.copy(H0T[s][:, :], HT_ps[:, :]), reads=(PB[2 + s],), writes=(h0tb[s],))
        k.op("pool", lambda e: e.tensor_tensor(CZb[s][:], XBC[:, 10:12, t0:t0 + 64], BDROW[:, b, :].unsqueeze(1).broadcast_to([128, 2, 64]), ALU.mult),
             reads=(xbcb[10], xbcb[11], cb), writes=(czb[s],))
        k.op("pool", lambda e: e.tensor_tensor(BZb[s][:], T["BTOK"][0:64, :], BDCOL[:, b:b + 1].broadcast_to([64, 256]), ALU.mult),
             reads=(btokb, cb), writes=(bzb[s],))
        for g in range(2):
            k.op("pe", lambda e, g=g: e.matmul(PS[6 + g][0:64, :], CZb[s][:, g, :], H0T[s][:, g * 512:(g + 1) * 512], start=(b == 0), stop=(b == 15)),
                 reads=(czb[s], h0tb[s]), writes=(PB[6 + g],), track=(b == 15 or g == 1))
        cb0 = 4 if s == 0 else 0
        csv = lambda j: PS[cb0 + j // 4][:, (j % 4) * 128:(j % 4 + 1) * 128]
        for j in range(8):
            k.op("pe", lambda e, j=j: e.matmul(csv(j), T["XW"][0:64, j * 128:(j + 1) * 128], BZb[s][:, (j // 4) * 128:(j // 4 + 1) * 128], start=True, stop=True),
                 reads=(xwb, bzb[s]), writes=(PB[cb0 + j // 4],), track=(j % 4 == 3))
        for j in range(8):
            k.op("dve", lambda e, j=j: e.scalar_tensor_tensor(OST[so][:, j, :], H0[s3][:, j, :], CDX[:, j, b:b + 1], csv(j), ALU.mult, ALU.add),
                 reads=(cdxb, PB[cb0 + j // 4], h0b[s3]), writes=(ostb[so],), track=(j == 7))
        k.dma("sp", nssm_s[b].rearrange("(j a) p n -> (a p) j n", a=2), OST[so][:], ost_ds[so], reads=(ostb[so],), is_out=True)
        if b + NH < 16:
            k.dma("sp", H0[s3][:], h0src(b + NH), h0_ds[s3], writes=(h0b[s3],))
    ssd_Y(blk, T)
    ssd_P(blk, True, SST, HP, 0, sstb, hpb)
    ssd_P(blk, True, SST, HP, 1, sstb, hpb)
    dump("ys", YMIX[:, 8:16, :], ysb, BF16)
    k.barrier()

    r3.reset()
    r3.off = 20480
    WS = WSU
    wsb = wsub
    U = [r3.alloc([128, 16 + NP], F32) for _ in range(2)]
    US = [r3.alloc([128, 19, 16], F32) for _ in range(2)]
    TA = [r3.alloc([128, 16 + NP], F32) for _ in range(2)]
    TAS = [r3.alloc([128, 19, 16], F32) for _ in range(2)]
    DD = [r3.alloc([128, 2, NT], BF16) for _ in range(2)]
    SPL = r3.alloc([128, 8, 240], F32)
    ub_, tab, ddb = k.bufl(2, "u"), k.bufl(2, "ta"), k.bufl(2, "dd")
    splb = k.buf("spl")
    ypb = k.bufl(8, "yp")
    k.dma("sp", SPL[:], spool.rearrange("p (c f) -> p c f", c=8), ld_ds[6], writes=(splb,))
    for q in range(2):
        k.op("pool", lambda e, q=q: e.memset(U[q][:, 0:16], 0.0), writes=(ub_[q],))
    pending_pool = []

    def flush_pool():
        while pending_pool:
            g_, dq_ = pending_pool.pop(0)
            for ec in range(2):
                for ti, (t0, n) in enumerate(TT):
                    bi = next_bank()
                    for k2 in range(2):
                        k.op("pe", lambda e, k2=k2: e.matmul(PS[bi][:, 0:n], WPOOL[:, g_, k2, ec * 128:(ec + 1) * 128], DD[dq_][:, k2, t0:t0 + n],
                                                             start=(k2 == 0), stop=(k2 == 1)),
                             reads=(wpb, ddb[dq_]), writes=(PB[bi],), track=(k2 == 1))
                    yc = 2 * g_ + ec
                    k.op("act", lambda e: e.activation(YMIX[:, yc, t0:t0 + n], PS[bi][:, 0:n], AF.Copy, scale=VEC[:, V_PSC + yc:V_PSC + yc + 1]),
                         reads=(PB[bi], vb), writes=(ypb[yc],))

    for ci, c in enumerate((6, 7, 4, 5, 2, 3, 0, 1)):
        grp, c4 = c // 4, c % 4
        q = c % 2
        g = c // 2
        w = POOLW[g]
        k.op("pool", lambda e: e.tensor_copy(US[q][:, 0:15, :], SPL[:, c, :].rearrange("p (j b) -> p j b", j=15)),
             reads=(splb,), writes=(ub_[q],))
        for ti, (t0, n) in enumerate(TT):
            bi = next_bank()
            for kc in range(KC):
                k.op("pe", lambda e, kc=kc: e.matmul(PS[bi][:, 0:n], WS[grp][:, kc, c4 * 128:(c4 + 1) * 128], XN[:, kc, t0:t0 + n],
                                                     start=(kc == 0), stop=(kc == KC - 1)),
                     reads=(XNb[ti], wsb[grp]), writes=(PB[bi],), track=(kc == KC - 1))
            if ti < 4:
                k.op("act", lambda e: e.copy(U[q][:, 16 + t0:16 + t0 + n], PS[bi][:, 0:n]), reads=(PB[bi],), writes=(ub_[q],))
            else:
                k.op("act", lambda e: e.copy(US[q][:, 15:19, :], PS[bi][:, 0:64].rearrange("p (b l) -> p l b", l=4)),
                     reads=(PB[bi],), writes=(ub_[q],))
        flush_pool()
        if ci == 7:
            XNf = XN[:].rearrange("p c t -> p (c t)")
            WO = [XNf[:, 4096 * i:4096 * (i + 1)].rearrange("p (a b) -> p a b", a=16) for i in range(4)]
            wob = k.bufl(4, "wo")
            wo_ds = [k.dsem("wo") for _ in range(4)]
            for i in range(4):
                k.dma("pool", WO[i][:], w_out.rearrange("(kc p) n -> p kc n", p=128)[:, :, i * 256:(i + 1) * 256], wo_ds[i],
                      writes=[wob[i]] + XNb)
        k.dma("sp", npool_p[:, c, :], U[q][:, 16 + NP - 15:16 + NP], next_out_ds(), reads=(ub_[q],), is_out=True)
        k.dma("sp", npool_s[:, c, :], US[q][:, 4:19, :].rearrange("p j b -> p (j b)"), next_out_ds(), reads=(ub_[q],), is_out=True)
        src, srcs, srcb = U[q], US[q], ub_[q]
        step = 1
        pp = 0
        while step < w:
            dst, dsts, dstb = TA[pp], TAS[pp], tab[pp]
            lo = 2 * step - 1
            k.op("dve", lambda e, src=src, dst=dst, lo=lo, step=step: e.tensor_tensor(dst[:, lo:], src[:, lo:], src[:, lo - step:16 + NP - step], ALU.add),
                 reads=(srcb,), writes=(dstb,))
            k.op("dve", lambda e, srcs=srcs, dsts=dsts, lo=lo, step=step: e.tensor_tensor(dsts[:, lo:, :], srcs[:, lo:, :], srcs[:, lo - step:19 - step, :], ALU.add),
                 reads=(srcb,), writes=(dstb,))
            src, srcs, srcb = dst, dsts, dstb
            pp = 1 - pp
            step *= 2
        dq = g % 2
        cc = c % 2
        k.op("dve", lambda e: e.scalar_tensor_tensor(DD[dq][:, cc, 0:NP], src[:, 16:16 + NP], 1.0 / w, U[q][:, 16:16 + NP], ALU.mult, ALU.subtract),
             reads=(srcb, ub_[q]), writes=(ddb[dq],))
        k.op("dve", lambda e: e.tensor_tensor(SMALL[:, 16:32], src[:, 16:32], INVC[:, g, :], ALU.mult), reads=(srcb, cb), writes=(smb,))
        k.op("dve", lambda e: e.tensor_tensor(DD[dq][:, cc, 0:16], SMALL[:, 16:32], U[q][:, 16:32], ALU.subtract), reads=(smb, ub_[q]), writes=(ddb[dq],))
        k.op("dve", lambda e: e.scalar_tensor_tensor(DD[dq][:, cc, NP:NT].rearrange("p (b l) -> p l b", l=4), srcs[:, 15:19, :], 1.0 / w,
                                                     US[q][:, 15:19, :], ALU.mult, ALU.subtract),
             reads=(srcb, ub_[q]), writes=(ddb[dq],))
        if cc == 1:
            pending_pool.append((g, dq))
    flush_pool()
    dump("yp", YMIX[:, 0:8, :], ypb, BF16)
    k.barrier(keep=list(zip(wo_ds, wob)))

    Hb = [k.bufl(5, "h%d_" % c) for c in range(KC)]
    hld = [k.dsem("hld") for _ in range(KC)]
    for c in range(KC):
        k.dma("sp", H[:, c, :], xT[c * 128:(c + 1) * 128, :], hld[c], reads=(Hb[c - 1][0],) if c else (), writes=Hb[c])
    for i in range(4):
        k.op("dve", lambda e, i=i: e.tensor_tensor(WO[i][:, 8:16, :], WO[i][:, 8:16, :],
                                                   VEC[:, V_GSSM:V_GSSM + 8].unsqueeze(2).broadcast_to([128, 8, 256]), ALU.mult),
             reads=(wob[i], vb), writes=(wob[i],))
    for cg in range(4):
        s = cg
        for oc in range(2):
            c = cg * 2 + oc
            for ti, (t0, n) in enumerate(TT):
                bi = next_bank()
                for kc in range(16):
                    k.op("pe", lambda e, kc=kc: e.matmul(PS[bi][:, 0:n], WO[s][:, kc, oc * 128:(oc + 1) * 128], YMIX[:, kc, t0:t0 + n],
                                                         start=(kc == 0), stop=(kc == 15)),
                         reads=(wob[s],), writes=(PB[bi],), track=(kc == 15))
                k.op("dve", lambda e: e.tensor_tensor(H[:, c, t0:t0 + n], H[:, c, t0:t0 + n], PS[bi][:, 0:n], ALU.add),
                     reads=(PB[bi], Hb[c][ti]), writes=(Hb[c][ti],))
    dump("h1", H[:], [b for l in Hb for b in l])
    k.barrier()

    def norm_from_h(gcol):
        r3x.reset()
        SQ = [r3x.alloc([128, KC, 512], BF16)] * 2
        RS = [r3x.alloc([128, 512], F32) for _ in range(2)]
        sqb, rsb = [k.buf("sq")] * 2, k.bufl(2, "rs")
        def tile(ti):
            t0, n = TT[ti]
            rmsnorm_tile(ti, H[:, :, t0:t0 + n], [Hb[c][ti] for c in range(KC)], gcol,
                         lambda kc: XN[:, kc, t0:t0 + n], (XNb[ti],), SQ, RS, sqb, rsb)
        return [sqb[0]] + rsb, tile

    WS = [R2[:, 16896 + 4096 * i:16896 + 4096 * (i + 1)].rearrange("p (a b) -> p a b", a=KC) for i in range(3)]
    wsb = k.bufl(3, "wsf")
    wi = [0]

    def next_ws(src):
        s = wi[0] % 3
        wi[0] += 1
        load_w(WS[s][:], src, WS_DS[s], wsb[s])
        return s
    w2v = w_ff2.rearrange("(fc p) n -> p fc n", p=128)
    pref = [next_ws(wview_kc(w_ff1, 0, 512)), next_ws(wview_kc(w_ff1, 512, 512)), next_ws(w2v[:, 0:8, 0:512])]
    n2bufs, norm2_tile = norm_from_h(V_GMLP)
    norm2_tile(0)
    norm2_tile(1)
    r2.reset()
    A = r2.alloc([128, KC, NT], BF16)
    r2.off += 3 * 8192
    RT = [r2.alloc([128, 512], F32) for _ in range(2)]
    WPLE = r2.alloc([128, 2, D], BF16)
    wple_off = r2.off
    rtb = k.bufl(2, "rt")
    ab = [k.bufl(5, "a%d_" % c) for c in range(KC)]
    rti = [0]
    for G in range(4):
        for sub in range(2):
            s = pref.pop(0) if pref else next_ws(wview_kc(w_ff1, G * 1024 + sub * 512, 512))
            for c4 in range(4):
                fc = sub * 4 + c4
                for ti, (t0, n) in enumerate(TT):
                    bi = next_bank()
                    for kc in range(KC):
                        k.op("pe", lambda e, kc=kc: e.matmul(PS[bi][:, 0:n], WS[s][:, kc, c4 * 128:(c4 + 1) * 128], XN[:, kc, t0:t0 + n],
                                                             start=(kc == 0), stop=(kc == KC - 1)),
                             reads=(XNb[ti], wsb[s]), writes=(PB[bi],), track=(kc == KC - 1))
                    if (G, sub, c4) == (0, 0, 0) and ti + 2 < 5:
                        norm2_tile(ti + 2)
                    r = rti[0] % 2
                    rti[0] += 1
                    k.op("act", lambda e: e.activation(RT[r][:, 0:n], PS[bi][:, 0:n], AF.Relu), reads=(PB[bi],), writes=(rtb[r],))
                    k.op("dve", lambda e: e.tensor_tensor(A[:, fc, t0:t0 + n], RT[r][:, 0:n], RT[r][:, 0:n], ALU.mult),
                         reads=(rtb[r],), writes=(ab[fc][ti],))
        for half in range(2):
            s = pref.pop(0) if pref else next_ws(w2v[:, G * 8:(G + 1) * 8, half * 512:(half + 1) * 512])
            for oc in range(4):
                c = half * 4 + oc
                for ti, (t0, n) in enumerate(TT):
                    bi = next_bank()
                    for fc in range(KC):
                        k.op("pe", lambda e, fc=fc: e.matmul(PS[bi][:, 0:n], WS[s][:, fc, oc * 128:(oc + 1) * 128], A[:, fc, t0:t0 + n],
                                                             start=(fc == 0), stop=(fc == KC - 1)),
                             reads=(ab[fc][ti], wsb[s]), writes=(PB[bi],), track=(fc == KC - 1))
                    k.op("dve", lambda e: e.tensor_tensor(H[:, c, t0:t0 + n], H[:, c, t0:t0 + n], PS[bi][:, 0:n], ALU.add),
                         reads=(PB[bi], Hb[c][ti]), writes=(Hb[c][ti],))
    r3x.reset()
    WG = [r3x.alloc([128, KC, 512], BF16) for _ in range(2)]
    wgb = k.bufl(2, "wsg")
    wpleb = k.buf("wple")
    wg_ds = [k.dsem("wg") for _ in range(2)]
    wple_ds = k.dsem("wple")
    for half in range(2):
        k.dma("pool", WG[half][:], wview_kc(w_gate, half * 512, 512), wg_ds[half], writes=[wgb[half]] + n2bufs)
    k.dma("pool", WPLE[:], w_ple.rearrange("(c p) n -> p c n", p=128), wple_ds, writes=(wpleb,))
    dump("h2", H[:], [b for l in Hb for b in l])
    k.barrier(keep=[(wg_ds[0], wgb[0]), (wg_ds[1], wgb[1]), (wple_ds, wpleb)])

    r2.reset()
    SQ = [r2.alloc([128, KC, 512], BF16) for _ in range(2)]
    RS = [r2.alloc([128, 512], F32) for _ in range(2)]
    WS = WG
    PT = r2.alloc([128, 2, NT], BF16)
    SG = [r2.alloc([128, 512], F32) for _ in range(2)]
    YOC = [r2.alloc([128, 512], F32) for _ in range(4)]
    assert r2.off <= wple_off - 4096
    sqb, rsb = k.bufl(2, "sq"), k.bufl(2, "rs")
    wsb, sgb, yocb = wgb, k.bufl(2, "sg"), k.bufl(4, "yoc")
    yoc_ds = [k.dsem("yoc") for _ in range(4)]
    ptb = k.buf("pt")
    k.dma("pool", PT[:], pT.rearrange("(c p) t -> p c t", p=128), ld_ds[9], writes=(ptb,))
    gi = [0]

    def n3(ti):
        t0, n = TT[ti]
        rmsnorm_tile(ti, H[:, :, t0:t0 + n], [Hb[c][ti] for c in range(KC)], V_GPLE,
                     lambda kc: XN[:, kc, t0:t0 + n], (XNb[ti],), SQ, RS, sqb, rsb)

    def gate(ti):
        t0, n = TT[ti]
        for c in range(KC):
            half, oc = c // 4, c % 4
            bi = next_bank()
            for kc in range(KC):
                k.op("pe", lambda e, kc=kc: e.matmul(PS[bi][:, 0:n], WS[half][:, kc, oc * 128:(oc + 1) * 128], XN[:, kc, t0:t0 + n],
                                                     start=(kc == 0), stop=(kc == KC - 1)),
                     reads=(XNb[ti], wsb[half]), writes=(PB[bi],), track=(kc == KC - 1))
            bj = next_bank()
            for k2 in range(2):
                k.op("pe", lambda e, k2=k2: e.matmul(PS[bj][:, 0:n], WPLE[:, k2, c * 128:(c + 1) * 128], PT[:, k2, t0:t0 + n],
                                                     start=(k2 == 0), stop=(k2 == 1)),
                     reads=(wpleb, ptb), writes=(PB[bj],), track=(k2 == 1))
            r = gi[0] % 2
            gi[0] += 1
            k.op("act", lambda e: e.activation(SG[r][:, 0:n], PS[bi][:, 0:n], AF.Sigmoid), reads=(PB[bi],), writes=(sgb[r],))
            k.op("dve", lambda e: e.tensor_tensor(SG[r][:, 0:n], SG[r][:, 0:n], PS[bj][:, 0:n], ALU.mult),
                 reads=(PB[bj], sgb[r]), writes=(sgb[r],))
            k.op("dve", lambda e: e.tensor_tensor(H[:, c, t0:t0 + n], H[:, c, t0:t0 + n], SG[r][:, 0:n], ALU.add),
                 reads=(sgb[r], Hb[c][ti]), writes=(Hb[c][ti],))

    def fn(ti):
        t0, n = TT[ti]

        def post(kc):
            k.dma("sp", yT[kc * 128:(kc + 1) * 128, t0:t0 + n], YOC[kc % 4][:, 0:n], yoc_ds[kc % 4], reads=(yocb[kc % 4],), is_out=True)
        rmsnorm_tile(ti, H[:, :, t0:t0 + n], [Hb[c][ti] for c in range(KC)], V_GFIN,
                     lambda kc: YOC[kc % 4][:, 0:n], lambda kc: (yocb[kc % 4],), SQ, RS, sqb, rsb, post=post)

    for ti in range(5):
        n3(ti)
    gate(0)
    gate(1)
    fn(0)
    gate(2)
    fn(1)
    gate(3)
    fn(2)
    gate(4)
    fn(3)
    fn(4)
    k.finish()
    return k, dbg


_CACHE = {}


def _host_inputs(inp, i):
    f = np.float32
    xs = inp["x_sample"][16 * i:16 * i + 16].reshape(64, D)
    xTc = np.ascontiguousarray(np.concatenate([inp["x_prompt"][i], xs], axis=0).T, dtype=f)
    ps = inp["p_sample"][0, 16 * i:16 * i + 16].reshape(64, 256)
    pTc = np.ascontiguousarray(np.concatenate([inp["p_prompt"][0, i], ps], axis=0).T, dtype=f)
    spool = np.ascontiguousarray(inp["state_pool"][0, 16 * i:16 * i + 16].reshape(16, 15, 8, 128).transpose(3, 2, 1, 0).reshape(128, 8 * 15 * 16), dtype=f)
    sconv = np.ascontiguousarray(inp["state_conv"][0, 16 * i:16 * i + 16].reshape(16, 3, 12, 128).transpose(3, 2, 1, 0).reshape(128, 12 * 3 * 16), dtype=f)
    sssm = np.ascontiguousarray(inp["state_ssm"][0, 16 * i:16 * i + 16], dtype=f)
    return {"xT": xTc, "pT": pTc, "spool": spool, "sconv": sconv, "sssm": sssm}


def _host_shared(inp):
    f = np.float32
    cols = lambda v, n: np.asarray(v, dtype=f).reshape(n, 128).T
    vecs = np.zeros((128, NV), dtype=f)
    vecs[:, V_GMIX:V_GMIX + 8] = cols(inp["norm_mix_g"][0], 8)
    vecs[:, V_GMLP:V_GMLP + 8] = cols(inp["norm_mlp_g"][0], 8)
    vecs[:, V_GPLE:V_GPLE + 8] = cols(inp["norm_ple_g"][0], 8)
    vecs[:, V_GFIN:V_GFIN + 8] = cols(inp["final_norm_g"], 8)
    vecs[:, V_PSC:V_PSC + 8] = cols(inp["pool_scale"][0], 8)
    vecs[:, V_GSSM:V_GSSM + 8] = cols(inp["ssm_norm_g"][0], 8)
    vecs[:, V_CB:V_CB + 12] = cols(inp["conv_b"][0], 12)
    for j in range(4):
        vecs[:, V_CW + 12 * j:V_CW + 12 * j + 12] = cols(inp["conv_w"][0, j], 12)
    hvv = np.stack([inp["dt_bias"][0], inp["a_log"][0], inp["d_skip"][0]], axis=1).astype(f)
    c = np.ascontiguousarray
    return {"w_in": c(inp["w_in"][0], dtype=f), "w_pool": c(inp["w_pool"][0], dtype=f), "w_out": c(inp["w_out"][0], dtype=f),
            "w_ff1": c(inp["w_ff1"][0], dtype=f), "w_ff2": c(inp["w_ff2"][0], dtype=f), "w_gate": c(inp["w_gate"][0], dtype=f),
            "w_ple": c(inp["w_ple"][0], dtype=f), "vecs": vecs, "hv": c(hvv), "dsk": c(inp["d_skip"][0].reshape(1, 16), dtype=f)}


def kernel(debug=False, **inp):
    inp = {n: np.asarray(v) for n, v in inp.items()}
    key = bool(debug)
    if key not in _CACHE:
        _CACHE[key] = build_program(debug=debug)[0]
    kb = _CACHE[key]
    shared = _host_shared(inp)
    in_maps = []
    for i in range(8):
        m = dict(shared)
        m.update(_host_inputs(inp, i))
        in_maps.append(m)
    res = run_bass_kernel_spmd(kb.nc, in_maps, core_ids=list(range(8)))
    R = res.results
    f = np.float32
    y_prompt = np.stack([R[i]["yT"][:, :NP].T for i in range(8)]).astype(f)
    y_sample = np.concatenate([R[i]["yT"][:, NP:].T.reshape(16, 4, D) for i in range(8)]).astype(f)
    pool_p = np.stack([R[i]["npool_p"].transpose(2, 1, 0).reshape(15, D) for i in range(8)])[None].astype(f)
    conv_p = np.stack([R[i]["nconv_p"].transpose(2, 1, 0).reshape(3, 1536) for i in range(8)])[None].astype(f)
    ssm_p = np.stack([R[i]["nssm_p"].reshape(128, 16, 64).transpose(1, 2, 0) for i in range(8)])[None].astype(f)
    pool_s = np.concatenate([R[i]["npool_s"].reshape(128, 8, 15, 16).transpose(3, 2, 1, 0).reshape(16, 15, D) for i in range(8)])[None].astype(f)
    conv_s = np.concatenate([R[i]["nconv_s"].reshape(128, 12, 3, 16).transpose(3, 2, 1, 0).reshape(16, 3, 1536) for i in range(8)])[None].astype(f)
    ssm_s = np.concatenate([R[i]["nssm_s"] for i in range(8)])[None].astype(f)
    outs = (np.ascontiguousarray(y_prompt), np.ascontiguousarray(y_sample), np.ascontiguousarray(pool_p), np.ascontiguousarray(conv_p),
            np.ascontiguousarray(ssm_p), np.ascontiguousarray(pool_s), np.ascontiguousarray(conv_s), np.ascontiguousarray(ssm_s))
    if debug:
        return outs, R
    return outs
```
